# Optimizing a Trainium2 kernel written in Bass

```python
import math
import jax, jax.numpy as jnp
from jax import lax
import numpy as np

D_MODEL = 2048
BATCH = 2
SEQ = 16384
DEPTH = 1

HEAD_DIM = 128
GRID_W = 64
Q_BLOCK = 128
EPS = 1e-6
A_HEADS = D_MODEL // (2 * HEAD_DIM)
A_KV_HEADS = A_HEADS // 4
A_GROUP = A_HEADS // A_KV_HEADS
A_ROPE_THETA = 10000.0
AXIAL_DIM = HEAD_DIM // 2
B_VDIM = 2 * HEAD_DIM
B_HEADS = D_MODEL // (2 * B_VDIM)
PARTIAL_ROPE_DIM = HEAD_DIM // 4
PARTIAL_ROPE_THETA = 500000.0
A_Q = A_HEADS * HEAD_DIM
A_KV = A_KV_HEADS * HEAD_DIM
B_QK = B_HEADS * 2 * HEAD_DIM
B_V = B_HEADS * B_VDIM
D_IN = A_Q + 2 * A_KV + 2 * B_QK + B_V
D_MIX = A_Q + B_V
D_FF = 5632

kernel_name = "hybrid_gqa_axial_diffattn_macaron"


def _rms_norm(x, gain):
    x32 = x.astype(jnp.float32)
    y = x32 * lax.rsqrt(jnp.mean(x32 * x32, axis=-1, keepdims=True) + EPS)
    return (y * gain.astype(jnp.float32)).astype(x.dtype)


def _rope_table(pos, dim, theta):
    inv_freq = theta ** (-jnp.arange(0, dim, 2, dtype=jnp.float32) / dim)
    ang = pos[:, None] * inv_freq[None, :]
    return jnp.cos(ang), jnp.sin(ang)


def _apply_rope(x, cos, sin):
    half = x.shape[-1] // 2
    shape = (1, x.shape[1]) + (1,) * (x.ndim - 3) + (half,)
    c = cos.reshape(shape).astype(x.dtype)
    s = sin.reshape(shape).astype(x.dtype)
    x1, x2 = x[..., :half], x[..., half:]
    return jnp.concatenate([x1 * c - x2 * s, x2 * c + x1 * s], axis=-1)


def _axial_rope(x, row_cs, col_cs):
    return jnp.concatenate([
        _apply_rope(x[..., :AXIAL_DIM], *row_cs),
        _apply_rope(x[..., AXIAL_DIM:], *col_cs)], axis=-1)


def _partial_rope(x, cs):
    return jnp.concatenate([
        _apply_rope(x[..., :PARTIAL_ROPE_DIM], *cs),
        x[..., PARTIAL_ROPE_DIM:]], axis=-1)


def _sweep_queries(fn, q):
    b, s = q.shape[0], q.shape[1]
    nb = s // Q_BLOCK
    qb = jnp.moveaxis(q.reshape((b, nb, Q_BLOCK) + q.shape[2:]), 1, 0)
    out = lax.map(fn, qb)
    return jnp.moveaxis(out, 0, 1).reshape(b, s, -1)


def _swiglu(h, w_gu, w_down):
    g, u = jnp.split(h @ w_gu, 2, axis=-1)
    return (jax.nn.silu(g) * u) @ w_down


def _gqa_axial(qa, ka, va):
    scale = HEAD_DIM ** -0.5
    b = qa.shape[0]

    def block(qb):
        qb = qb.reshape(b, Q_BLOCK, A_KV_HEADS, A_GROUP, HEAD_DIM)
        s = jnp.einsum('bqgrd,bkgd->bgrqk', qb, ka).astype(jnp.float32) * scale
        p = jax.nn.softmax(s, axis=-1).astype(va.dtype)
        o = jnp.einsum('bgrqk,bkgd->bqgrd', p, va)
        return o.reshape(b, Q_BLOCK, A_Q)

    return _sweep_queries(block, qa)


def _diff_attention(qd, kd, vd, lam, lambda_init, subln):
    scale = HEAD_DIM ** -0.5
    b = qd.shape[0]

    def block(qb):
        s = jnp.einsum('bqhcd,bkhcd->bhcqk', qb, kd).astype(jnp.float32) * scale
        p = jax.nn.softmax(s, axis=-1)
        attn = (p[:, :, 0] - lam * p[:, :, 1]).astype(vd.dtype)
        o = jnp.einsum('bhqk,bkhe->bqhe', attn, vd)
        o = _rms_norm(o, subln) * (1.0 - lambda_init)
        return o.reshape(b, Q_BLOCK, B_V)

    return _sweep_queries(block, qd)


def setup_inputs(seed: int = 0) -> dict:
    key = jax.random.key(seed)
    ks = jax.random.split(key, 24)
    f32 = jnp.float32

    def w(k, shape, fan_in):
        return jax.random.normal(k, shape, f32) * (fan_in ** -0.5)

    def gain(k, shape):
        return 1.0 + 0.02 * jax.random.normal(k, shape, f32)

    return {
        "x": jax.random.normal(ks[0], (BATCH, SEQ, D_MODEL), f32),
        "ffn1_norm": gain(ks[1], (DEPTH, D_MODEL)),
        "ffn1_w_gu": w(ks[2], (DEPTH, D_MODEL, 2 * D_FF), D_MODEL),
        "ffn1_w_down": w(ks[3], (DEPTH, D_FF, D_MODEL), D_FF),
        "mix_norm": gain(ks[4], (DEPTH, D_MODEL)),
        "w_in": w(ks[5], (DEPTH, D_MODEL, D_IN), D_MODEL),
        "a_q_norm": gain(ks[6], (DEPTH, HEAD_DIM)),
        "a_k_norm": gain(ks[7], (DEPTH, HEAD_DIM)),
        "b_q_norm": gain(ks[8], (DEPTH, HEAD_DIM)),
        "b_k_norm": gain(ks[9], (DEPTH, HEAD_DIM)),
        "b_lambda_q1": 0.1 * jax.random.normal(ks[10], (DEPTH, HEAD_DIM), f32),
        "b_lambda_k1": 0.1 * jax.random.normal(ks[11], (DEPTH, HEAD_DIM), f32),
        "b_lambda_q2": 0.1 * jax.random.normal(ks[12], (DEPTH, HEAD_DIM), f32),
        "b_lambda_k2": 0.1 * jax.random.normal(ks[13], (DEPTH, HEAD_DIM), f32),
        "b_subln": gain(ks[14], (DEPTH, B_VDIM)),
        "w_out": w(ks[15], (DEPTH, D_MIX, D_MODEL), D_MIX),
        "ffn2_norm": gain(ks[16], (DEPTH, D_MODEL)),
        "ffn2_w_gu": w(ks[17], (DEPTH, D_MODEL, 2 * D_FF), D_MODEL),
        "ffn2_w_down": w(ks[18], (DEPTH, D_FF, D_MODEL), D_FF),
        "out_norm": gain(ks[19], (DEPTH, D_MODEL)),
    }


def reference(x, ffn1_norm, ffn1_w_gu, ffn1_w_down, mix_norm, w_in,
              a_q_norm, a_k_norm, b_q_norm, b_k_norm,
              b_lambda_q1, b_lambda_k1, b_lambda_q2, b_lambda_k2, b_subln,
              w_out, ffn2_norm, ffn2_w_gu, ffn2_w_down, out_norm):
    b, s, _ = x.shape
    rows = s // GRID_W
    row_pos = jnp.broadcast_to(jnp.arange(rows, dtype=jnp.float32)[:, None], (rows, GRID_W)).reshape(-1)
    col_pos = jnp.broadcast_to(jnp.arange(GRID_W, dtype=jnp.float32)[None, :], (rows, GRID_W)).reshape(-1)
    lin_pos = jnp.arange(s, dtype=jnp.float32)
    row_cs = _rope_table(row_pos, AXIAL_DIM, A_ROPE_THETA)
    col_cs = _rope_table(col_pos, AXIAL_DIM, A_ROPE_THETA)
    part_cs = _rope_table(lin_pos, PARTIAL_ROPE_DIM, PARTIAL_ROPE_THETA)

    for l in range(DEPTH):
        x = x + 0.5 * _swiglu(_rms_norm(x, ffn1_norm[l]), ffn1_w_gu[l], ffn1_w_down[l])

        h = _rms_norm(x, mix_norm[l])
        proj = h @ w_in[l]
        i0 = A_Q; i1 = i0 + A_KV; i2 = i1 + A_KV; i3 = i2 + B_QK; i4 = i3 + B_QK
        qa = proj[..., :i0].reshape(b, s, A_HEADS, HEAD_DIM)
        ka = proj[..., i0:i1].reshape(b, s, A_KV_HEADS, HEAD_DIM)
        va = proj[..., i1:i2].reshape(b, s, A_KV_HEADS, HEAD_DIM)
        qd = proj[..., i2:i3].reshape(b, s, B_HEADS, 2, HEAD_DIM)
        kd = proj[..., i3:i4].reshape(b, s, B_HEADS, 2, HEAD_DIM)
        vd = proj[..., i4:].reshape(b, s, B_HEADS, B_VDIM)

        qa = _axial_rope(_rms_norm(qa, a_q_norm[l]), row_cs, col_cs)
        ka = _axial_rope(_rms_norm(ka, a_k_norm[l]), row_cs, col_cs)
        out_a = _gqa_axial(qa, ka, va)

        qd = _partial_rope(_rms_norm(qd, b_q_norm[l]), part_cs)
        kd = _partial_rope(_rms_norm(kd, b_k_norm[l]), part_cs)
        lambda_init = 0.8 - 0.6 * math.exp(-0.3 * l)
        lam = (jnp.exp(jnp.sum(b_lambda_q1[l].astype(jnp.float32) * b_lambda_k1[l].astype(jnp.float32)))
               - jnp.exp(jnp.sum(b_lambda_q2[l].astype(jnp.float32) * b_lambda_k2[l].astype(jnp.float32)))
               + lambda_init)
        out_b = _diff_attention(qd, kd, vd, lam, lambda_init, b_subln[l])

        x = x + jnp.concatenate([out_a, out_b], axis=-1) @ w_out[l]

        x = x + 0.5 * _swiglu(_rms_norm(x, ffn2_norm[l]), ffn2_w_gu[l], ffn2_w_down[l])

        x = _rms_norm(x, out_norm[l])
    return x
```

```python
import math
from contextlib import ExitStack

import numpy as np
import concourse.bass as bass
import concourse.mybir as mybir
from concourse.bass_utils import run_bass_kernel_spmd

F32 = mybir.dt.float32
BF16 = mybir.dt.bfloat16
AF = mybir.ActivationFunctionType
ALU = mybir.AluOpType

D = 2048
KC = 16
DFF = 5632
JC = 44
T = 512
TOK = 4096
NT = TOK // T
SEQ = 16384
EPS = 1e-6
SCALE = 128 ** -0.5
LAMBDA_INIT = 0.8 - 0.6 * math.exp(0.0)
G = 256
ARENA_BYTES = 180 * 1024
PS_KEY = 1 << 20
DR_KEY = 1 << 21
SEM_LIMIT = 30000


class View:
    __slots__ = ("ap", "keys")

    def __init__(self, ap, keys):
        self.ap = ap
        self.keys = keys


class Op:
    __slots__ = ("eng", "emit", "deps", "signal", "sem", "val", "epoch", "stream", "inc", "idx")


class Sched:
    def __init__(self, sem_pool):
        self.sem_pool = list(sem_pool)
        self.q = {e: [] for e in ("pe", "act", "dve", "pool", "sp")}
        self.streams = {}
        self.lastw = {}
        self.readers = {}
        self.nops = 0

    def op(self, eng, emit, reads=(), writes=(), dma=None, inc=None, nosame=False):
        o = Op()
        o.eng = eng
        o.emit = emit
        o.signal = False
        o.sem = None
        o.val = 0
        o.epoch = 0
        o.stream = dma if dma is not None else eng
        o.inc = inc if inc is not None else (16 if dma is not None else 1)
        deps = {}
        st = o.stream
        is_dma = dma is not None
        lastw = self.lastw
        readers = self.readers
        for k in reads:
            w = lastw.get(k)
            if w is not None and not ((st == "pe" or nosame) and w.stream == st):
                deps[id(w)] = w
        for k in writes:
            w = lastw.get(k)
            if w is not None and (is_dma or w.stream != st):
                deps[id(w)] = w
            rs = readers.get(k)
            if rs:
                for rst, r in rs.items():
                    if is_dma or rst != st:
                        deps[id(r)] = r
        lst = self.streams.setdefault(st, [])
        if is_dma and lst and st != "cc":
            p = lst[-1]
            deps[id(p)] = p
        deps.pop(id(o), None)
        best = {}
        for d in deps.values():
            cur = best.get(d.stream)
            if cur is None or cur.idx < d.idx:
                best[d.stream] = d
        for d in best.values():
            d.signal = True
        o.deps = list(best.values())
        self.nops += 1
        o.idx = self.nops
        for k in writes:
            lastw[k] = o
            readers[k] = {}
        for k in reads:
            rs = readers.get(k)
            if rs is None:
                rs = readers[k] = {}
            rs[st] = o
        lst.append(o)
        self.q[eng].append(o)
        return o

    def barrier(self, eng, ops):
        o = Op()
        o.eng = eng
        o.emit = None
        o.signal = False
        o.sem = None
        o.val = 0
        o.epoch = 0
        o.stream = eng
        o.inc = 1
        o.idx = 0
        for d in ops:
            d.signal = True
        o.deps = list(ops)
        self.q[eng].append(o)
        return o

    def finalize(self):
        for st, lst in self.streams.items():
            cnt = 0
            epoch = 0
            sem = None
            for o in lst:
                if not o.signal:
                    continue
                if sem is None or cnt + o.inc > SEM_LIMIT:
                    sem = self.sem_pool.pop()
                    cnt = 0
                    epoch += 1
                cnt += o.inc
                o.sem = sem
                o.val = cnt
                o.epoch = epoch

    def replay(self, eng, e):
        waited = {}
        for o in self.q[eng]:
            best = {}
            for d in o.deps:
                cur = best.get(d.stream)
                if cur is None or (cur.epoch, cur.val) < (d.epoch, d.val):
                    best[d.stream] = d
            for d in best.values():
                cur = waited.get(d.stream)
                key = (d.epoch, d.val)
                if cur is not None and cur >= key:
                    continue
                e.wait_ge(d.sem, d.val)
                waited[d.stream] = key
            if o.emit is not None:
                inst = o.emit(e)
                if o.signal:
                    inst.then_inc(o.sem, o.inc)


class Buf:
    def __init__(self, arena, off, n, dt):
        self.esz = 4 if dt is F32 else 2
        self.off = off
        self.n = n
        self.dt = dt
        nb = n * self.esz
        assert off % 4 == 0 and nb % 4 == 0
        ap = arena[:, off // 4:(off + nb) // 4]
        if dt is not F32:
            ap = ap.bitcast(dt)
        self.ap = ap

    def _keys(self, a, b):
        lo = self.off + a * self.esz
        hi = self.off + b * self.esz
        return tuple(range(lo // G, (hi - 1) // G + 1))

    def cols(self, a, b):
        return View(self.ap[:, a:b], self._keys(a, b))

    def sub(self, i, m):
        return self.cols(i * m, (i + 1) * m)

    def all(self):
        return View(self.ap, self._keys(0, self.n))

    def all3(self, m):
        return View(self.ap.rearrange("p (n m) -> p n m", m=m), self._keys(0, self.n))

    def range3(self, i0, i1, m):
        return View(self.ap[:, i0 * m:i1 * m].rearrange("p (n m) -> p n m", m=m), self._keys(i0 * m, i1 * m))


def build_nc(preconv=True, dbg=None):
    dbg = dbg or {}
    nc = bass.Bass("TRN2", target_bir_lowering=False)

    def din(name, shape, dt=F32):
        return nc.dram_tensor(name, shape, dt, kind="ExternalInput").ap()

    xT = din("xT", [KC, 128, TOK])
    gains_d = din("gains", [128, 4 * KC])
    small_d = din("small", [128, 16])
    wgu_d = [din("wgu1", [JC, 128, 2 * KC * 128]), din("wgu2", [JC, 128, 2 * KC * 128])]
    wd_d = [din("wd1", [KC, 128, JC * 128]), din("wd2", [KC, 128, JC * 128])]
    winc_d = din("winc", [36, 128, KC * 128])
    wout_d = din("wout", [KC, 128, KC * 128])
    rope_d = din("rope", [4, 128, TOK])
    perm_d = din("perm", [128, 3 * 128])
    outT = nc.dram_tensor("outT", [KC, 128, TOK], F32, kind="ExternalOutput").ap()

    skind = dict(kind="ExternalOutput") if dbg else {}
    X1 = nc.dram_tensor("X1", [KC, 128, TOK], F32, **skind).ap()
    QS = nc.dram_tensor("QS", [16, 128, TOK], BF16, **skind).ap()
    KL = [nc.dram_tensor("KL%d" % i, [128, TOK], BF16).ap() for i in range(10)]
    VL = [nc.dram_tensor("VL%d" % i, [128, TOK], BF16).ap() for i in range(10)]
    KGt = [nc.dram_tensor("KG%d" % i, [512, TOK], BF16).ap() for i in range(10)]
    VGt = [nc.dram_tensor("VG%d" % i, [512, TOK], BF16).ap() for i in range(10)]
    AT = nc.dram_tensor("AT", [16, 128, TOK], BF16, **skind).ap()
    if preconv:
        wgu_b = [nc.dram_tensor("wgu1b", [JC, 128, 2 * KC * 128], BF16).ap(),
                 nc.dram_tensor("wgu2b", [JC, 128, 2 * KC * 128], BF16).ap()]
        wd_b = [nc.dram_tensor("wd1b", [KC, 128, JC * 128], BF16).ap(),
                nc.dram_tensor("wd2b", [KC, 128, JC * 128], BF16).ap()]
        winc_b = nc.dram_tensor("wincb", [36, 128, KC * 128], BF16).ap()
        wout_b = nc.dram_tensor("woutb", [KC, 128, KC * 128], BF16).ap()

    dkeys = {}

    def dk(*name):
        k = dkeys.get(name)
        if k is None:
            k = dkeys[name] = DR_KEY + len(dkeys)
        return k

    with ExitStack() as es:
        arena = es.enter_context(nc.sbuf_tensor("arena", [128, ARENA_BYTES // 4], F32))
        ps = es.enter_context(nc.psum_tensor("ps", [128, 8, 512], F32))
        sems = [es.enter_context(nc.semaphore("s%d" % i)) for i in range(90)]
        block = es.enter_context(nc.Block())
        S = Sched(sems)
        K = 1024

        def psv(b):
            return View(ps[:, b, :], (PS_KEY + b,))

        def psbf(b, a, c):
            return View(ps[:, b, :].bitcast(BF16)[:, a:c], (PS_KEY + b,))

        def psc(b, a, c):
            return View(ps[:, b, a:c], (PS_KEY + b,))

        cbase = 176 * K
        gains = Buf(arena, cbase, 64, F32)
        small = Buf(arena, cbase + 256, 16, F32)
        misc = Buf(arena, cbase + 512, 16, F32)
        ones_f = Buf(arena, cbase + 768, 128, F32)
        perm_bf = Buf(arena, cbase + 768 + 512, 3 * 128, BF16)
        ones_bf = Buf(arena, cbase + 768 + 512 + 768, 128, BF16)

        xt = Buf(arena, 0, KC * T, F32)
        hT = Buf(arena, 32 * K, KC * T, BF16)
        actT = Buf(arena, 48 * K, JC * T, BF16)
        wgu = [Buf(arena, 92 * K + i * 8 * K, 2 * KC * 128, BF16) for i in range(3)]
        wdh = [Buf(arena, 116 * K + i * 5632, 22 * 128, BF16) for i in range(3)]
        mb = 133 * K
        sqb = [Buf(arena, mb + i * 2 * K, T, F32) for i in range(2)]
        rstd = [Buf(arena, mb + 4 * K + i * 2 * K, T, F32) for i in range(2)]
        sg = [Buf(arena, mb + 8 * K + i * 2 * K, T, F32) for i in range(2)]
        pb0 = 48 * K
        wqk = [Buf(arena, pb0 + i * 4 * K, KC * 128, BF16) for i in range(3)]
        ropeb = Buf(arena, pb0 + 12 * K, 4 * T, F32)
        t1b = [Buf(arena, pb0 + 20 * K + i * 2 * K, T, F32) for i in range(2)]
        t2b = [Buf(arena, pb0 + 24 * K + i * 2 * K, T, F32) for i in range(2)]
        qnb = [Buf(arena, pb0 + 28 * K + i * K, T, BF16) for i in range(2)]
        resb = [Buf(arena, pb0 + 30 * K + i * K, T, BF16) for i in range(3)]
        vTb = [Buf(arena, pb0 + 33 * K + i * K, T, BF16) for i in range(2)]

        R = [Buf(arena, i * 32 * K, SEQ, BF16) for i in range(4)]
        p2 = 128 * K
        Qt = [Buf(arena, p2 + i * K, T, BF16) for i in range(4)]
        NPB = 6
        Pb = [Buf(arena, p2 + 4 * K + i * 2 * K, 2 * T, BF16) for i in range(NPB)]
        q2 = p2 + 16 * K
        o1 = Buf(arena, q2, 2 * T, F32)
        tmpb = Buf(arena, q2 + 4 * K, 2 * T, F32)
        sqd = Buf(arena, q2 + 8 * K, 2 * T, F32)
        sums = [Buf(arena, q2 + 12 * K + i * 2 * K, T, F32) for i in range(2)]
        rs3 = Buf(arena, q2 + 16 * K, T, F32)
        ost = [Buf(arena, q2 + 18 * K + i * K, T, BF16) for i in range(3)]
        accD = [Buf(arena, q2 + 21 * K + i * 2 * K, T, F32) for i in range(2)]
        accP = [Buf(arena, q2 + 25 * K + i * 2 * K, T, F32) for i in range(2)]
        assert q2 + 29 * K <= 176 * K

        def mm(out, l, r, start, stop):
            return S.op("pe", lambda e, o=out.ap, a=l.ap, b=r.ap: e.matmul(o, a, b, start=start, stop=stop),
                        reads=l.keys + r.keys, writes=out.keys)

        def tr(out, in_, ident):
            return S.op("pe", lambda e, o=out.ap, a=in_.ap, b=ident.ap: e.transpose(o, a, b),
                        reads=in_.keys + ident.keys, writes=out.keys)

        def act(out, in_, func, scale=None, bias=None):
            kw = {}
            rk = in_.keys
            if scale is not None:
                kw["scale"] = scale
            if bias is not None:
                kw["bias"] = bias.ap
                rk = rk + bias.keys
            return S.op("act", lambda e, o=out.ap, i=in_.ap: e.activation(out=o, in_=i, func=func, **kw),
                        reads=rk, writes=out.keys)

        def stt(out, in0, scalar, in1, op0, op1):
            rk = in0.keys + in1.keys
            sc = scalar
            if isinstance(scalar, View):
                rk = rk + scalar.keys
                sc = scalar.ap
            return S.op("dve", lambda e, o=out.ap, a=in0.ap, b=in1.ap: e.scalar_tensor_tensor(
                out=o, in0=a, scalar=sc, in1=b, op0=op0, op1=op1), reads=rk, writes=out.keys)

        def tt(out, in0, in1, op, eng="dve", nosame=False):
            return S.op(eng, lambda e, o=out.ap, a=in0.ap, b=in1.ap: e.tensor_tensor(out=o, in0=a, in1=b, op=op),
                        reads=in0.keys + in1.keys, writes=out.keys, nosame=nosame)

        def tsc(out, in0, s1, op0):
            return S.op("dve", lambda e, o=out.ap, a=in0.ap: e.tensor_scalar(
                out=o, in0=a, scalar1=s1, scalar2=None, op0=op0), reads=in0.keys, writes=out.keys)

        def cp(out, in_, eng="dve", nosame=False):
            if eng == "act":
                return act(out, in_, AF.Copy)
            return S.op(eng, lambda e, o=out.ap, i=in_.ap: e.tensor_copy(out=o, in_=i),
                        reads=in_.keys, writes=out.keys, nosame=nosame)

        def recip(out, in_):
            return S.op("dve", lambda e, o=out.ap, i=in_.ap: e.reciprocal(out=o, in_=i),
                        reads=in_.keys, writes=out.keys)

        def memset(v, val, eng="dve"):
            return S.op(eng, lambda e, a=v.ap: e.memset(a, val), writes=v.keys)

        def dma(eng, out_ap, in_ap, reads, writes, stream):
            return S.op(eng, lambda e, o=out_ap, i=in_ap: e.dma_start(out=o, in_=i),
                        reads=reads, writes=writes, dma=stream)

        cv_i = [0]

        def conv(dst_ap, src_ap, key, after=()):
            st = "cv%d" % (cv_i[0] % 6)
            cv_i[0] += 1
            dma("pool", dst_ap, src_ap, after, (key,), st)

        def wload(dst, src_b, src_d, key, stream):
            if preconv:
                dma("sp", dst.ap, src_b, (key,), dst.keys, stream)
            else:
                dma("pool", dst.ap, src_d, (), dst.keys, stream)

        dma("sp", gains.ap, gains_d, (), gains.all().keys, "ldc0")
        dma("sp", small.ap, small_d, (), small.all().keys, "ldc1")
        dma("pool", perm_bf.ap, perm_d, (), perm_bf.all().keys, "ldc2")
        memset(ones_f.all(), 1.0)
        memset(ones_bf.all(), 1.0)
        memset(misc.all(), 0.0)
        memset(misc.cols(0, 1), EPS)
        eps_c = misc.cols(0, 1)
        permA = perm_bf.cols(0, 128)
        permB = perm_bf.cols(128, 256)
        ident = perm_bf.cols(256, 384)
        tt(misc.cols(1, 2), small.cols(4, 5), small.cols(5, 6), ALU.mult)
        tt(misc.cols(2, 3), small.cols(6, 7), small.cols(7, 8), ALU.mult)
        mm(psc(7, 0, 2), ones_f.all(), misc.cols(1, 3), True, True)
        act(misc.cols(3, 5), psc(7, 0, 2), AF.Exp)
        tt(misc.cols(5, 6), misc.cols(4, 5), misc.cols(3, 4), ALU.subtract)
        tsc(misc.cols(5, 6), misc.cols(5, 6), -LAMBDA_INIT, ALU.add)
        tsc(misc.cols(6, 8), small.cols(8, 10), 1.0 - LAMBDA_INIT, ALU.mult)
        neglam = misc.cols(5, 6)

        if preconv:
            def conv_ffn(f):
                for j in range(JC):
                    conv(wgu_b[f][j], wgu_d[f][j], dk("wgu", f, j))
                for m in range(KC):
                    conv(wd_b[f][m], wd_d[f][m], dk("wd", f, m))
            conv_ffn(0)
            for c in range(36):
                conv(winc_b[c], winc_d[c], dk("winc", c))
            late_conv = [(wout_b[m], wout_d[m], dk("wout", m)) for m in range(KC)]
            late_conv += [(wgu_b[1][j], wgu_d[1][j], dk("wgu", 1, j)) for j in range(JC)]
            late_conv += [(wd_b[1][m], wd_d[1][m], dk("wd", 1, m)) for m in range(KC)]


        def rmsnorm_stats(src, nchunks, dim, psb, rbuf):
            for kc in range(nchunks):
                sq = sqb[kc % 2].all()
                act(sq, src(kc), AF.Square)
                mm(psv(psb), ones_f.all(), sq, kc == 0, kc == nchunks - 1)
            act(rbuf, psv(psb), AF.Sqrt, scale=1.0 / dim, bias=eps_c)
            recip(rbuf, rbuf)

        def norm_to_hT(gbase):
            r = rstd[0].all()
            rmsnorm_stats(lambda kc: xt.sub(kc, T), KC, D, 6, r)
            for kc in range(KC):
                stt(hT.sub(kc, T), xt.sub(kc, T), gains.cols(gbase + kc, gbase + kc + 1), r, ALU.mult, ALU.mult)

        def ffn(f):
            def ld_gu(j):
                s = wgu[j % 3]
                wload(s.all(), wgu_b[f][j] if preconv else None, wgu_d[f][j], dk("wgu", f, j), "ldgu%d" % (j % 3))

            def ld_d(i):
                m, half = divmod(i, 2)
                s = wdh[i % 3]
                sb = wd_b[f][m][:, half * 2816:(half + 1) * 2816] if preconv else None
                sd = wd_d[f][m][:, half * 2816:(half + 1) * 2816]
                wload(s.all(), sb, sd, dk("wd", f, m), "ldd%d" % (i % 3))

            for j in range(3):
                ld_gu(j)
            for i in range(3):
                ld_d(i)
            for j in range(JC):
                s = wgu[j % 3]
                pg, pu = (0, 1) if j % 2 == 0 else (2, 3)
                for kc in range(KC):
                    mm(psv(pg), s.sub(kc, 128), hT.sub(kc, T), kc == 0, kc == KC - 1)
                for kc in range(KC):
                    mm(psv(pu), s.sub(KC + kc, 128), hT.sub(kc, T), kc == 0, kc == KC - 1)
                if j + 3 < JC:
                    ld_gu(j + 3)
                sgv = sg[j % 2].all()
                act(sgv, psv(pg), AF.Silu)
                tt(actT.sub(j, T), sgv, psv(pu), ALU.mult)
            for m in range(KC):
                pbk = 4 + m % 2
                for half in range(2):
                    i = 2 * m + half
                    s = wdh[i % 3]
                    for jj in range(22):
                        jc = half * 22 + jj
                        mm(psv(pbk), s.sub(jj, 128), actT.sub(jc, T), jc == 0, jc == JC - 1)
                    if i + 3 < 2 * KC:
                        ld_d(i + 3)
                stt(xt.sub(m, T), psv(pbk), 0.5, xt.sub(m, T), ALU.mult, ALU.add)

        def load_tile(dst, src, keys, stream):
            for hh in range(2):
                dma("sp", dst.range3(hh * 8, hh * 8 + 8, T).ap,
                    src[hh * 8:hh * 8 + 8].rearrange("k p t -> p k t"),
                    keys, dst.range3(hh * 8, hh * 8 + 8, T).keys, "%s%d" % (stream, hh))

        def store_tile(dst, src, keys, stream):
            ops = []
            for hh in range(2):
                ops.append(dma("sp", dst[hh * 8:hh * 8 + 8].rearrange("k p t -> p k t"),
                               src.range3(hh * 8, hh * 8 + 8, T).ap,
                               src.range3(hh * 8, hh * 8 + 8, T).keys, keys, "%s%d" % (stream, hh)))
            return ops

        def qk_spec(c):
            if c < 8:
                return True, True, 0, "Q", c
            if c < 10:
                return True, True, 1, "K", c - 8
            if c < 18:
                return True, False, 2, "Q", 8 + (c - 10)
            if c < 26:
                return True, False, 3, "K", 2 + (c - 18)
            return False, False, 0, "V", c - 26

        for t in range(dbg.get("nt1", NT)):
            tsl = slice(t * T, (t + 1) * T)
            load_tile(xt, xT[:, :, tsl], (), "ldx")
            norm_to_hT(0)
            ffn(0)
            store_tile(X1[:, :, tsl], xt, (dk("X1", t),), "stx")
            norm_to_hT(KC)
            dma("sp", ropeb.all3(T).ap, rope_d[:, :, tsl].rearrange("f p t -> p f t"), (), ropeb.all().keys, "ldrope")

            def ld_w(c):
                wload(wqk[c % 3].all(), winc_b[c] if preconv else None, winc_d[c], dk("winc", c), "ldqk%d" % (c % 3))

            for c in range(3):
                ld_w(c)
            PBK = (0, 1, 4)

            def stage_proj(c):
                s = wqk[c % 3]
                pbk = PBK[c % 3]
                for kc in range(KC):
                    mm(psv(pbk), s.sub(kc, 128), hT.sub(kc, T), kc == 0, kc == KC - 1)
                if c + 3 < 36:
                    ld_w(c + 3)
                if qk_spec(c)[0]:
                    act(sqb[c % 2].all(), psv(pbk), AF.Square)
                else:
                    act(vTb[c % 2].all(), psv(pbk), AF.Copy)

            def stage_ss(c):
                isqk, useA, gcol, kind, chunk = qk_spec(c)
                pbk = PBK[c % 3]
                if isqk:
                    r = rstd[c % 2].all()
                    mm(psv(2), ones_f.all(), sqb[c % 2].all(), True, True)
                    act(r, psv(2), AF.Sqrt, scale=1.0 / 128, bias=eps_c)
                    recip(r, r)
                    stt(qnb[c % 2].all(), psv(pbk), small.cols(gcol, gcol + 1), r, ALU.mult, ALU.mult)
                else:
                    res = resb[c % 3].all()
                    for s4 in range(4):
                        tr(psbf(5, s4 * 128, (s4 + 1) * 128), vTb[c % 2].cols(s4 * 128, (s4 + 1) * 128), ident)
                    cp(res, psbf(5, 0, 512))
                    dma("sp", VL[chunk][:, tsl], res.ap, res.keys,
                        (dk("VL", chunk, t),), "stres%d" % (c % 3))

            def stage_rot(c):
                isqk, useA, gcol, kind, chunk = qk_spec(c)
                if not isqk:
                    return
                res = resb[c % 3].all()
                qn = qnb[c % 2].all()
                mm(psv(3), permA if useA else permB, qn, True, True)
                cosv = ropeb.sub(0 if useA else 2, T)
                sinv = ropeb.sub(1 if useA else 3, T)
                t1 = t1b[c % 2].all()
                t2 = t2b[c % 2].all()
                tt(t1, qn, cosv, ALU.mult)
                tt(t2, psv(3), sinv, ALU.mult)
                tt(res, t1, t2, ALU.add)
                if kind == "Q":
                    dma("sp", QS[chunk][:, tsl], res.ap, res.keys, (dk("QS", chunk, t),), "stres%d" % (c % 3))
                else:
                    dma("sp", KL[chunk][:, tsl], res.ap, res.keys,
                        (dk("KL", chunk, t),), "stres%d" % (c % 3))

            for c in range(36 + 2):
                if c < 36:
                    stage_proj(c)
                if 1 <= c <= 36:
                    stage_ss(c - 1)
                if c >= 2:
                    stage_rot(c - 2)

        RUN_REST = dbg.get("stop") != "p1"
        def rest():
            groups4 = [[0, 1, 2, 3], [4, 5, 6, 7]]

            def gather(src_l, dst_l, nm, ch):
                S.op("pool", lambda e, a=src_l[ch], b=dst_l[ch]: e.collective_compute(
                    "AllGather", ALU.bypass, replica_groups=groups4, ins=[a.opt()], outs=[b.opt()]),
                    reads=tuple(dk(nm + "L", ch, t) for t in range(NT)), writes=(dk(nm + "G", ch),), dma="cc", inc=1)
            for ch in (0, 1):
                gather(KL, KGt, "K", ch)
                gather(VL, VGt, "V", ch)
            for h in range(4):
                for ch in (2 + 2 * h, 3 + 2 * h):
                    gather(KL, KGt, "K", ch)
                for ch in (2 + 2 * h, 3 + 2 * h):
                    gather(VL, VGt, "V", ch)

            if dbg.get("stop") == "cc":
                S.barrier("pool", [S.streams["cc"][-1]])
                return [lst[-1] for st, lst in S.streams.items() if st not in ("pe", "act", "dve", "pool", "cc")]
            def ld_kv(ri, src, chunk, nm):
                dma("sp", R[ri].ap.rearrange("p (r t) -> p r t", r=4), src[chunk].rearrange("(r p) t -> p r t", p=128),
                    (dk(nm, chunk),), R[ri].all().keys, "ldR%d" % ri)

            groups = []
            for g in range(2):
                for r in range(4):
                    for qg in range(NT):
                        groups.append(dict(kind="A", k=g, v=[2 + g], qchunk=g * 4 + r, qg=qg, first=(r == 0 and qg == 0),
                                           unit=("A", g)))
            for h in range(4):
                for qg in range(NT):
                    for c in range(2):
                        groups.append(dict(kind="B", k=c, v=[2, 3], qchunk=8 + 2 * h + c, qg=qg, h=h, c=c,
                                           first=(qg == 0 and c == 0), unit=("B", h)))
            if dbg.get("groups") == "small":
                groups = [gr for gr in groups if gr["qg"] < 2 and ((gr["kind"] == "A" and gr["qchunk"] == 0) or
                                                                   (gr["kind"] == "B" and gr["h"] == 0))]
            NG = len(groups)

            def unit_loads(unit):
                if unit[0] == "A":
                    g = unit[1]
                    ld_kv(g, KGt, g, "KG")
                    ld_kv(2 + g, VGt, g, "VG")
                else:
                    h = unit[1]
                    ld_kv(0, KGt, 2 + 2 * h, "KG")
                    ld_kv(1, KGt, 2 + 2 * h + 1, "KG")
                    ld_kv(2, VGt, 2 + 2 * h, "VG")
                    ld_kv(3, VGt, 2 + 2 * h + 1, "VG")

            def ld_q(gi):
                gr = groups[gi]
                qsl = slice(gr["qg"] * T, (gr["qg"] + 1) * T)
                dma("sp", Qt[gi % 4].ap, QS[gr["qchunk"]][:, qsl], (dk("QS", gr["qchunk"], gr["qg"]),),
                    Qt[gi % 4].all().keys, "ldq%d" % (gi % 4))

            ost_i = [0]

            def store_att(chunk, qg, v):
                dma("sp", AT[chunk][:, qg * T:(qg + 1) * T], v.ap, v.keys, (dk("AT", chunk, qg),),
                    "stat%d" % (ost_i[0] % 3))

            def next_ost():
                ost_i[0] += 1
                return ost[ost_i[0] % 3].all()

            PC0 = 192

            def obanks_of(gi):
                gr = groups[gi]
                if gr["kind"] == "A":
                    return [4 + gi % 2]
                return [4, 5]

            def rcp_via_act(dst, src_ps, scale_in, bias, pw):
                act(dst, src_ps, AF.Ln, scale=scale_in, bias=bias)
                act(dst, dst, AF.Exp, scale=pw)

            def epilogue1(gi):
                gr = groups[gi]
                obanks = obanks_of(gi)
                if gr["kind"] == "B":
                    for half in range(2):
                        act(tmpb.sub(half, T), psv(obanks[half]), AF.Copy)

            def epilogue(gi):
                gr = groups[gi]
                a = gi % 2
                obanks = obanks_of(gi)
                tt(accD[a].cols(PC0, T), accD[a].cols(PC0, T), accP[a].cols(PC0, T), ALU.add)
                isA = gr["kind"] == "A"
                tb = 6 + gi % 2 if isA else 6
                mm(psv(tb), ones_f.all(), accD[a].all(), not isA, True)
                sm = sums[a].all()
                rcp_via_act(sm, psv(tb), 1.0, None, -1.0)
                if gr["kind"] == "A":
                    o = next_ost()
                    tt(o, psv(obanks[0]), sm, ALU.mult)
                    store_att(gr["qchunk"], gr["qg"], o)
                elif gr["c"] == 0:
                    for half in range(2):
                        tt(o1.sub(half, T), tmpb.sub(half, T), sm, ALU.mult)
                else:
                    for half in range(2):
                        stt(tmpb.sub(half, T), tmpb.sub(half, T), neglam, sm, ALU.mult, ALU.mult)
                        tt(o1.sub(half, T), o1.sub(half, T), tmpb.sub(half, T), ALU.add)
                    for half in range(2):
                        act(sqd.sub(half, T), o1.sub(half, T), AF.Square)
                        mm(psv(7), ones_f.all(), sqd.sub(half, T), half == 0, half == 1)
                    rcp_via_act(rs3.all(), psv(7), 1.0 / 256, eps_c, -0.5)
                    for half in range(2):
                        o = next_ost()
                        stt(o, o1.sub(half, T), misc.cols(6 + half, 7 + half), rs3.all(), ALU.mult, ALU.mult)
                        store_att(8 + 2 * gr["h"] + half, gr["qg"], o)

            unit_loads(("A", 0))
            unit_loads(("A", 1))
            ld_q(0)
            ld_q(1)
            NKS = SEQ // 256
            steps = [(gi, s) for gi in range(NG) for s in range(NKS)]

            SUMP = 6

            def pe_sum_step(gi, s):
                return groups[gi]["kind"] == "A" and s % SUMP == SUMP - 1

            def emit_qk(i):
                gi, s = steps[i]
                gr = groups[gi]
                if s == 0 and gi + 2 < NG:
                    ld_q(gi + 2)
                if preconv and late_conv and i >= 64 and i % 24 == 0:
                    d_, s_, k_ = late_conv.pop(0)
                    conv(d_, s_, k_)
                p = i % 2
                for u in range(2):
                    kt = 2 * s + u
                    mm(psv(2 * p + u), R[gr["k"]].cols(kt * 128, (kt + 1) * 128), Qt[gi % 4].all(), True, True)
                Pv = Pb[i % NPB]
                S.op("act", lambda e, o=Pv.all3(T).ap, i_=ps[:, 2 * p:2 * p + 2, :]: e.activation(
                    out=o, in_=i_, func=AF.Exp, scale=SCALE),
                    reads=(PS_KEY + 2 * p, PS_KEY + 2 * p + 1), writes=Pv.all().keys)
                a = gi % 2
                if pe_sum_step(gi, s):
                    return
                if s == 0:
                    cp(accD[a].all(), Pv.sub(0, T), nosame=True)
                    cp(accP[a].cols(PC0, T), Pv.cols(T + PC0, 2 * T), eng="pool", nosame=True)
                else:
                    tt(accD[a].all(), accD[a].all(), Pv.sub(0, T), ALU.add, nosame=True)
                    tt(accP[a].cols(PC0, T), accP[a].cols(PC0, T), Pv.cols(T + PC0, 2 * T), ALU.add, eng="pool", nosame=True)
                tt(accD[a].cols(0, PC0), accD[a].cols(0, PC0), Pv.cols(T, T + PC0), ALU.add, nosame=True)

            def emit_pv(i):
                gi, s = steps[i]
                gr = groups[gi]
                obanks = obanks_of(gi)
                Pv = Pb[i % NPB]
                for u in range(2):
                    kt = 2 * s + u
                    for hf, ob in enumerate(obanks):
                        mm(psv(ob), R[gr["v"][hf]].cols(kt * 128, (kt + 1) * 128), Pv.sub(u, T),
                           kt == 0, kt == 2 * NKS - 1)
                    if pe_sum_step(gi, s):
                        mm(psv(6 + gi % 2), ones_bf.all(), Pv.sub(u, T), s == SUMP - 1 and u == 0, False)
                if s == NKS - 1:
                    epilogue1(gi)
                    pending.append((i + 1 + EPI_DELAY, gi))

            NTL = len(steps)
            pending = []
            EPI_DELAY = 4
            for i in range(NTL + 1):
                while pending and pending[0][0] <= i:
                    epilogue(pending.pop(0)[1])
                pv_done = False
                if i < NTL:
                    gi, s = steps[i]
                    gr = groups[gi]
                    if s == 0 and gr["first"] and gr["kind"] == "B":
                        if i >= 1:
                            emit_pv(i - 1)
                            pv_done = True
                        unit_loads(gr["unit"])
                    emit_qk(i)
                if i >= 1 and not pv_done:
                    emit_pv(i - 1)
            while pending:
                epilogue(pending.pop(0)[1])

            out_stores = []
            for t in range(0 if dbg.get("stop") == "p2" else NT):
                tsl = slice(t * T, (t + 1) * T)
                atkeys = tuple(dk("AT", ch, t) for ch in range(16))
                load_tile(hT, AT[:, :, tsl], atkeys, "ldat")
                load_tile(xt, X1[:, :, tsl], (dk("X1", t),), "ldx")

                def ld_wo(m):
                    wload(wqk[m % 3].all(), wout_b[m] if preconv else None, wout_d[m], dk("wout", m), "ldqk%d" % (m % 3))

                for m in range(3):
                    ld_wo(m)
                for m in range(KC):
                    s = wqk[m % 3]
                    for kc in range(KC):
                        mm(psv(m % 2), s.sub(kc, 128), hT.sub(kc, T), kc == 0, kc == KC - 1)
                    if m + 3 < KC:
                        ld_wo(m + 3)
                    tt(xt.sub(m, T), psv(m % 2), xt.sub(m, T), ALU.add)
                norm_to_hT(2 * KC)
                ffn(1)
                r = rstd[0].all()
                rmsnorm_stats(lambda kc: xt.sub(kc, T), KC, D, 6, r)
                for kc in range(KC):
                    stt(xt.sub(kc, T), xt.sub(kc, T), gains.cols(3 * KC + kc, 3 * KC + kc + 1), r, ALU.mult, ALU.mult)
                out_stores += store_tile(outT[:, :, tsl], xt, (dk("OUT", t),), "sto")

            if dbg.get("stop") == "p2":
                return [lst[-1] for st, lst in S.streams.items() if st not in ("pe", "act", "dve", "pool")]
            return out_stores

        if RUN_REST:
            out_stores = rest()
        else:
            out_stores = [lst[-1] for st, lst in S.streams.items() if st not in ("pe", "act", "dve", "pool")]
        S.barrier("sp", out_stores)
        S.finalize()

        @block.tensor
        def _(e):
            S.replay("pe", e)

        @block.scalar
        def _(e):
            S.replay("act", e)

        @block.vector
        def _(e):
            S.replay("dve", e)

        @block.gpsimd
        def _(e):
            S.replay("pool", e)

        @block.sync
        def _(e):
            S.replay("sp", e)

    return nc


def _rope_tables(qtr):
    s = (np.arange(TOK, dtype=np.float32) + np.float32(qtr * TOK))
    row = np.floor(s / 64).astype(np.float32)
    col = (s - row * 64).astype(np.float32)
    fA = (np.float32(10000.0) ** (-np.arange(0, 64, 2, dtype=np.float32) / np.float32(64))).astype(np.float32)
    fB = (np.float32(500000.0) ** (-np.arange(0, 32, 2, dtype=np.float32) / np.float32(32))).astype(np.float32)
    angr = (row[None, :] * fA[:, None]).astype(np.float32)
    angc = (col[None, :] * fA[:, None]).astype(np.float32)
    angl = (s[None, :] * fB[:, None]).astype(np.float32)
    cosA = np.concatenate([np.cos(angr), np.cos(angr), np.cos(angc), np.cos(angc)], 0)
    sinA = np.concatenate([-np.sin(angr), np.sin(angr), -np.sin(angc), np.sin(angc)], 0)
    cosB = np.ones((128, TOK), np.float32)
    sinB = np.zeros((128, TOK), np.float32)
    cosB[0:16] = np.cos(angl)
    cosB[16:32] = np.cos(angl)
    sinB[0:16] = -np.sin(angl)
    sinB[16:32] = np.sin(angl)
    return np.ascontiguousarray(np.stack([cosA, sinA, cosB, sinB], 0).astype(np.float32))


def _perm_consts():
    PA = np.zeros((128, 128), np.float32)
    PB = np.zeros((128, 128), np.float32)
    for m in range(128):
        blk = (m // 64) * 64
        r = m - blk
        PA[blk + (r + 32) % 64, m] = 1.0
    for m in range(32):
        PB[(m + 16) % 32, m] = 1.0
    return np.ascontiguousarray(np.concatenate([PA, PB, np.eye(128, dtype=np.float32)], 1))


_NC_CACHE = {}


def kernel(x, ffn1_norm, ffn1_w_gu, ffn1_w_down, mix_norm, w_in,
           a_q_norm, a_k_norm, b_q_norm, b_k_norm,
           b_lambda_q1, b_lambda_k1, b_lambda_q2, b_lambda_k2, b_subln,
           w_out, ffn2_norm, ffn2_w_gu, ffn2_w_down, out_norm):
    f32 = np.float32
    x = np.asarray(x, f32)

    def lay_gu(w):
        w = np.asarray(w, f32)[0].reshape(KC, 128, 2, JC, 128)
        return np.ascontiguousarray(w.transpose(3, 1, 2, 0, 4)).reshape(JC, 128, 2 * KC * 128)

    def lay_d(w):
        w = np.asarray(w, f32)[0].reshape(JC, 128, KC, 128)
        return np.ascontiguousarray(w.transpose(2, 1, 0, 3)).reshape(KC, 128, JC * 128)

    def lay_sq(w):
        w = np.asarray(w, f32).reshape(KC, 128, KC, 128)
        return np.ascontiguousarray(w.transpose(2, 1, 0, 3)).reshape(KC, 128, KC * 128)

    win = np.asarray(w_in, f32)[0]
    col0 = ([c * 128 for c in range(8)] + [1024 + g * 128 for g in range(2)]
            + [1536 + i * 128 for i in range(8)] + [2560 + i * 128 for i in range(8)]
            + [1280 + g * 128 for g in range(2)] + [3584 + i * 128 for i in range(8)])
    winc = np.stack([win[:, c0:c0 + 128].reshape(KC, 128, 128).transpose(1, 0, 2).reshape(128, KC * 128)
                     for c0 in col0], 0)
    winc = np.ascontiguousarray(winc)

    def gl(v):
        return np.asarray(v, f32).reshape(KC, 128).T

    gains = np.ascontiguousarray(np.concatenate([gl(ffn1_norm), gl(mix_norm), gl(ffn2_norm), gl(out_norm)], 1))
    small = np.zeros((128, 16), f32)
    for i, v in enumerate([a_q_norm, a_k_norm, b_q_norm, b_k_norm, b_lambda_q1, b_lambda_k1, b_lambda_q2, b_lambda_k2]):
        small[:, i] = np.asarray(v, f32).reshape(128)
    sl = np.asarray(b_subln, f32).reshape(2, 128)
    small[:, 8] = sl[0]
    small[:, 9] = sl[1]

    common = dict(gains=gains, small=small, wgu1=lay_gu(ffn1_w_gu), wgu2=lay_gu(ffn2_w_gu),
                  wd1=lay_d(ffn1_w_down), wd2=lay_d(ffn2_w_down), winc=winc,
                  wout=lay_sq(np.asarray(w_out, f32)[0]), perm=_perm_consts())
    ropes = [_rope_tables(q) for q in range(4)]
    in_maps = []
    for c in range(8):
        b, q = divmod(c, 4)
        xs = x[b, q * TOK:(q + 1) * TOK, :]
        m = dict(common)
        m["xT"] = np.ascontiguousarray(xs.T).reshape(KC, 128, TOK)
        m["rope"] = ropes[q]
        in_maps.append(m)

    if "nc" not in _NC_CACHE:
        _NC_CACHE["nc"] = build_nc()
    nc = _NC_CACHE["nc"]
    res = run_bass_kernel_spmd(nc, in_maps, core_ids=list(range(8)))
    out = np.empty((2, SEQ, D), f32)
    for c in range(8):
        b, q = divmod(c, 4)
        o = np.asarray(res.results[c]["outT"], f32).reshape(D, TOK)
        out[b, q * TOK:(q + 1) * TOK, :] = o.T
    return out
```

```python
import math
from contextlib import ExitStack

import numpy as np
import concourse.bass as bass
import concourse.mybir as mybir
from concourse.bass_utils import run_bass_kernel_spmd

F32 = mybir.dt.float32
BF16 = mybir.dt.bfloat16
AF = mybir.ActivationFunctionType
ALU = mybir.AluOpType

D = 2048
KC = 16
DFF = 5632
JC = 44
T = 512
TOK = 4096
NT = TOK // T
SEQ = 16384
EPS = 1e-6
SCALE = 128 ** -0.5
LAMBDA_INIT = 0.8 - 0.6 * math.exp(0.0)
G = 256
ARENA_BYTES = 180 * 1024
PS_KEY = 1 << 20
DR_KEY = 1 << 21
SEM_LIMIT = 30000


class View:
    __slots__ = ("ap", "keys")

    def __init__(self, ap, keys):
        self.ap = ap
        self.keys = keys


class Op:
    __slots__ = ("eng", "emit", "deps", "signal", "sem", "val", "epoch", "stream", "inc", "idx")


class Sched:
    def __init__(self, sem_pool):
        self.sem_pool = list(sem_pool)
        self.q = {e: [] for e in ("pe", "act", "dve", "pool", "sp")}
        self.streams = {}
        self.lastw = {}
        self.readers = {}
        self.nops = 0

    def op(self, eng, emit, reads=(), writes=(), dma=None, inc=None, nosame=False):
        o = Op()
        o.eng = eng
        o.emit = emit
        o.signal = False
        o.sem = None
        o.val = 0
        o.epoch = 0
        o.stream = dma if dma is not None else eng
        o.inc = inc if inc is not None else (16 if dma is not None else 1)
        deps = {}
        st = o.stream
        is_dma = dma is not None
        lastw = self.lastw
        readers = self.readers
        for k in reads:
            w = lastw.get(k)
            if w is not None and not ((st == "pe" or nosame) and w.stream == st):
                deps[id(w)] = w
        for k in writes:
            w = lastw.get(k)
            if w is not None and (is_dma or w.stream != st):
                deps[id(w)] = w
            rs = readers.get(k)
            if rs:
                for rst, r in rs.items():
                    if is_dma or rst != st:
                        deps[id(r)] = r
        lst = self.streams.setdefault(st, [])
        if is_dma and lst and st != "cc":
            p = lst[-1]
            deps[id(p)] = p
        deps.pop(id(o), None)
        best = {}
        for d in deps.values():
            cur = best.get(d.stream)
            if cur is None or cur.idx < d.idx:
                best[d.stream] = d
        for d in best.values():
            d.signal = True
        o.deps = list(best.values())
        self.nops += 1
        o.idx = self.nops
        for k in writes:
            lastw[k] = o
            readers[k] = {}
        for k in reads:
            rs = readers.get(k)
            if rs is None:
                rs = readers[k] = {}
            rs[st] = o
        lst.append(o)
        self.q[eng].append(o)
        return o

    def barrier(self, eng, ops):
        o = Op()
        o.eng = eng
        o.emit = None
        o.signal = False
        o.sem = None
        o.val = 0
        o.epoch = 0
        o.stream = eng
        o.inc = 1
        o.idx = 0
        for d in ops:
            d.signal = True
        o.deps = list(ops)
        self.q[eng].append(o)
        return o

    def finalize(self):
        for st, lst in self.streams.items():
            cnt = 0
            epoch = 0
            sem = None
            for o in lst:
                if not o.signal:
                    continue
                if sem is None or cnt + o.inc > SEM_LIMIT:
                    sem = self.sem_pool.pop()
                    cnt = 0
                    epoch += 1
                cnt += o.inc
                o.sem = sem
                o.val = cnt
                o.epoch = epoch

    def replay(self, eng, e):
        waited = {}
        for o in self.q[eng]:
            best = {}
            for d in o.deps:
                cur = best.get(d.stream)
                if cur is None or (cur.epoch, cur.val) < (d.epoch, d.val):
                    best[d.stream] = d
            for d in best.values():
                cur = waited.get(d.stream)
                key = (d.epoch, d.val)
                if cur is not None and cur >= key:
                    continue
                e.wait_ge(d.sem, d.val)
                waited[d.stream] = key
            if o.emit is not None:
                inst = o.emit(e)
                if o.signal:
                    inst.then_inc(o.sem, o.inc)


class Buf:
    def __init__(self, arena, off, n, dt):
        self.esz = 4 if dt is F32 else 2
        self.off = off
        self.n = n
        self.dt = dt
        nb = n * self.esz
        assert off % 4 == 0 and nb % 4 == 0
        ap = arena[:, off // 4:(off + nb) // 4]
        if dt is not F32:
            ap = ap.bitcast(dt)
        self.ap = ap

    def _keys(self, a, b):
        lo = self.off + a * self.esz
        hi = self.off + b * self.esz
        return tuple(range(lo // G, (hi - 1) // G + 1))

    def cols(self, a, b):
        return View(self.ap[:, a:b], self._keys(a, b))

    def sub(self, i, m):
        return self.cols(i * m, (i + 1) * m)

    def all(self):
        return View(self.ap, self._keys(0, self.n))

    def all3(self, m):
        return View(self.ap.rearrange("p (n m) -> p n m", m=m), self._keys(0, self.n))

    def range3(self, i0, i1, m):
        return View(self.ap[:, i0 * m:i1 * m].rearrange("p (n m) -> p n m", m=m), self._keys(i0 * m, i1 * m))


def build_nc(preconv=True, dbg=None):
    dbg = dbg or {}
    nc = bass.Bass("TRN2", target_bir_lowering=False)

    def din(name, shape, dt=F32):
        return nc.dram_tensor(name, shape, dt, kind="ExternalInput").ap()

    xT = din("xT", [KC, 128, TOK])
    gains_d = din("gains", [128, 4 * KC])
    small_d = din("small", [128, 16])
    wgu_d = [din("wgu1", [JC, 128, 2 * KC * 128]), din("wgu2", [JC, 128, 2 * KC * 128])]
    wd_d = [din("wd1", [KC, 128, JC * 128]), din("wd2", [KC, 128, JC * 128])]
    winc_d = din("winc", [36, 128, KC * 128])
    wout_d = din("wout", [KC, 128, KC * 128])
    rope_d = din("rope", [4, 128, TOK])
    perm_d = din("perm", [128, 3 * 128])
    outT = nc.dram_tensor("outT", [KC, 128, TOK], F32, kind="ExternalOutput").ap()

    skind = dict(kind="ExternalOutput") if dbg else {}
    X1 = nc.dram_tensor("X1", [KC, 128, TOK], F32, **skind).ap()
    QS = nc.dram_tensor("QS", [16, 128, TOK], BF16, **skind).ap()
    KL = [nc.dram_tensor("KL%d" % i, [128, TOK], BF16).ap() for i in range(10)]
    VL = [nc.dram_tensor("VL%d" % i, [128, TOK], BF16).ap() for i in range(10)]
    KGt = [nc.dram_tensor("KG%d" % i, [512, TOK], BF16).ap() for i in range(10)]
    VGt = [nc.dram_tensor("VG%d" % i, [512, TOK], BF16).ap() for i in range(10)]
    AT = nc.dram_tensor("AT", [16, 128, TOK], BF16, **skind).ap()
    if preconv:
        wgu_b = [nc.dram_tensor("wgu1b", [JC, 128, 2 * KC * 128], BF16).ap(),
                 nc.dram_tensor("wgu2b", [JC, 128, 2 * KC * 128], BF16).ap()]
        wd_b = [nc.dram_tensor("wd1b", [KC, 128, JC * 128], BF16).ap(),
                nc.dram_tensor("wd2b", [KC, 128, JC * 128], BF16).ap()]
        winc_b = nc.dram_tensor("wincb", [36, 128, KC * 128], BF16).ap()
        wout_b = nc.dram_tensor("woutb", [KC, 128, KC * 128], BF16).ap()

    dkeys = {}

    def dk(*name):
        k = dkeys.get(name)
        if k is None:
            k = dkeys[name] = DR_KEY + len(dkeys)
        return k

    with ExitStack() as es:
        arena = es.enter_context(nc.sbuf_tensor("arena", [128, ARENA_BYTES // 4], F32))
        ps = es.enter_context(nc.psum_tensor("ps", [128, 8, 512], F32))
        sems = [es.enter_context(nc.semaphore("s%d" % i)) for i in range(90)]
        block = es.enter_context(nc.Block())
        S = Sched(sems)
        K = 1024

        def psv(b):
            return View(ps[:, b, :], (PS_KEY + b,))

        def psbf(b, a, c):
            return View(ps[:, b, :].bitcast(BF16)[:, a:c], (PS_KEY + b,))

        def psc(b, a, c):
            return View(ps[:, b, a:c], (PS_KEY + b,))

        cbase = 176 * K
        gains = Buf(arena, cbase, 64, F32)
        small = Buf(arena, cbase + 256, 16, F32)
        misc = Buf(arena, cbase + 512, 16, F32)
        ones_f = Buf(arena, cbase + 768, 128, F32)
        perm_bf = Buf(arena, cbase + 768 + 512, 3 * 128, BF16)
        ones_bf = Buf(arena, cbase + 768 + 512 + 768, 128, BF16)

        xt = Buf(arena, 0, KC * T, F32)
        hT = Buf(arena, 32 * K, KC * T, BF16)
        actT = Buf(arena, 48 * K, JC * T, BF16)
        wgu = [Buf(arena, 92 * K + i * 8 * K, 2 * KC * 128, BF16) for i in range(3)]
        wdh = [Buf(arena, 116 * K + i * 5632, 22 * 128, BF16) for i in range(3)]
        mb = 133 * K
        sqb = [Buf(arena, mb + i * 2 * K, T, F32) for i in range(2)]
        rstd = [Buf(arena, mb + 4 * K + i * 2 * K, T, F32) for i in range(2)]
        sg = [Buf(arena, mb + 8 * K + i * 2 * K, T, F32) for i in range(2)]
        pb0 = 48 * K
        wqk = [Buf(arena, pb0 + i * 4 * K, KC * 128, BF16) for i in range(3)]
        ropeb = Buf(arena, pb0 + 12 * K, 4 * T, F32)
        t1b = [Buf(arena, pb0 + 20 * K + i * 2 * K, T, F32) for i in range(2)]
        t2b = [Buf(arena, pb0 + 24 * K + i * 2 * K, T, F32) for i in range(2)]
        qnb = [Buf(arena, pb0 + 28 * K + i * K, T, BF16) for i in range(2)]
        resb = [Buf(arena, pb0 + 30 * K + i * K, T, BF16) for i in range(3)]
        vTb = [Buf(arena, pb0 + 33 * K + i * K, T, BF16) for i in range(2)]

        R = [Buf(arena, i * 32 * K, SEQ, BF16) for i in range(4)]
        p2 = 128 * K
        Qt = [Buf(arena, p2 + i * K, T, BF16) for i in range(4)]
        NPB = 6
        Pb = [Buf(arena, p2 + 4 * K + i * 2 * K, 2 * T, BF16) for i in range(NPB)]
        q2 = p2 + 16 * K
        o1 = Buf(arena, q2, 2 * T, F32)
        tmpb = Buf(arena, q2 + 4 * K, 2 * T, F32)
        sqd = Buf(arena, q2 + 8 * K, 2 * T, F32)
        sums = [Buf(arena, q2 + 12 * K + i * 2 * K, T, F32) for i in range(2)]
        rs3 = Buf(arena, q2 + 16 * K, T, F32)
        ost = [Buf(arena, q2 + 18 * K + i * K, T, BF16) for i in range(3)]
        accD = [Buf(arena, q2 + 21 * K + i * 2 * K, T, F32) for i in range(2)]
        accP = [Buf(arena, q2 + 25 * K + i * 2 * K, T, F32) for i in range(2)]
        assert q2 + 29 * K <= 176 * K

        def mm(out, l, r, start, stop):
            return S.op("pe", lambda e, o=out.ap, a=l.ap, b=r.ap: e.matmul(o, a, b, start=start, stop=stop),
                        reads=l.keys + r.keys, writes=out.keys)

        def tr(out, in_, ident):
            return S.op("pe", lambda e, o=out.ap, a=in_.ap, b=ident.ap: e.transpose(o, a, b),
                        reads=in_.keys + ident.keys, writes=out.keys)

        def act(out, in_, func, scale=None, bias=None):
            kw = {}
            rk = in_.keys
            if scale is not None:
                kw["scale"] = scale
            if bias is not None:
                kw["bias"] = bias.ap
                rk = rk + bias.keys
            return S.op("act", lambda e, o=out.ap, i=in_.ap: e.activation(out=o, in_=i, func=func, **kw),
                        reads=rk, writes=out.keys)

        def stt(out, in0, scalar, in1, op0, op1):
            rk = in0.keys + in1.keys
            sc = scalar
            if isinstance(scalar, View):
                rk = rk + scalar.keys
                sc = scalar.ap
            return S.op("dve", lambda e, o=out.ap, a=in0.ap, b=in1.ap: e.scalar_tensor_tensor(
                out=o, in0=a, scalar=sc, in1=b, op0=op0, op1=op1), reads=rk, writes=out.keys)

        def tt(out, in0, in1, op, eng="dve", nosame=False):
            return S.op(eng, lambda e, o=out.ap, a=in0.ap, b=in1.ap: e.tensor_tensor(out=o, in0=a, in1=b, op=op),
                        reads=in0.keys + in1.keys, writes=out.keys, nosame=nosame)

        def tsc(out, in0, s1, op0):
            return S.op("dve", lambda e, o=out.ap, a=in0.ap: e.tensor_scalar(
                out=o, in0=a, scalar1=s1, scalar2=None, op0=op0), reads=in0.keys, writes=out.keys)

        def cp(out, in_, eng="dve", nosame=False):
            if eng == "act":
                return act(out, in_, AF.Copy)
            return S.op(eng, lambda e, o=out.ap, i=in_.ap: e.tensor_copy(out=o, in_=i),
                        reads=in_.keys, writes=out.keys, nosame=nosame)

        def recip(out, in_):
            return S.op("dve", lambda e, o=out.ap, i=in_.ap: e.reciprocal(out=o, in_=i),
                        reads=in_.keys, writes=out.keys)

        def memset(v, val, eng="dve"):
            return S.op(eng, lambda e, a=v.ap: e.memset(a, val), writes=v.keys)

        def dma(eng, out_ap, in_ap, reads, writes, stream):
            return S.op(eng, lambda e, o=out_ap, i=in_ap: e.dma_start(out=o, in_=i),
                        reads=reads, writes=writes, dma=stream)

        cv_i = [0]

        def conv(dst_ap, src_ap, key, after=()):
            st = "cv%d" % (cv_i[0] % 6)
            cv_i[0] += 1
            dma("pool", dst_ap, src_ap, after, (key,), st)

        def wload(dst, src_b, src_d, key, stream):
            if preconv:
                dma("sp", dst.ap, src_b, (key,), dst.keys, stream)
            else:
                dma("pool", dst.ap, src_d, (), dst.keys, stream)

        dma("sp", gains.ap, gains_d, (), gains.all().keys, "ldc0")
        dma("sp", small.ap, small_d, (), small.all().keys, "ldc1")
        dma("pool", perm_bf.ap, perm_d, (), perm_bf.all().keys, "ldc2")
        memset(ones_f.all(), 1.0)
        memset(ones_bf.all(), 1.0)
        memset(misc.all(), 0.0)
        memset(misc.cols(0, 1), EPS)
        eps_c = misc.cols(0, 1)
        permA = perm_bf.cols(0, 128)
        permB = perm_bf.cols(128, 256)
        ident = perm_bf.cols(256, 384)
        tt(misc.cols(1, 2), small.cols(4, 5), small.cols(5, 6), ALU.mult)
        tt(misc.cols(2, 3), small.cols(6, 7), small.cols(7, 8), ALU.mult)
        mm(psc(7, 0, 2), ones_f.all(), misc.cols(1, 3), True, True)
        act(misc.cols(3, 5), psc(7, 0, 2), AF.Exp)
        tt(misc.cols(5, 6), misc.cols(4, 5), misc.cols(3, 4), ALU.subtract)
        tsc(misc.cols(5, 6), misc.cols(5, 6), -LAMBDA_INIT, ALU.add)
        tsc(misc.cols(6, 8), small.cols(8, 10), 1.0 - LAMBDA_INIT, ALU.mult)
        neglam = misc.cols(5, 6)

        if preconv:
            def conv_ffn(f):
                for j in range(JC):
                    conv(wgu_b[f][j], wgu_d[f][j], dk("wgu", f, j))
                for m in range(KC):
                    conv(wd_b[f][m], wd_d[f][m], dk("wd", f, m))
            conv_ffn(0)
            for c in range(36):
                conv(winc_b[c], winc_d[c], dk("winc", c))
            late_conv = [(wout_b[m], wout_d[m], dk("wout", m)) for m in range(KC)]
            late_conv += [(wgu_b[1][j], wgu_d[1][j], dk("wgu", 1, j)) for j in range(JC)]
            late_conv += [(wd_b[1][m], wd_d[1][m], dk("wd", 1, m)) for m in range(KC)]


        def rmsnorm_stats(src, nchunks, dim, psb, rbuf):
            for kc in range(nchunks):
                sq = sqb[kc % 2].all()
                act(sq, src(kc), AF.Square)
                mm(psv(psb), ones_f.all(), sq, kc == 0, kc == nchunks - 1)
            act(rbuf, psv(psb), AF.Sqrt, scale=1.0 / dim, bias=eps_c)
            recip(rbuf, rbuf)

        def norm_to_hT(gbase):
            r = rstd[0].all()
            rmsnorm_stats(lambda kc: xt.sub(kc, T), KC, D, 6, r)
            for kc in range(KC):
                stt(hT.sub(kc, T), xt.sub(kc, T), gains.cols(gbase + kc, gbase + kc + 1), r, ALU.mult, ALU.mult)

        def ffn(f):
            def ld_gu(j):
                s = wgu[j % 3]
                wload(s.all(), wgu_b[f][j] if preconv else None, wgu_d[f][j], dk("wgu", f, j), "ldgu%d" % (j % 3))

            def ld_d(i):
                m, half = divmod(i, 2)
                s = wdh[i % 3]
                sb = wd_b[f][m][:, half * 2816:(half + 1) * 2816] if preconv else None
                sd = wd_d[f][m][:, half * 2816:(half + 1) * 2816]
                wload(s.all(), sb, sd, dk("wd", f, m), "ldd%d" % (i % 3))

            for j in range(3):
                ld_gu(j)
            for i in range(3):
                ld_d(i)
            for j in range(JC):
                s = wgu[j % 3]
                pg, pu = (0, 1) if j % 2 == 0 else (2, 3)
                for kc in range(KC):
                    mm(psv(pg), s.sub(kc, 128), hT.sub(kc, T), kc == 0, kc == KC - 1)
                for kc in range(KC):
                    mm(psv(pu), s.sub(KC + kc, 128), hT.sub(kc, T), kc == 0, kc == KC - 1)
                if j + 3 < JC:
                    ld_gu(j + 3)
                sgv = sg[j % 2].all()
                act(sgv, psv(pg), AF.Silu)
                tt(actT.sub(j, T), sgv, psv(pu), ALU.mult)
            for m in range(KC):
                pbk = 4 + m % 2
                for half in range(2):
                    i = 2 * m + half
                    s = wdh[i % 3]
                    for jj in range(22):
                        jc = half * 22 + jj
                        mm(psv(pbk), s.sub(jj, 128), actT.sub(jc, T), jc == 0, jc == JC - 1)
                    if i + 3 < 2 * KC:
                        ld_d(i + 3)
                stt(xt.sub(m, T), psv(pbk), 0.5, xt.sub(m, T), ALU.mult, ALU.add)

        def load_tile(dst, src, keys, stream):
            for hh in range(2):
                dma("sp", dst.range3(hh * 8, hh * 8 + 8, T).ap,
                    src[hh * 8:hh * 8 + 8].rearrange("k p t -> p k t"),
                    keys, dst.range3(hh * 8, hh * 8 + 8, T).keys, "%s%d" % (stream, hh))

        def store_tile(dst, src, keys, stream):
            ops = []
            for hh in range(2):
                ops.append(dma("sp", dst[hh * 8:hh * 8 + 8].rearrange("k p t -> p k t"),
                               src.range3(hh * 8, hh * 8 + 8, T).ap,
                               src.range3(hh * 8, hh * 8 + 8, T).keys, keys, "%s%d" % (stream, hh)))
            return ops

        def qk_spec(c):
            if c < 8:
                return True, True, 0, "Q", c
            if c < 10:
                return True, True, 1, "K", c - 8
            if c < 18:
                return True, False, 2, "Q", 8 + (c - 10)
            if c < 26:
                return True, False, 3, "K", 2 + (c - 18)
            return False, False, 0, "V", c - 26

        for t in range(dbg.get("nt1", NT)):
            tsl = slice(t * T, (t + 1) * T)
            load_tile(xt, xT[:, :, tsl], (), "ldx")
            norm_to_hT(0)
            ffn(0)
            store_tile(X1[:, :, tsl], xt, (dk("X1", t),), "stx")
            norm_to_hT(KC)
            dma("sp", ropeb.all3(T).ap, rope_d[:, :, tsl].rearrange("f p t -> p f t"), (), ropeb.all().keys, "ldrope")

            def ld_w(c):
                wload(wqk[c % 3].all(), winc_b[c] if preconv else None, winc_d[c], dk("winc", c), "ldqk%d" % (c % 3))

            for c in range(3):
                ld_w(c)
            PBK = (0, 1, 4)

            def stage_proj(c):
                s = wqk[c % 3]
                pbk = PBK[c % 3]
                for kc in range(KC):
                    mm(psv(pbk), s.sub(kc, 128), hT.sub(kc, T), kc == 0, kc == KC - 1)
                if c + 3 < 36:
                    ld_w(c + 3)
                if qk_spec(c)[0]:
                    act(sqb[c % 2].all(), psv(pbk), AF.Square)
                else:
                    act(vTb[c % 2].all(), psv(pbk), AF.Copy)

            def stage_ss(c):
                isqk, useA, gcol, kind, chunk = qk_spec(c)
                pbk = PBK[c % 3]
                if isqk:
                    r = rstd[c % 2].all()
                    mm(psv(2), ones_f.all(), sqb[c % 2].all(), True, True)
                    act(r, psv(2), AF.Sqrt, scale=1.0 / 128, bias=eps_c)
                    recip(r, r)
                    stt(qnb[c % 2].all(), psv(pbk), small.cols(gcol, gcol + 1), r, ALU.mult, ALU.mult)
                else:
                    res = resb[c % 3].all()
                    for s4 in range(4):
                        tr(psbf(5, s4 * 128, (s4 + 1) * 128), vTb[c % 2].cols(s4 * 128, (s4 + 1) * 128), ident)
                    cp(res, psbf(5, 0, 512))
                    dma("sp", VL[chunk][:, tsl], res.ap, res.keys,
                        (dk("VL", chunk, t),), "stres%d" % (c % 3))

            def stage_rot(c):
                isqk, useA, gcol, kind, chunk = qk_spec(c)
                if not isqk:
                    return
                res = resb[c % 3].all()
                qn = qnb[c % 2].all()
                mm(psv(3), permA if useA else permB, qn, True, True)
                cosv = ropeb.sub(0 if useA else 2, T)
                sinv = ropeb.sub(1 if useA else 3, T)
                t1 = t1b[c % 2].all()
                t2 = t2b[c % 2].all()
                tt(t1, qn, cosv, ALU.mult)
                tt(t2, psv(3), sinv, ALU.mult)
                tt(res, t1, t2, ALU.add)
                if kind == "Q":
                    dma("sp", QS[chunk][:, tsl], res.ap, res.keys, (dk("QS", chunk, t),), "stres%d" % (c % 3))
                else:
                    dma("sp", KL[chunk][:, tsl], res.ap, res.keys,
                        (dk("KL", chunk, t),), "stres%d" % (c % 3))

            for c in range(36 + 2):
                if c < 36:
                    stage_proj(c)
                if 1 <= c <= 36:
                    stage_ss(c - 1)
                if c >= 2:
                    stage_rot(c - 2)

        RUN_REST = dbg.get("stop") != "p1"
        def rest():
            groups4 = [[0, 1, 2, 3], [4, 5, 6, 7]]

            def gather(src_l, dst_l, nm, ch):
                S.op("pool", lambda e, a=src_l[ch], b=dst_l[ch]: e.collective_compute(
                    "AllGather", ALU.bypass, replica_groups=groups4, ins=[a.opt()], outs=[b.opt()]),
                    reads=tuple(dk(nm + "L", ch, t) for t in range(NT)), writes=(dk(nm + "G", ch),), dma="cc", inc=1)
            for ch in (0, 1):
                gather(KL, KGt, "K", ch)
                gather(VL, VGt, "V", ch)
            for h in range(4):
                for ch in (2 + 2 * h, 3 + 2 * h):
                    gather(KL, KGt, "K", ch)
                for ch in (2 + 2 * h, 3 + 2 * h):
                    gather(VL, VGt, "V", ch)

            if dbg.get("stop") == "cc":
                S.barrier("pool", [S.streams["cc"][-1]])
                return [lst[-1] for st, lst in S.streams.items() if st not in ("pe", "act", "dve", "pool", "cc")]
            def ld_kv(ri, src, chunk, nm):
                dma("sp", R[ri].ap.rearrange("p (r t) -> p r t", r=4), src[chunk].rearrange("(r p) t -> p r t", p=128),
                    (dk(nm, chunk),), R[ri].all().keys, "ldR%d" % ri)

            groups = []
            for g in range(2):
                for r in range(4):
                    for qg in range(NT):
                        groups.append(dict(kind="A", k=g, v=[2 + g], qchunk=g * 4 + r, qg=qg, first=(r == 0 and qg == 0),
                                           unit=("A", g)))
            for h in range(4):
                for qg in range(NT):
                    for c in range(2):
                        groups.append(dict(kind="B", k=c, v=[2, 3], qchunk=8 + 2 * h + c, qg=qg, h=h, c=c,
                                           first=(qg == 0 and c == 0), unit=("B", h)))
            if dbg.get("groups") == "small":
                groups = [gr for gr in groups if gr["qg"] < 2 and ((gr["kind"] == "A" and gr["qchunk"] == 0) or
                                                                   (gr["kind"] == "B" and gr["h"] == 0))]
            NG = len(groups)

            def unit_loads(unit):
                if unit[0] == "A":
                    g = unit[1]
                    ld_kv(g, KGt, g, "KG")
                    ld_kv(2 + g, VGt, g, "VG")
                else:
                    h = unit[1]
                    ld_kv(0, KGt, 2 + 2 * h, "KG")
                    ld_kv(1, KGt, 2 + 2 * h + 1, "KG")
                    ld_kv(2, VGt, 2 + 2 * h, "VG")
                    ld_kv(3, VGt, 2 + 2 * h + 1, "VG")

            def ld_q(gi):
                gr = groups[gi]
                qsl = slice(gr["qg"] * T, (gr["qg"] + 1) * T)
                dma("sp", Qt[gi % 4].ap, QS[gr["qchunk"]][:, qsl], (dk("QS", gr["qchunk"], gr["qg"]),),
                    Qt[gi % 4].all().keys, "ldq%d" % (gi % 4))

            ost_i = [0]

            def store_att(chunk, qg, v):
                dma("sp", AT[chunk][:, qg * T:(qg + 1) * T], v.ap, v.keys, (dk("AT", chunk, qg),),
                    "stat%d" % (ost_i[0] % 3))

            def next_ost():
                ost_i[0] += 1
                return ost[ost_i[0] % 3].all()

            PC0 = 192

            def obanks_of(gi):
                gr = groups[gi]
                if gr["kind"] == "A":
                    return [6]
                return [4, 5]

            def rcp_via_act(dst, src_ps, scale_in, bias, pw):
                act(dst, src_ps, AF.Ln, scale=scale_in, bias=bias)
                act(dst, dst, AF.Exp, scale=pw)

            def epilogue1(gi):
                gr = groups[gi]
                obanks = obanks_of(gi)
                if gr["kind"] == "B":
                    for half in range(2):
                        cp(tmpb.sub(half, T), psv(obanks[half]))
                else:
                    cp(tmpb.sub(gi % 2, T), psv(obanks[0]))

            def epilogue(gi):
                gr = groups[gi]
                a = gi % 2
                obanks = obanks_of(gi)
                tt(accD[a].cols(PC0, T), accD[a].cols(PC0, T), accP[a].cols(PC0, T), ALU.add)
                isA = gr["kind"] == "A"
                tb = 7 if isA else 6
                mm(psv(tb), ones_f.all(), accD[a].all(), not isA, True)
                sm = sums[a].all()
                rcp_via_act(sm, psv(tb), 1.0, None, -1.0)
                if gr["kind"] == "A":
                    o = next_ost()
                    tt(o, tmpb.sub(gi % 2, T), sm, ALU.mult)
                    store_att(gr["qchunk"], gr["qg"], o)
                elif gr["c"] == 0:
                    for half in range(2):
                        tt(o1.sub(half, T), tmpb.sub(half, T), sm, ALU.mult)
                else:
                    for half in range(2):
                        stt(tmpb.sub(half, T), tmpb.sub(half, T), neglam, sm, ALU.mult, ALU.mult)
                        tt(o1.sub(half, T), o1.sub(half, T), tmpb.sub(half, T), ALU.add)
                    for half in range(2):
                        act(sqd.sub(half, T), o1.sub(half, T), AF.Square)
                        mm(psv(7), ones_f.all(), sqd.sub(half, T), half == 0, half == 1)
                    rcp_via_act(rs3.all(), psv(7), 1.0 / 256, eps_c, -0.5)
                    for half in range(2):
                        o = next_ost()
                        stt(o, o1.sub(half, T), misc.cols(6 + half, 7 + half), rs3.all(), ALU.mult, ALU.mult)
                        store_att(8 + 2 * gr["h"] + half, gr["qg"], o)

            unit_loads(("A", 0))
            unit_loads(("A", 1))
            ld_q(0)
            ld_q(1)
            NKS = SEQ // 256
            steps = [(gi, s) for gi in range(NG) for s in range(NKS)]

            SUMP = 4

            def pe_sum_step(gi, s):
                return groups[gi]["kind"] == "A" and s % SUMP == SUMP - 1

            def emit_qk(i):
                gi, s = steps[i]
                gr = groups[gi]
                if s == 0 and gi + 2 < NG:
                    ld_q(gi + 2)
                if preconv and late_conv and i >= 64 and i % 24 == 0:
                    d_, s_, k_ = late_conv.pop(0)
                    conv(d_, s_, k_)
                p = i % 3 if gr["kind"] == "A" else i % 2
                for u in range(2):
                    kt = 2 * s + u
                    mm(psv(2 * p + u), R[gr["k"]].cols(kt * 128, (kt + 1) * 128), Qt[gi % 4].all(), True, True)
                Pv = Pb[i % NPB]
                S.op("act", lambda e, o=Pv.all3(T).ap, i_=ps[:, 2 * p:2 * p + 2, :]: e.activation(
                    out=o, in_=i_, func=AF.Exp, scale=SCALE),
                    reads=(PS_KEY + 2 * p, PS_KEY + 2 * p + 1), writes=Pv.all().keys)
                a = gi % 2
                if pe_sum_step(gi, s):
                    return
                if s == 0:
                    cp(accD[a].all(), Pv.sub(0, T), nosame=True)
                    cp(accP[a].cols(PC0, T), Pv.cols(T + PC0, 2 * T), eng="pool", nosame=True)
                else:
                    tt(accD[a].all(), accD[a].all(), Pv.sub(0, T), ALU.add, nosame=True)
                    tt(accP[a].cols(PC0, T), accP[a].cols(PC0, T), Pv.cols(T + PC0, 2 * T), ALU.add, eng="pool", nosame=True)
                tt(accD[a].cols(0, PC0), accD[a].cols(0, PC0), Pv.cols(T, T + PC0), ALU.add, nosame=True)

            def emit_pv(i):
                gi, s = steps[i]
                gr = groups[gi]
                obanks = obanks_of(gi)
                Pv = Pb[i % NPB]
                for u in range(2):
                    kt = 2 * s + u
                    for hf, ob in enumerate(obanks):
                        mm(psv(ob), R[gr["v"][hf]].cols(kt * 128, (kt + 1) * 128), Pv.sub(u, T),
                           kt == 0, kt == 2 * NKS - 1)
                    if pe_sum_step(gi, s):
                        mm(psv(7), ones_bf.all(), Pv.sub(u, T), s == SUMP - 1 and u == 0, False)
                if s == NKS - 1:
                    epilogue1(gi)
                    pending.append((i + 1 + (2 if gr["kind"] == "A" else EPI_DELAY), gi))

            NTL = len(steps)
            pending = []
            EPI_DELAY = 4
            next_pv = [0]

            def lag(j):
                return 2 if groups[steps[j][0]]["kind"] == "A" else 1

            def flush_pv(i, force=False):
                while next_pv[0] < NTL and (force or next_pv[0] + lag(next_pv[0]) <= i) and next_pv[0] < i + (1 if force else 0):
                    emit_pv(next_pv[0])
                    next_pv[0] += 1

            for i in range(NTL + 3):
                while pending and pending[0][0] <= i:
                    epilogue(pending.pop(0)[1])
                if i < NTL:
                    gi, s = steps[i]
                    gr = groups[gi]
                    if s == 0 and gr["first"] and gr["kind"] == "B":
                        while next_pv[0] < i:
                            emit_pv(next_pv[0])
                            next_pv[0] += 1
                        unit_loads(gr["unit"])
                    emit_qk(i)
                while next_pv[0] < NTL and next_pv[0] < i + 1 and next_pv[0] + lag(next_pv[0]) <= i:
                    emit_pv(next_pv[0])
                    next_pv[0] += 1
            assert next_pv[0] == NTL
            while pending:
                epilogue(pending.pop(0)[1])

            out_stores = []
            for t in range(0 if dbg.get("stop") == "p2" else NT):
                tsl = slice(t * T, (t + 1) * T)
                atkeys = tuple(dk("AT", ch, t) for ch in range(16))
                load_tile(hT, AT[:, :, tsl], atkeys, "ldat")
                load_tile(xt, X1[:, :, tsl], (dk("X1", t),), "ldx")

                def ld_wo(m):
                    wload(wqk[m % 3].all(), wout_b[m] if preconv else None, wout_d[m], dk("wout", m), "ldqk%d" % (m % 3))

                for m in range(3):
                    ld_wo(m)
                for m in range(KC):
                    s = wqk[m % 3]
                    for kc in range(KC):
                        mm(psv(m % 2), s.sub(kc, 128), hT.sub(kc, T), kc == 0, kc == KC - 1)
                    if m + 3 < KC:
                        ld_wo(m + 3)
                    tt(xt.sub(m, T), psv(m % 2), xt.sub(m, T), ALU.add)
                norm_to_hT(2 * KC)
                ffn(1)
                r = rstd[0].all()
                rmsnorm_stats(lambda kc: xt.sub(kc, T), KC, D, 6, r)
                for kc in range(KC):
                    stt(xt.sub(kc, T), xt.sub(kc, T), gains.cols(3 * KC + kc, 3 * KC + kc + 1), r, ALU.mult, ALU.mult)
                out_stores += store_tile(outT[:, :, tsl], xt, (dk("OUT", t),), "sto")

            if dbg.get("stop") == "p2":
                return [lst[-1] for st, lst in S.streams.items() if st not in ("pe", "act", "dve", "pool")]
            return out_stores

        if RUN_REST:
            out_stores = rest()
        else:
            out_stores = [lst[-1] for st, lst in S.streams.items() if st not in ("pe", "act", "dve", "pool")]
        S.barrier("sp", out_stores)
        S.finalize()

        @block.tensor
        def _(e):
            S.replay("pe", e)

        @block.scalar
        def _(e):
            S.replay("act", e)

        @block.vector
        def _(e):
            S.replay("dve", e)

        @block.gpsimd
        def _(e):
            S.replay("pool", e)

        @block.sync
        def _(e):
            S.replay("sp", e)

    return nc


def _rope_tables(qtr):
    s = (np.arange(TOK, dtype=np.float32) + np.float32(qtr * TOK))
    row = np.floor(s / 64).astype(np.float32)
    col = (s - row * 64).astype(np.float32)
    fA = (np.float32(10000.0) ** (-np.arange(0, 64, 2, dtype=np.float32) / np.float32(64))).astype(np.float32)
    fB = (np.float32(500000.0) ** (-np.arange(0, 32, 2, dtype=np.float32) / np.float32(32))).astype(np.float32)
    angr = (row[None, :] * fA[:, None]).astype(np.float32)
    angc = (col[None, :] * fA[:, None]).astype(np.float32)
    angl = (s[None, :] * fB[:, None]).astype(np.float32)
    cosA = np.concatenate([np.cos(angr), np.cos(angr), np.cos(angc), np.cos(angc)], 0)
    sinA = np.concatenate([-np.sin(angr), np.sin(angr), -np.sin(angc), np.sin(angc)], 0)
    cosB = np.ones((128, TOK), np.float32)
    sinB = np.zeros((128, TOK), np.float32)
    cosB[0:16] = np.cos(angl)
    cosB[16:32] = np.cos(angl)
    sinB[0:16] = -np.sin(angl)
    sinB[16:32] = np.sin(angl)
    return np.ascontiguousarray(np.stack([cosA, sinA, cosB, sinB], 0).astype(np.float32))


def _perm_consts():
    PA = np.zeros((128, 128), np.float32)
    PB = np.zeros((128, 128), np.float32)
    for m in range(128):
        blk = (m // 64) * 64
        r = m - blk
        PA[blk + (r + 32) % 64, m] = 1.0
    for m in range(32):
        PB[(m + 16) % 32, m] = 1.0
    return np.ascontiguousarray(np.concatenate([PA, PB, np.eye(128, dtype=np.float32)], 1))


_NC_CACHE = {}


def kernel(x, ffn1_norm, ffn1_w_gu, ffn1_w_down, mix_norm, w_in,
           a_q_norm, a_k_norm, b_q_norm, b_k_norm,
           b_lambda_q1, b_lambda_k1, b_lambda_q2, b_lambda_k2, b_subln,
           w_out, ffn2_norm, ffn2_w_gu, ffn2_w_down, out_norm):
    f32 = np.float32
    x = np.asarray(x, f32)

    def lay_gu(w):
        w = np.asarray(w, f32)[0].reshape(KC, 128, 2, JC, 128)
        return np.ascontiguousarray(w.transpose(3, 1, 2, 0, 4)).reshape(JC, 128, 2 * KC * 128)

    def lay_d(w):
        w = np.asarray(w, f32)[0].reshape(JC, 128, KC, 128)
        return np.ascontiguousarray(w.transpose(2, 1, 0, 3)).reshape(KC, 128, JC * 128)

    def lay_sq(w):
        w = np.asarray(w, f32).reshape(KC, 128, KC, 128)
        return np.ascontiguousarray(w.transpose(2, 1, 0, 3)).reshape(KC, 128, KC * 128)

    win = np.asarray(w_in, f32)[0]
    col0 = ([c * 128 for c in range(8)] + [1024 + g * 128 for g in range(2)]
            + [1536 + i * 128 for i in range(8)] + [2560 + i * 128 for i in range(8)]
            + [1280 + g * 128 for g in range(2)] + [3584 + i * 128 for i in range(8)])
    winc = np.stack([win[:, c0:c0 + 128].reshape(KC, 128, 128).transpose(1, 0, 2).reshape(128, KC * 128)
                     for c0 in col0], 0)
    winc = np.ascontiguousarray(winc)

    def gl(v):
        return np.asarray(v, f32).reshape(KC, 128).T

    gains = np.ascontiguousarray(np.concatenate([gl(ffn1_norm), gl(mix_norm), gl(ffn2_norm), gl(out_norm)], 1))
    small = np.zeros((128, 16), f32)
    for i, v in enumerate([a_q_norm, a_k_norm, b_q_norm, b_k_norm, b_lambda_q1, b_lambda_k1, b_lambda_q2, b_lambda_k2]):
        small[:, i] = np.asarray(v, f32).reshape(128)
    sl = np.asarray(b_subln, f32).reshape(2, 128)
    small[:, 8] = sl[0]
    small[:, 9] = sl[1]

    common = dict(gains=gains, small=small, wgu1=lay_gu(ffn1_w_gu), wgu2=lay_gu(ffn2_w_gu),
                  wd1=lay_d(ffn1_w_down), wd2=lay_d(ffn2_w_down), winc=winc,
                  wout=lay_sq(np.asarray(w_out, f32)[0]), perm=_perm_consts())
    ropes = [_rope_tables(q) for q in range(4)]
    in_maps = []
    for c in range(8):
        b, q = divmod(c, 4)
        xs = x[b, q * TOK:(q + 1) * TOK, :]
        m = dict(common)
        m["xT"] = np.ascontiguousarray(xs.T).reshape(KC, 128, TOK)
        m["rope"] = ropes[q]
        in_maps.append(m)

    if "nc" not in _NC_CACHE:
        _NC_CACHE["nc"] = build_nc()
    nc = _NC_CACHE["nc"]
    res = run_bass_kernel_spmd(nc, in_maps, core_ids=list(range(8)))
    out = np.empty((2, SEQ, D), f32)
    for c in range(8):
        b, q = divmod(c, 4)
        o = np.asarray(res.results[c]["outT"], f32).reshape(D, TOK)
        out[b, q * TOK:(q + 1) * TOK, :] = o.T
    return out
```

```python
import math
from contextlib import ExitStack

import numpy as np
import concourse.bass as bass
import concourse.mybir as mybir
from concourse.bass_utils import run_bass_kernel_spmd

F32 = mybir.dt.float32
BF16 = mybir.dt.bfloat16
AF = mybir.ActivationFunctionType
ALU = mybir.AluOpType

D = 2048
KC = 16
DFF = 5632
JC = 44
T = 512
TOK = 4096
NT = TOK // T
SEQ = 16384
EPS = 1e-6
SCALE = 128 ** -0.5
LAMBDA_INIT = 0.8 - 0.6 * math.exp(0.0)
G = 256
ARENA_BYTES = 180 * 1024
PS_KEY = 1 << 20
DR_KEY = 1 << 21
SEM_LIMIT = 30000


class View:
    __slots__ = ("ap", "keys")

    def __init__(self, ap, keys):
        self.ap = ap
        self.keys = keys


class Op:
    __slots__ = ("eng", "emit", "deps", "signal", "sem", "val", "epoch", "stream", "inc", "idx")


class Sched:
    def __init__(self, sem_pool):
        self.sem_pool = list(sem_pool)
        self.q = {e: [] for e in ("pe", "act", "dve", "pool", "sp")}
        self.streams = {}
        self.lastw = {}
        self.readers = {}
        self.nops = 0

    def op(self, eng, emit, reads=(), writes=(), dma=None, inc=None, nosame=False):
        o = Op()
        o.eng = eng
        o.emit = emit
        o.signal = False
        o.sem = None
        o.val = 0
        o.epoch = 0
        o.stream = dma if dma is not None else eng
        o.inc = inc if inc is not None else (16 if dma is not None else 1)
        deps = {}
        st = o.stream
        is_dma = dma is not None
        lastw = self.lastw
        readers = self.readers
        for k in reads:
            w = lastw.get(k)
            if w is not None and not ((st == "pe" or nosame) and w.stream == st):
                deps[id(w)] = w
        for k in writes:
            w = lastw.get(k)
            if w is not None and (is_dma or w.stream != st):
                deps[id(w)] = w
            rs = readers.get(k)
            if rs:
                for rst, r in rs.items():
                    if is_dma or rst != st:
                        deps[id(r)] = r
        lst = self.streams.setdefault(st, [])
        if is_dma and lst and st != "cc":
            p = lst[-1]
            deps[id(p)] = p
        deps.pop(id(o), None)
        best = {}
        for d in deps.values():
            cur = best.get(d.stream)
            if cur is None or cur.idx < d.idx:
                best[d.stream] = d
        for d in best.values():
            d.signal = True
        o.deps = list(best.values())
        self.nops += 1
        o.idx = self.nops
        for k in writes:
            lastw[k] = o
            readers[k] = {}
        for k in reads:
            rs = readers.get(k)
            if rs is None:
                rs = readers[k] = {}
            rs[st] = o
        lst.append(o)
        self.q[eng].append(o)
        return o

    def barrier(self, eng, ops):
        o = Op()
        o.eng = eng
        o.emit = None
        o.signal = False
        o.sem = None
        o.val = 0
        o.epoch = 0
        o.stream = eng
        o.inc = 1
        o.idx = 0
        for d in ops:
            d.signal = True
        o.deps = list(ops)
        self.q[eng].append(o)
        return o

    def finalize(self):
        for st, lst in self.streams.items():
            cnt = 0
            epoch = 0
            sem = None
            for o in lst:
                if not o.signal:
                    continue
                if sem is None or cnt + o.inc > SEM_LIMIT:
                    sem = self.sem_pool.pop()
                    cnt = 0
                    epoch += 1
                cnt += o.inc
                o.sem = sem
                o.val = cnt
                o.epoch = epoch

    def replay(self, eng, e):
        waited = {}
        for o in self.q[eng]:
            best = {}
            for d in o.deps:
                cur = best.get(d.stream)
                if cur is None or (cur.epoch, cur.val) < (d.epoch, d.val):
                    best[d.stream] = d
            for d in best.values():
                cur = waited.get(d.stream)
                key = (d.epoch, d.val)
                if cur is not None and cur >= key:
                    continue
                e.wait_ge(d.sem, d.val)
                waited[d.stream] = key
            if o.emit is not None:
                inst = o.emit(e)
                if o.signal:
                    inst.then_inc(o.sem, o.inc)


class Buf:
    def __init__(self, arena, off, n, dt):
        self.esz = 4 if dt is F32 else 2
        self.off = off
        self.n = n
        self.dt = dt
        nb = n * self.esz
        assert off % 4 == 0 and nb % 4 == 0
        ap = arena[:, off // 4:(off + nb) // 4]
        if dt is not F32:
            ap = ap.bitcast(dt)
        self.ap = ap

    def _keys(self, a, b):
        lo = self.off + a * self.esz
        hi = self.off + b * self.esz
        return tuple(range(lo // G, (hi - 1) // G + 1))

    def cols(self, a, b):
        return View(self.ap[:, a:b], self._keys(a, b))

    def sub(self, i, m):
        return self.cols(i * m, (i + 1) * m)

    def all(self):
        return View(self.ap, self._keys(0, self.n))

    def all3(self, m):
        return View(self.ap.rearrange("p (n m) -> p n m", m=m), self._keys(0, self.n))

    def range3(self, i0, i1, m):
        return View(self.ap[:, i0 * m:i1 * m].rearrange("p (n m) -> p n m", m=m), self._keys(i0 * m, i1 * m))


def build_nc(preconv=True, dbg=None):
    dbg = dbg or {}
    nc = bass.Bass("TRN2", target_bir_lowering=False)

    def din(name, shape, dt=F32):
        return nc.dram_tensor(name, shape, dt, kind="ExternalInput").ap()

    xT = din("xT", [KC, 128, TOK])
    gains_d = din("gains", [128, 4 * KC])
    small_d = din("small", [128, 16])
    wgu_d = [din("wgu1", [JC, 128, 2 * KC * 128]), din("wgu2", [JC, 128, 2 * KC * 128])]
    wd_d = [din("wd1", [KC, 128, JC * 128]), din("wd2", [KC, 128, JC * 128])]
    winc_d = din("winc", [36, 128, KC * 128])
    wout_d = din("wout", [KC, 128, KC * 128])
    rope_d = din("rope", [4, 128, TOK])
    perm_d = din("perm", [128, 3 * 128])
    outT = nc.dram_tensor("outT", [KC, 128, TOK], F32, kind="ExternalOutput").ap()

    skind = dict(kind="ExternalOutput") if dbg else {}
    X1 = nc.dram_tensor("X1", [KC, 128, TOK], F32, **skind).ap()
    QS = nc.dram_tensor("QS", [16, 128, TOK], BF16, **skind).ap()
    KL = [nc.dram_tensor("KL%d" % i, [128, TOK], BF16).ap() for i in range(10)]
    VL = [nc.dram_tensor("VL%d" % i, [128, TOK], BF16).ap() for i in range(10)]
    KGt = [nc.dram_tensor("KG%d" % i, [512, TOK], BF16).ap() for i in range(10)]
    VGt = [nc.dram_tensor("VG%d" % i, [512, TOK], BF16).ap() for i in range(10)]
    AT = nc.dram_tensor("AT", [16, 128, TOK], BF16, **skind).ap()
    if preconv:
        wgu_b = [nc.dram_tensor("wgu1b", [JC, 128, 2 * KC * 128], BF16).ap(),
                 nc.dram_tensor("wgu2b", [JC, 128, 2 * KC * 128], BF16).ap()]
        wd_b = [nc.dram_tensor("wd1b", [KC, 128, JC * 128], BF16).ap(),
                nc.dram_tensor("wd2b", [KC, 128, JC * 128], BF16).ap()]
        winc_b = nc.dram_tensor("wincb", [36, 128, KC * 128], BF16).ap()
        wout_b = nc.dram_tensor("woutb", [KC, 128, KC * 128], BF16).ap()

    dkeys = {}

    def dk(*name):
        k = dkeys.get(name)
        if k is None:
            k = dkeys[name] = DR_KEY + len(dkeys)
        return k

    with ExitStack() as es:
        arena = es.enter_context(nc.sbuf_tensor("arena", [128, ARENA_BYTES // 4], F32))
        ps = es.enter_context(nc.psum_tensor("ps", [128, 8, 512], F32))
        sems = [es.enter_context(nc.semaphore("s%d" % i)) for i in range(90)]
        block = es.enter_context(nc.Block())
        S = Sched(sems)
        K = 1024

        def psv(b):
            return View(ps[:, b, :], (PS_KEY + b,))

        def psbf(b, a, c):
            return View(ps[:, b, :].bitcast(BF16)[:, a:c], (PS_KEY + b,))

        def psc(b, a, c):
            return View(ps[:, b, a:c], (PS_KEY + b,))

        cbase = 176 * K
        gains = Buf(arena, cbase, 64, F32)
        small = Buf(arena, cbase + 256, 16, F32)
        misc = Buf(arena, cbase + 512, 16, F32)
        ones_f = Buf(arena, cbase + 768, 128, F32)
        perm_bf = Buf(arena, cbase + 768 + 512, 3 * 128, BF16)
        ones_bf = Buf(arena, cbase + 768 + 512 + 768, 128, BF16)

        xt = Buf(arena, 0, KC * T, F32)
        hT = Buf(arena, 32 * K, KC * T, BF16)
        actT = Buf(arena, 48 * K, JC * T, BF16)
        wgu = [Buf(arena, 92 * K + i * 8 * K, 2 * KC * 128, BF16) for i in range(3)]
        wdh = [Buf(arena, 116 * K + i * 5632, 22 * 128, BF16) for i in range(3)]
        mb = 133 * K
        sqb = [Buf(arena, mb + i * 2 * K, T, F32) for i in range(2)]
        rstd = [Buf(arena, mb + 4 * K + i * 2 * K, T, F32) for i in range(2)]
        sg = [Buf(arena, mb + 8 * K + i * 2 * K, T, F32) for i in range(2)]
        pb0 = 48 * K
        wqk = [Buf(arena, pb0 + i * 4 * K, KC * 128, BF16) for i in range(3)]
        ropeb = Buf(arena, pb0 + 12 * K, 4 * T, F32)
        t1b = [Buf(arena, pb0 + 20 * K + i * 2 * K, T, F32) for i in range(2)]
        t2b = [Buf(arena, pb0 + 24 * K + i * 2 * K, T, F32) for i in range(2)]
        qnb = [Buf(arena, pb0 + 28 * K + i * K, T, BF16) for i in range(2)]
        resb = [Buf(arena, pb0 + 30 * K + i * K, T, BF16) for i in range(3)]
        vTb = [Buf(arena, pb0 + 33 * K + i * K, T, BF16) for i in range(2)]

        R = [Buf(arena, i * 32 * K, SEQ, BF16) for i in range(4)]
        p2 = 128 * K
        Qt = [Buf(arena, p2 + i * K, T, BF16) for i in range(4)]
        NPB = 6
        Pb = [Buf(arena, p2 + 4 * K + i * 2 * K, 2 * T, BF16) for i in range(NPB)]
        q2 = p2 + 16 * K
        o1 = Buf(arena, q2, 2 * T, F32)
        tmpb = Buf(arena, q2 + 4 * K, 2 * T, F32)
        sqd = Buf(arena, q2 + 8 * K, 2 * T, F32)
        sums = [Buf(arena, q2 + 12 * K + i * 2 * K, T, F32) for i in range(2)]
        rs3 = Buf(arena, q2 + 16 * K, T, F32)
        ost = [Buf(arena, q2 + 18 * K + i * K, T, BF16) for i in range(3)]
        accD = [Buf(arena, q2 + 21 * K + i * 2 * K, T, F32) for i in range(2)]
        accP = [Buf(arena, q2 + 25 * K + i * 2 * K, T, F32) for i in range(2)]
        assert q2 + 29 * K <= 176 * K

        def mm(out, l, r, start, stop):
            return S.op("pe", lambda e, o=out.ap, a=l.ap, b=r.ap: e.matmul(o, a, b, start=start, stop=stop),
                        reads=l.keys + r.keys, writes=out.keys)

        def tr(out, in_, ident):
            return S.op("pe", lambda e, o=out.ap, a=in_.ap, b=ident.ap: e.transpose(o, a, b),
                        reads=in_.keys + ident.keys, writes=out.keys)

        def act(out, in_, func, scale=None, bias=None):
            kw = {}
            rk = in_.keys
            if scale is not None:
                kw["scale"] = scale
            if bias is not None:
                kw["bias"] = bias.ap
                rk = rk + bias.keys
            return S.op("act", lambda e, o=out.ap, i=in_.ap: e.activation(out=o, in_=i, func=func, **kw),
                        reads=rk, writes=out.keys)

        def stt(out, in0, scalar, in1, op0, op1):
            rk = in0.keys + in1.keys
            sc = scalar
            if isinstance(scalar, View):
                rk = rk + scalar.keys
                sc = scalar.ap
            return S.op("dve", lambda e, o=out.ap, a=in0.ap, b=in1.ap: e.scalar_tensor_tensor(
                out=o, in0=a, scalar=sc, in1=b, op0=op0, op1=op1), reads=rk, writes=out.keys)

        def tt(out, in0, in1, op, eng="dve", nosame=False):
            return S.op(eng, lambda e, o=out.ap, a=in0.ap, b=in1.ap: e.tensor_tensor(out=o, in0=a, in1=b, op=op),
                        reads=in0.keys + in1.keys, writes=out.keys, nosame=nosame)

        def tsc(out, in0, s1, op0):
            return S.op("dve", lambda e, o=out.ap, a=in0.ap: e.tensor_scalar(
                out=o, in0=a, scalar1=s1, scalar2=None, op0=op0), reads=in0.keys, writes=out.keys)

        def cp(out, in_, eng="dve", nosame=False):
            if eng == "act":
                return act(out, in_, AF.Copy)
            return S.op(eng, lambda e, o=out.ap, i=in_.ap: e.tensor_copy(out=o, in_=i),
                        reads=in_.keys, writes=out.keys, nosame=nosame)

        def recip(out, in_):
            return S.op("dve", lambda e, o=out.ap, i=in_.ap: e.reciprocal(out=o, in_=i),
                        reads=in_.keys, writes=out.keys)

        def memset(v, val, eng="dve"):
            return S.op(eng, lambda e, a=v.ap: e.memset(a, val), writes=v.keys)

        def dma(eng, out_ap, in_ap, reads, writes, stream):
            return S.op(eng, lambda e, o=out_ap, i=in_ap: e.dma_start(out=o, in_=i),
                        reads=reads, writes=writes, dma=stream)

        cv_i = [0]

        def conv(dst_ap, src_ap, key, after=()):
            st = "cv%d" % (cv_i[0] % 6)
            cv_i[0] += 1
            dma("pool", dst_ap, src_ap, after, (key,), st)

        def wload(dst, src_b, src_d, key, stream):
            if preconv:
                dma("sp", dst.ap, src_b, (key,), dst.keys, stream)
            else:
                dma("pool", dst.ap, src_d, (), dst.keys, stream)

        dma("sp", gains.ap, gains_d, (), gains.all().keys, "ldc0")
        dma("sp", small.ap, small_d, (), small.all().keys, "ldc1")
        dma("pool", perm_bf.ap, perm_d, (), perm_bf.all().keys, "ldc2")
        memset(ones_f.all(), 1.0)
        memset(ones_bf.all(), 1.0)
        memset(misc.all(), 0.0)
        memset(misc.cols(0, 1), EPS)
        eps_c = misc.cols(0, 1)
        permA = perm_bf.cols(0, 128)
        permB = perm_bf.cols(128, 256)
        ident = perm_bf.cols(256, 384)
        tt(misc.cols(1, 2), small.cols(4, 5), small.cols(5, 6), ALU.mult)
        tt(misc.cols(2, 3), small.cols(6, 7), small.cols(7, 8), ALU.mult)
        mm(psc(7, 0, 2), ones_f.all(), misc.cols(1, 3), True, True)
        act(misc.cols(3, 5), psc(7, 0, 2), AF.Exp)
        tt(misc.cols(5, 6), misc.cols(4, 5), misc.cols(3, 4), ALU.subtract)
        tsc(misc.cols(5, 6), misc.cols(5, 6), -LAMBDA_INIT, ALU.add)
        tsc(misc.cols(6, 8), small.cols(8, 10), 1.0 - LAMBDA_INIT, ALU.mult)
        neglam = misc.cols(5, 6)

        if preconv:
            def conv_ffn(f):
                for j in range(JC):
                    conv(wgu_b[f][j], wgu_d[f][j], dk("wgu", f, j))
                for m in range(KC):
                    conv(wd_b[f][m], wd_d[f][m], dk("wd", f, m))
            conv_ffn(0)
            for c in range(36):
                conv(winc_b[c], winc_d[c], dk("winc", c))
            late_conv = [(wout_b[m], wout_d[m], dk("wout", m)) for m in range(KC)]
            late_conv += [(wgu_b[1][j], wgu_d[1][j], dk("wgu", 1, j)) for j in range(JC)]
            late_conv += [(wd_b[1][m], wd_d[1][m], dk("wd", 1, m)) for m in range(KC)]


        def rmsnorm_stats(src, nchunks, dim, psb, rbuf):
            for kc in range(nchunks):
                sq = sqb[kc % 2].all()
                act(sq, src(kc), AF.Square)
                mm(psv(psb), ones_f.all(), sq, kc == 0, kc == nchunks - 1)
            act(rbuf, psv(psb), AF.Sqrt, scale=1.0 / dim, bias=eps_c)
            recip(rbuf, rbuf)

        def norm_to_hT(gbase):
            r = rstd[0].all()
            rmsnorm_stats(lambda kc: xt.sub(kc, T), KC, D, 6, r)
            for kc in range(KC):
                stt(hT.sub(kc, T), xt.sub(kc, T), gains.cols(gbase + kc, gbase + kc + 1), r, ALU.mult, ALU.mult)

        def ffn(f):
            def ld_gu(j):
                s = wgu[j % 3]
                wload(s.all(), wgu_b[f][j] if preconv else None, wgu_d[f][j], dk("wgu", f, j), "ldgu%d" % (j % 3))

            def ld_d(i):
                m, half = divmod(i, 2)
                s = wdh[i % 3]
                sb = wd_b[f][m][:, half * 2816:(half + 1) * 2816] if preconv else None
                sd = wd_d[f][m][:, half * 2816:(half + 1) * 2816]
                wload(s.all(), sb, sd, dk("wd", f, m), "ldd%d" % (i % 3))

            for j in range(3):
                ld_gu(j)
            for i in range(3):
                ld_d(i)
            for j in range(JC):
                s = wgu[j % 3]
                pg, pu = (0, 1) if j % 2 == 0 else (2, 3)
                for kc in range(KC):
                    mm(psv(pg), s.sub(kc, 128), hT.sub(kc, T), kc == 0, kc == KC - 1)
                for kc in range(KC):
                    mm(psv(pu), s.sub(KC + kc, 128), hT.sub(kc, T), kc == 0, kc == KC - 1)
                if j + 3 < JC:
                    ld_gu(j + 3)
                sgv = sg[j % 2].all()
                act(sgv, psv(pg), AF.Silu)
                tt(actT.sub(j, T), sgv, psv(pu), ALU.mult)
            for m in range(KC):
                pbk = 4 + m % 2
                for half in range(2):
                    i = 2 * m + half
                    s = wdh[i % 3]
                    for jj in range(22):
                        jc = half * 22 + jj
                        mm(psv(pbk), s.sub(jj, 128), actT.sub(jc, T), jc == 0, jc == JC - 1)
                    if i + 3 < 2 * KC:
                        ld_d(i + 3)
                stt(xt.sub(m, T), psv(pbk), 0.5, xt.sub(m, T), ALU.mult, ALU.add)

        def load_tile(dst, src, keys, stream):
            for hh in range(2):
                dma("sp", dst.range3(hh * 8, hh * 8 + 8, T).ap,
                    src[hh * 8:hh * 8 + 8].rearrange("k p t -> p k t"),
                    keys, dst.range3(hh * 8, hh * 8 + 8, T).keys, "%s%d" % (stream, hh))

        def store_tile(dst, src, keys, stream):
            ops = []
            for hh in range(2):
                ops.append(dma("sp", dst[hh * 8:hh * 8 + 8].rearrange("k p t -> p k t"),
                               src.range3(hh * 8, hh * 8 + 8, T).ap,
                               src.range3(hh * 8, hh * 8 + 8, T).keys, keys, "%s%d" % (stream, hh)))
            return ops

        def qk_spec(c):
            if c < 8:
                return True, True, 0, "Q", c
            if c < 10:
                return True, True, 1, "K", c - 8
            if c < 18:
                return True, False, 2, "Q", 8 + (c - 10)
            if c < 26:
                return True, False, 3, "K", 2 + (c - 18)
            return False, False, 0, "V", c - 26

        for t in range(dbg.get("nt1", NT)):
            tsl = slice(t * T, (t + 1) * T)
            load_tile(xt, xT[:, :, tsl], (), "ldx")
            norm_to_hT(0)
            ffn(0)
            store_tile(X1[:, :, tsl], xt, (dk("X1", t),), "stx")
            norm_to_hT(KC)
            dma("sp", ropeb.all3(T).ap, rope_d[:, :, tsl].rearrange("f p t -> p f t"), (), ropeb.all().keys, "ldrope")

            def ld_w(c):
                wload(wqk[c % 3].all(), winc_b[c] if preconv else None, winc_d[c], dk("winc", c), "ldqk%d" % (c % 3))

            for c in range(3):
                ld_w(c)
            PBK = (0, 1, 4)

            def stage_proj(c):
                s = wqk[c % 3]
                pbk = PBK[c % 3]
                for kc in range(KC):
                    mm(psv(pbk), s.sub(kc, 128), hT.sub(kc, T), kc == 0, kc == KC - 1)
                if c + 3 < 36:
                    ld_w(c + 3)
                if qk_spec(c)[0]:
                    act(sqb[c % 2].all(), psv(pbk), AF.Square)
                else:
                    act(vTb[c % 2].all(), psv(pbk), AF.Copy)

            def stage_ss(c):
                isqk, useA, gcol, kind, chunk = qk_spec(c)
                pbk = PBK[c % 3]
                if isqk:
                    r = rstd[c % 2].all()
                    mm(psv(2), ones_f.all(), sqb[c % 2].all(), True, True)
                    act(r, psv(2), AF.Sqrt, scale=1.0 / 128, bias=eps_c)
                    recip(r, r)
                    stt(qnb[c % 2].all(), psv(pbk), small.cols(gcol, gcol + 1), r, ALU.mult, ALU.mult)
                else:
                    res = resb[c % 3].all()
                    for s4 in range(4):
                        tr(psbf(5, s4 * 128, (s4 + 1) * 128), vTb[c % 2].cols(s4 * 128, (s4 + 1) * 128), ident)
                    cp(res, psbf(5, 0, 512))
                    dma("sp", VL[chunk][:, tsl], res.ap, res.keys,
                        (dk("VL", chunk, t),), "stres%d" % (c % 3))

            def stage_rot(c):
                isqk, useA, gcol, kind, chunk = qk_spec(c)
                if not isqk:
                    return
                res = resb[c % 3].all()
                qn = qnb[c % 2].all()
                mm(psv(3), permA if useA else permB, qn, True, True)
                cosv = ropeb.sub(0 if useA else 2, T)
                sinv = ropeb.sub(1 if useA else 3, T)
                t1 = t1b[c % 2].all()
                t2 = t2b[c % 2].all()
                tt(t1, qn, cosv, ALU.mult)
                tt(t2, psv(3), sinv, ALU.mult)
                tt(res, t1, t2, ALU.add)
                if kind == "Q":
                    dma("sp", QS[chunk][:, tsl], res.ap, res.keys, (dk("QS", chunk, t),), "stres%d" % (c % 3))
                else:
                    dma("sp", KL[chunk][:, tsl], res.ap, res.keys,
                        (dk("KL", chunk, t),), "stres%d" % (c % 3))

            for c in range(36 + 2):
                if c < 36:
                    stage_proj(c)
                if 1 <= c <= 36:
                    stage_ss(c - 1)
                if c >= 2:
                    stage_rot(c - 2)

        RUN_REST = dbg.get("stop") != "p1"
        def rest():
            groups4 = [[0, 1, 2, 3], [4, 5, 6, 7]]

            def gather(src_l, dst_l, nm, ch):
                S.op("pool", lambda e, a=src_l[ch], b=dst_l[ch]: e.collective_compute(
                    "AllGather", ALU.bypass, replica_groups=groups4, ins=[a.opt()], outs=[b.opt()]),
                    reads=tuple(dk(nm + "L", ch, t) for t in range(NT)), writes=(dk(nm + "G", ch),), dma="cc", inc=1)
            for ch in (0, 1):
                gather(KL, KGt, "K", ch)
                gather(VL, VGt, "V", ch)
            for h in range(4):
                for ch in (2 + 2 * h, 3 + 2 * h):
                    gather(KL, KGt, "K", ch)
                for ch in (2 + 2 * h, 3 + 2 * h):
                    gather(VL, VGt, "V", ch)

            if dbg.get("stop") == "cc":
                S.barrier("pool", [S.streams["cc"][-1]])
                return [lst[-1] for st, lst in S.streams.items() if st not in ("pe", "act", "dve", "pool", "cc")]
            def ld_kv(ri, src, chunk, nm, quarters=(0, 1, 2, 3)):
                for r in quarters:
                    dst = R[ri].cols(r * TOK, (r + 1) * TOK)
                    dma("sp", dst.ap, src[chunk][r * 128:(r + 1) * 128, :], (dk(nm, chunk),), dst.keys,
                        "ldR%d_%d" % (ri, r))

            groups = []
            for g in range(2):
                for r in range(4):
                    for qg in range(NT):
                        groups.append(dict(kind="A", k=g, v=[2 + g], qchunk=g * 4 + r, qg=qg, first=(r == 0 and qg == 0),
                                           unit=("A", g)))
            for h in range(4):
                for qg in range(NT):
                    for c in range(2):
                        groups.append(dict(kind="B", k=c, v=[2, 3], qchunk=8 + 2 * h + c, qg=qg, h=h, c=c,
                                           first=(qg == 0 and c == 0), unit=("B", h)))
            if dbg.get("groups") == "small":
                groups = [gr for gr in groups if gr["qg"] < 2 and ((gr["kind"] == "A" and gr["qchunk"] == 0) or
                                                                   (gr["kind"] == "B" and gr["h"] == 0))]
            NG = len(groups)

            reload_at = {}
            if NG == 128:
                reload_at[31] = [(0, KGt, 2, "KG"), (2, VGt, 2, "VG")]
                reload_at[63] = [(1, KGt, 3, "KG"), (3, VGt, 3, "VG")]
                for h in range(3):
                    nk = 2 + 2 * (h + 1)
                    reload_at[64 + 16 * h + 14] = [(0, KGt, nk, "KG")]
                    reload_at[64 + 16 * h + 15] = [(1, KGt, nk + 1, "KG"), (2, VGt, nk, "VG"), (3, VGt, nk + 1, "VG")]

            def unit_loads(unit):
                if unit[0] == "A":
                    g = unit[1]
                    ld_kv(g, KGt, g, "KG")
                    ld_kv(2 + g, VGt, g, "VG")
                else:
                    h = unit[1]
                    ld_kv(0, KGt, 2 + 2 * h, "KG")
                    ld_kv(1, KGt, 2 + 2 * h + 1, "KG")
                    ld_kv(2, VGt, 2 + 2 * h, "VG")
                    ld_kv(3, VGt, 2 + 2 * h + 1, "VG")

            def ld_q(gi):
                gr = groups[gi]
                qsl = slice(gr["qg"] * T, (gr["qg"] + 1) * T)
                dma("sp", Qt[gi % 4].ap, QS[gr["qchunk"]][:, qsl], (dk("QS", gr["qchunk"], gr["qg"]),),
                    Qt[gi % 4].all().keys, "ldq%d" % (gi % 4))

            ost_i = [0]

            def store_att(chunk, qg, v):
                dma("sp", AT[chunk][:, qg * T:(qg + 1) * T], v.ap, v.keys, (dk("AT", chunk, qg),),
                    "stat%d" % (ost_i[0] % 3))

            def next_ost():
                ost_i[0] += 1
                return ost[ost_i[0] % 3].all()

            PC0 = 192

            def obanks_of(gi):
                gr = groups[gi]
                if gr["kind"] == "A":
                    return [6]
                return [4, 5]

            def rcp_via_act(dst, src_ps, scale_in, bias, pw):
                act(dst, src_ps, AF.Ln, scale=scale_in, bias=bias)
                act(dst, dst, AF.Exp, scale=pw)

            def epilogue1(gi):
                gr = groups[gi]
                obanks = obanks_of(gi)
                if gr["kind"] == "B":
                    for half in range(2):
                        cp(tmpb.sub(half, T), psv(obanks[half]))
                else:
                    cp(tmpb.sub(gi % 2, T), psv(obanks[0]))

            def epilogue(gi):
                gr = groups[gi]
                a = gi % 2
                obanks = obanks_of(gi)
                tt(accD[a].cols(PC0, T), accD[a].cols(PC0, T), accP[a].cols(PC0, T), ALU.add)
                isA = gr["kind"] == "A"
                tb = 7 if isA else 6
                mm(psv(tb), ones_f.all(), accD[a].all(), not isA, True)
                sm = sums[a].all()
                rcp_via_act(sm, psv(tb), 1.0, None, -1.0)
                if gr["kind"] == "A":
                    o = next_ost()
                    tt(o, tmpb.sub(gi % 2, T), sm, ALU.mult)
                    store_att(gr["qchunk"], gr["qg"], o)
                elif gr["c"] == 0:
                    for half in range(2):
                        tt(o1.sub(half, T), tmpb.sub(half, T), sm, ALU.mult)
                else:
                    for half in range(2):
                        stt(tmpb.sub(half, T), tmpb.sub(half, T), neglam, sm, ALU.mult, ALU.mult)
                        tt(o1.sub(half, T), o1.sub(half, T), tmpb.sub(half, T), ALU.add)
                    for half in range(2):
                        act(sqd.sub(half, T), o1.sub(half, T), AF.Square)
                        mm(psv(7), ones_f.all(), sqd.sub(half, T), half == 0, half == 1)
                    rcp_via_act(rs3.all(), psv(7), 1.0 / 256, eps_c, -0.5)
                    for half in range(2):
                        o = next_ost()
                        stt(o, o1.sub(half, T), misc.cols(6 + half, 7 + half), rs3.all(), ALU.mult, ALU.mult)
                        store_att(8 + 2 * gr["h"] + half, gr["qg"], o)

            unit_loads(("A", 0))
            unit_loads(("A", 1))
            ld_q(0)
            ld_q(1)
            NKS = SEQ // 256
            steps = [(gi, s) for gi in range(NG) for s in range(NKS)]

            SUMP = 4

            def pe_sum_step(gi, s):
                return groups[gi]["kind"] == "A" and s % SUMP == SUMP - 1

            def emit_qk(i):
                gi, s = steps[i]
                gr = groups[gi]
                if s == 0 and gi + 2 < NG:
                    ld_q(gi + 2)
                if preconv and late_conv and i >= 64 and i % 24 == 0:
                    d_, s_, k_ = late_conv.pop(0)
                    conv(d_, s_, k_)
                p = i % 3 if gr["kind"] == "A" else i % 2
                for u in range(2):
                    kt = 2 * s + u
                    mm(psv(2 * p + u), R[gr["k"]].cols(kt * 128, (kt + 1) * 128), Qt[gi % 4].all(), True, True)
                Pv = Pb[i % NPB]
                S.op("act", lambda e, o=Pv.all3(T).ap, i_=ps[:, 2 * p:2 * p + 2, :]: e.activation(
                    out=o, in_=i_, func=AF.Exp, scale=SCALE),
                    reads=(PS_KEY + 2 * p, PS_KEY + 2 * p + 1), writes=Pv.all().keys)
                a = gi % 2
                if pe_sum_step(gi, s):
                    return
                if s == 0:
                    cp(accD[a].all(), Pv.sub(0, T), nosame=True)
                    cp(accP[a].cols(PC0, T), Pv.cols(T + PC0, 2 * T), eng="pool", nosame=True)
                else:
                    tt(accD[a].all(), accD[a].all(), Pv.sub(0, T), ALU.add, nosame=True)
                    tt(accP[a].cols(PC0, T), accP[a].cols(PC0, T), Pv.cols(T + PC0, 2 * T), ALU.add, eng="pool", nosame=True)
                tt(accD[a].cols(0, PC0), accD[a].cols(0, PC0), Pv.cols(T, T + PC0), ALU.add, nosame=True)

            def emit_pv(i):
                gi, s = steps[i]
                gr = groups[gi]
                obanks = obanks_of(gi)
                Pv = Pb[i % NPB]
                for u in range(2):
                    kt = 2 * s + u
                    for hf, ob in enumerate(obanks):
                        mm(psv(ob), R[gr["v"][hf]].cols(kt * 128, (kt + 1) * 128), Pv.sub(u, T),
                           kt == 0, kt == 2 * NKS - 1)
                    if pe_sum_step(gi, s):
                        mm(psv(7), ones_bf.all(), Pv.sub(u, T), s == SUMP - 1 and u == 0, False)
                if gi in reload_at and (s + 1) % 16 == 0:
                    for (ri, src_l, chunk, nm) in reload_at[gi]:
                        ld_kv(ri, src_l, chunk, nm, quarters=((s + 1) // 16 - 1,))
                if s == NKS - 1:
                    epilogue1(gi)
                    pending.append((i + 1 + (2 if gr["kind"] == "A" else EPI_DELAY), gi))

            NTL = len(steps)
            pending = []
            EPI_DELAY = 4
            next_pv = [0]

            def lag(j):
                return 2 if groups[steps[j][0]]["kind"] == "A" else 1

            for i in range(NTL + 3):
                while pending and pending[0][0] <= i:
                    epilogue(pending.pop(0)[1])
                if i < NTL:
                    gi, s = steps[i]
                    gr = groups[gi]
                    if s == 0 and gr["first"] and gr["kind"] == "B" and not reload_at:
                        while next_pv[0] < i:
                            emit_pv(next_pv[0])
                            next_pv[0] += 1
                        unit_loads(gr["unit"])
                    emit_qk(i)
                while next_pv[0] < NTL and next_pv[0] < i + 1 and next_pv[0] + lag(next_pv[0]) <= i:
                    emit_pv(next_pv[0])
                    next_pv[0] += 1
            assert next_pv[0] == NTL
            while pending:
                epilogue(pending.pop(0)[1])

            out_stores = []
            for t in range(0 if dbg.get("stop") == "p2" else NT):
                tsl = slice(t * T, (t + 1) * T)
                atkeys = tuple(dk("AT", ch, t) for ch in range(16))
                load_tile(hT, AT[:, :, tsl], atkeys, "ldat")
                load_tile(xt, X1[:, :, tsl], (dk("X1", t),), "ldx")

                def ld_wo(m):
                    wload(wqk[m % 3].all(), wout_b[m] if preconv else None, wout_d[m], dk("wout", m), "ldqk%d" % (m % 3))

                for m in range(3):
                    ld_wo(m)
                for m in range(KC):
                    s = wqk[m % 3]
                    for kc in range(KC):
                        mm(psv(m % 2), s.sub(kc, 128), hT.sub(kc, T), kc == 0, kc == KC - 1)
                    if m + 3 < KC:
                        ld_wo(m + 3)
                    tt(xt.sub(m, T), psv(m % 2), xt.sub(m, T), ALU.add)
                norm_to_hT(2 * KC)
                ffn(1)
                r = rstd[0].all()
                rmsnorm_stats(lambda kc: xt.sub(kc, T), KC, D, 6, r)
                for kc in range(KC):
                    stt(xt.sub(kc, T), xt.sub(kc, T), gains.cols(3 * KC + kc, 3 * KC + kc + 1), r, ALU.mult, ALU.mult)
                out_stores += store_tile(outT[:, :, tsl], xt, (dk("OUT", t),), "sto")

            if dbg.get("stop") == "p2":
                return [lst[-1] for st, lst in S.streams.items() if st not in ("pe", "act", "dve", "pool")]
            return out_stores

        if RUN_REST:
            out_stores = rest()
        else:
            out_stores = [lst[-1] for st, lst in S.streams.items() if st not in ("pe", "act", "dve", "pool")]
        S.barrier("sp", out_stores)
        S.finalize()

        @block.tensor
        def _(e):
            S.replay("pe", e)

        @block.scalar
        def _(e):
            S.replay("act", e)

        @block.vector
        def _(e):
            S.replay("dve", e)

        @block.gpsimd
        def _(e):
            S.replay("pool", e)

        @block.sync
        def _(e):
            S.replay("sp", e)

    return nc


def _rope_tables(qtr):
    s = (np.arange(TOK, dtype=np.float32) + np.float32(qtr * TOK))
    row = np.floor(s / 64).astype(np.float32)
    col = (s - row * 64).astype(np.float32)
    fA = (np.float32(10000.0) ** (-np.arange(0, 64, 2, dtype=np.float32) / np.float32(64))).astype(np.float32)
    fB = (np.float32(500000.0) ** (-np.arange(0, 32, 2, dtype=np.float32) / np.float32(32))).astype(np.float32)
    angr = (row[None, :] * fA[:, None]).astype(np.float32)
    angc = (col[None, :] * fA[:, None]).astype(np.float32)
    angl = (s[None, :] * fB[:, None]).astype(np.float32)
    cosA = np.concatenate([np.cos(angr), np.cos(angr), np.cos(angc), np.cos(angc)], 0)
    sinA = np.concatenate([-np.sin(angr), np.sin(angr), -np.sin(angc), np.sin(angc)], 0)
    cosB = np.ones((128, TOK), np.float32)
    sinB = np.zeros((128, TOK), np.float32)
    cosB[0:16] = np.cos(angl)
    cosB[16:32] = np.cos(angl)
    sinB[0:16] = -np.sin(angl)
    sinB[16:32] = np.sin(angl)
    return np.ascontiguousarray(np.stack([cosA, sinA, cosB, sinB], 0).astype(np.float32))


def _perm_consts():
    PA = np.zeros((128, 128), np.float32)
    PB = np.zeros((128, 128), np.float32)
    for m in range(128):
        blk = (m // 64) * 64
        r = m - blk
        PA[blk + (r + 32) % 64, m] = 1.0
    for m in range(32):
        PB[(m + 16) % 32, m] = 1.0
    return np.ascontiguousarray(np.concatenate([PA, PB, np.eye(128, dtype=np.float32)], 1))


_NC_CACHE = {}


def kernel(x, ffn1_norm, ffn1_w_gu, ffn1_w_down, mix_norm, w_in,
           a_q_norm, a_k_norm, b_q_norm, b_k_norm,
           b_lambda_q1, b_lambda_k1, b_lambda_q2, b_lambda_k2, b_subln,
           w_out, ffn2_norm, ffn2_w_gu, ffn2_w_down, out_norm):
    f32 = np.float32
    x = np.asarray(x, f32)

    def lay_gu(w):
        w = np.asarray(w, f32)[0].reshape(KC, 128, 2, JC, 128)
        return np.ascontiguousarray(w.transpose(3, 1, 2, 0, 4)).reshape(JC, 128, 2 * KC * 128)

    def lay_d(w):
        w = np.asarray(w, f32)[0].reshape(JC, 128, KC, 128)
        return np.ascontiguousarray(w.transpose(2, 1, 0, 3)).reshape(KC, 128, JC * 128)

    def lay_sq(w):
        w = np.asarray(w, f32).reshape(KC, 128, KC, 128)
        return np.ascontiguousarray(w.transpose(2, 1, 0, 3)).reshape(KC, 128, KC * 128)

    win = np.asarray(w_in, f32)[0]
    col0 = ([c * 128 for c in range(8)] + [1024 + g * 128 for g in range(2)]
            + [1536 + i * 128 for i in range(8)] + [2560 + i * 128 for i in range(8)]
            + [1280 + g * 128 for g in range(2)] + [3584 + i * 128 for i in range(8)])
    winc = np.stack([win[:, c0:c0 + 128].reshape(KC, 128, 128).transpose(1, 0, 2).reshape(128, KC * 128)
                     for c0 in col0], 0)
    winc = np.ascontiguousarray(winc)

    def gl(v):
        return np.asarray(v, f32).reshape(KC, 128).T

    gains = np.ascontiguousarray(np.concatenate([gl(ffn1_norm), gl(mix_norm), gl(ffn2_norm), gl(out_norm)], 1))
    small = np.zeros((128, 16), f32)
    for i, v in enumerate([a_q_norm, a_k_norm, b_q_norm, b_k_norm, b_lambda_q1, b_lambda_k1, b_lambda_q2, b_lambda_k2]):
        small[:, i] = np.asarray(v, f32).reshape(128)
    sl = np.asarray(b_subln, f32).reshape(2, 128)
    small[:, 8] = sl[0]
    small[:, 9] = sl[1]

    common = dict(gains=gains, small=small, wgu1=lay_gu(ffn1_w_gu), wgu2=lay_gu(ffn2_w_gu),
                  wd1=lay_d(ffn1_w_down), wd2=lay_d(ffn2_w_down), winc=winc,
                  wout=lay_sq(np.asarray(w_out, f32)[0]), perm=_perm_consts())
    ropes = [_rope_tables(q) for q in range(4)]
    in_maps = []
    for c in range(8):
        b, q = divmod(c, 4)
        xs = x[b, q * TOK:(q + 1) * TOK, :]
        m = dict(common)
        m["xT"] = np.ascontiguousarray(xs.T).reshape(KC, 128, TOK)
        m["rope"] = ropes[q]
        in_maps.append(m)

    if "nc" not in _NC_CACHE:
        _NC_CACHE["nc"] = build_nc()
    nc = _NC_CACHE["nc"]
    res = run_bass_kernel_spmd(nc, in_maps, core_ids=list(range(8)))
    out = np.empty((2, SEQ, D), f32)
    for c in range(8):
        b, q = divmod(c, 4)
        o = np.asarray(res.results[c]["outT"], f32).reshape(D, TOK)
        out[b, q * TOK:(q + 1) * TOK, :] = o.T
    return out
```

```python
import math
from contextlib import ExitStack

import numpy as np
import concourse.bass as bass
import concourse.mybir as mybir
from concourse.bass_utils import run_bass_kernel_spmd

F32 = mybir.dt.float32
BF16 = mybir.dt.bfloat16
AF = mybir.ActivationFunctionType
ALU = mybir.AluOpType

D = 2048
KC = 16
DFF = 5632
JC = 44
T = 512
TOK = 4096
NT = TOK // T
SEQ = 16384
EPS = 1e-6
SCALE = 128 ** -0.5
LAMBDA_INIT = 0.8 - 0.6 * math.exp(0.0)
G = 256
ARENA_BYTES = 184 * 1024
PS_KEY = 1 << 20
DR_KEY = 1 << 21
SEM_LIMIT = 30000


class View:
    __slots__ = ("ap", "keys")

    def __init__(self, ap, keys):
        self.ap = ap
        self.keys = keys


class Op:
    __slots__ = ("eng", "emit", "deps", "signal", "sem", "val", "epoch", "stream", "inc", "idx")


class Sched:
    def __init__(self, sem_pool):
        self.sem_pool = list(sem_pool)
        self.q = {e: [] for e in ("pe", "act", "dve", "pool", "sp")}
        self.streams = {}
        self.lastw = {}
        self.readers = {}
        self.nops = 0

    def op(self, eng, emit, reads=(), writes=(), dma=None, inc=None, nosame=False):
        o = Op()
        o.eng = eng
        o.emit = emit
        o.signal = False
        o.sem = None
        o.val = 0
        o.epoch = 0
        o.stream = dma if dma is not None else eng
        o.inc = inc if inc is not None else (16 if dma is not None else 1)
        deps = {}
        st = o.stream
        is_dma = dma is not None
        lastw = self.lastw
        readers = self.readers
        for k in reads:
            w = lastw.get(k)
            if w is not None and not ((st == "pe" or nosame) and w.stream == st):
                deps[id(w)] = w
        for k in writes:
            w = lastw.get(k)
            if w is not None and (is_dma or w.stream != st):
                deps[id(w)] = w
            rs = readers.get(k)
            if rs:
                for rst, r in rs.items():
                    if is_dma or rst != st:
                        deps[id(r)] = r
        lst = self.streams.setdefault(st, [])
        if is_dma and lst and st != "cc":
            p = lst[-1]
            deps[id(p)] = p
        deps.pop(id(o), None)
        best = {}
        for d in deps.values():
            cur = best.get(d.stream)
            if cur is None or cur.idx < d.idx:
                best[d.stream] = d
        for d in best.values():
            d.signal = True
        o.deps = list(best.values())
        self.nops += 1
        o.idx = self.nops
        for k in writes:
            lastw[k] = o
            readers[k] = {}
        for k in reads:
            rs = readers.get(k)
            if rs is None:
                rs = readers[k] = {}
            rs[st] = o
        lst.append(o)
        self.q[eng].append(o)
        return o

    def barrier(self, eng, ops):
        o = Op()
        o.eng = eng
        o.emit = None
        o.signal = False
        o.sem = None
        o.val = 0
        o.epoch = 0
        o.stream = eng
        o.inc = 1
        o.idx = 0
        for d in ops:
            d.signal = True
        o.deps = list(ops)
        self.q[eng].append(o)
        return o

    def finalize(self):
        for st, lst in self.streams.items():
            cnt = 0
            epoch = 0
            sem = None
            for o in lst:
                if not o.signal:
                    continue
                if sem is None or cnt + o.inc > SEM_LIMIT:
                    sem = self.sem_pool.pop()
                    cnt = 0
                    epoch += 1
                cnt += o.inc
                o.sem = sem
                o.val = cnt
                o.epoch = epoch

    def replay(self, eng, e):
        waited = {}
        for o in self.q[eng]:
            best = {}
            for d in o.deps:
                cur = best.get(d.stream)
                if cur is None or (cur.epoch, cur.val) < (d.epoch, d.val):
                    best[d.stream] = d
            for d in best.values():
                cur = waited.get(d.stream)
                key = (d.epoch, d.val)
                if cur is not None and cur >= key:
                    continue
                e.wait_ge(d.sem, d.val)
                waited[d.stream] = key
            if o.emit is not None:
                inst = o.emit(e)
                if o.signal:
                    inst.then_inc(o.sem, o.inc)


class Buf:
    def __init__(self, arena, off, n, dt):
        self.esz = 4 if dt is F32 else 2
        self.off = off
        self.n = n
        self.dt = dt
        nb = n * self.esz
        assert off % 4 == 0 and nb % 4 == 0
        ap = arena[:, off // 4:(off + nb) // 4]
        if dt is not F32:
            ap = ap.bitcast(dt)
        self.ap = ap

    def _keys(self, a, b):
        lo = self.off + a * self.esz
        hi = self.off + b * self.esz
        return tuple(range(lo // G, (hi - 1) // G + 1))

    def cols(self, a, b):
        return View(self.ap[:, a:b], self._keys(a, b))

    def sub(self, i, m):
        return self.cols(i * m, (i + 1) * m)

    def all(self):
        return View(self.ap, self._keys(0, self.n))

    def all3(self, m):
        return View(self.ap.rearrange("p (n m) -> p n m", m=m), self._keys(0, self.n))

    def range3(self, i0, i1, m):
        return View(self.ap[:, i0 * m:i1 * m].rearrange("p (n m) -> p n m", m=m), self._keys(i0 * m, i1 * m))


def build_nc(preconv=True, dbg=None):
    dbg = dbg or {}
    nc = bass.Bass("TRN2", target_bir_lowering=False)

    def din(name, shape, dt=F32):
        return nc.dram_tensor(name, shape, dt, kind="ExternalInput").ap()

    xT = din("xT", [KC, 128, TOK])
    gains_d = din("gains", [128, 4 * KC])
    small_d = din("small", [128, 16])
    wgu_d = [din("wgu1", [JC, 128, 2 * KC * 128]), din("wgu2", [JC, 128, 2 * KC * 128])]
    wd_d = [din("wd1", [KC, 128, JC * 128]), din("wd2", [KC, 128, JC * 128])]
    winc_d = din("winc", [36, 128, KC * 128])
    wout_d = din("wout", [KC, 128, KC * 128])
    rope_d = din("rope", [4, 128, TOK])
    perm_d = din("perm", [128, 3 * 128])
    outT = nc.dram_tensor("outT", [KC, 128, TOK], F32, kind="ExternalOutput").ap()

    skind = dict(kind="ExternalOutput") if dbg else {}
    X1 = nc.dram_tensor("X1", [KC, 128, TOK], F32, **skind).ap()
    QS = nc.dram_tensor("QS", [16, 128, TOK], BF16, **skind).ap()
    KL = [nc.dram_tensor("KL%d" % i, [128, TOK], BF16).ap() for i in range(10)]
    VL = [nc.dram_tensor("VL%d" % i, [128, TOK], BF16).ap() for i in range(10)]
    KGt = [nc.dram_tensor("KG%d" % i, [512, TOK], BF16).ap() for i in range(10)]
    VGt = [nc.dram_tensor("VG%d" % i, [512, TOK], BF16).ap() for i in range(10)]
    AT = nc.dram_tensor("AT", [16, 128, TOK], BF16, **skind).ap()
    if preconv:
        wgu_b = [nc.dram_tensor("wgu1b", [JC, 128, 2 * KC * 128], BF16).ap(),
                 nc.dram_tensor("wgu2b", [JC, 128, 2 * KC * 128], BF16).ap()]
        wd_b = [nc.dram_tensor("wd1b", [KC, 128, JC * 128], BF16).ap(),
                nc.dram_tensor("wd2b", [KC, 128, JC * 128], BF16).ap()]
        winc_b = nc.dram_tensor("wincb", [36, 128, KC * 128], BF16).ap()
        wout_b = nc.dram_tensor("woutb", [KC, 128, KC * 128], BF16).ap()

    dkeys = {}

    def dk(*name):
        k = dkeys.get(name)
        if k is None:
            k = dkeys[name] = DR_KEY + len(dkeys)
        return k

    with ExitStack() as es:
        arena = es.enter_context(nc.sbuf_tensor("arena", [128, ARENA_BYTES // 4], F32))
        ps = es.enter_context(nc.psum_tensor("ps", [128, 8, 512], F32))
        sems = [es.enter_context(nc.semaphore("s%d" % i)) for i in range(90)]
        block = es.enter_context(nc.Block())
        S = Sched(sems)
        K = 1024

        def psv(b):
            return View(ps[:, b, :], (PS_KEY + b,))

        def psbf(b, a, c):
            return View(ps[:, b, :].bitcast(BF16)[:, a:c], (PS_KEY + b,))

        def psc(b, a, c):
            return View(ps[:, b, a:c], (PS_KEY + b,))

        cbase = 180 * K
        gains = Buf(arena, cbase, 64, F32)
        small = Buf(arena, cbase + 256, 16, F32)
        misc = Buf(arena, cbase + 512, 16, F32)
        ones_f = Buf(arena, cbase + 768, 128, F32)
        perm_bf = Buf(arena, cbase + 768 + 512, 3 * 128, BF16)
        ones_bf = Buf(arena, cbase + 768 + 512 + 768, 128, BF16)

        xts = [Buf(arena, 0, KC * T, F32), Buf(arena, 145 * K, KC * T, F32)]
        xt_cur = [xts[0]]
        hT = Buf(arena, 32 * K, KC * T, BF16)
        actT = Buf(arena, 48 * K, JC * T, BF16)
        wgu = [Buf(arena, 92 * K + i * 8 * K, 2 * KC * 128, BF16) for i in range(3)]
        wdh = [Buf(arena, 116 * K + i * 5632, 22 * 128, BF16) for i in range(3)]
        mb = 133 * K
        sqb = [Buf(arena, mb + i * 2 * K, T, F32) for i in range(2)]
        rstd = [Buf(arena, mb + 4 * K + i * 2 * K, T, F32) for i in range(2)]
        sg = [Buf(arena, mb + 8 * K + i * 2 * K, T, F32) for i in range(2)]
        pb0 = 48 * K
        wqk = [Buf(arena, pb0 + i * 4 * K, KC * 128, BF16) for i in range(3)]
        ropeb = Buf(arena, pb0 + 12 * K, 4 * T, F32)
        t1b = [Buf(arena, pb0 + 20 * K + i * 2 * K, T, F32) for i in range(2)]
        t2b = [Buf(arena, pb0 + 24 * K + i * 2 * K, T, F32) for i in range(2)]
        qnb = [Buf(arena, pb0 + 28 * K + i * K, T, BF16) for i in range(2)]
        resb = [Buf(arena, pb0 + 30 * K + i * K, T, BF16) for i in range(3)]
        vTb = [Buf(arena, pb0 + 33 * K + i * K, T, BF16) for i in range(2)]

        R = [Buf(arena, i * 32 * K, SEQ, BF16) for i in range(4)]
        p2 = 128 * K
        Qt = [Buf(arena, p2 + i * K, T, BF16) for i in range(4)]
        NPB = 6
        Pb = [Buf(arena, p2 + 4 * K + i * 2 * K, 2 * T, BF16) for i in range(NPB)]
        q2 = p2 + 16 * K
        o1 = Buf(arena, q2, 2 * T, F32)
        tmpb = Buf(arena, q2 + 4 * K, 2 * T, F32)
        sqd = Buf(arena, q2 + 8 * K, 2 * T, F32)
        sums = [Buf(arena, q2 + 12 * K + i * 2 * K, T, F32) for i in range(2)]
        rs3 = Buf(arena, q2 + 16 * K, T, F32)
        ost = [Buf(arena, q2 + 18 * K + i * K, T, BF16) for i in range(3)]
        accD = [Buf(arena, q2 + 21 * K + i * 2 * K, T, F32) for i in range(2)]
        accP = [Buf(arena, q2 + 25 * K + i * 2 * K, T, F32) for i in range(2)]
        assert q2 + 29 * K <= 180 * K

        def mm(out, l, r, start, stop):
            return S.op("pe", lambda e, o=out.ap, a=l.ap, b=r.ap: e.matmul(o, a, b, start=start, stop=stop),
                        reads=l.keys + r.keys, writes=out.keys)

        def tr(out, in_, ident):
            return S.op("pe", lambda e, o=out.ap, a=in_.ap, b=ident.ap: e.transpose(o, a, b),
                        reads=in_.keys + ident.keys, writes=out.keys)

        def act(out, in_, func, scale=None, bias=None):
            kw = {}
            rk = in_.keys
            if scale is not None:
                kw["scale"] = scale
            if bias is not None:
                kw["bias"] = bias.ap
                rk = rk + bias.keys
            return S.op("act", lambda e, o=out.ap, i=in_.ap: e.activation(out=o, in_=i, func=func, **kw),
                        reads=rk, writes=out.keys)

        def stt(out, in0, scalar, in1, op0, op1):
            rk = in0.keys + in1.keys
            sc = scalar
            if isinstance(scalar, View):
                rk = rk + scalar.keys
                sc = scalar.ap
            return S.op("dve", lambda e, o=out.ap, a=in0.ap, b=in1.ap: e.scalar_tensor_tensor(
                out=o, in0=a, scalar=sc, in1=b, op0=op0, op1=op1), reads=rk, writes=out.keys)

        def tt(out, in0, in1, op, eng="dve", nosame=False):
            return S.op(eng, lambda e, o=out.ap, a=in0.ap, b=in1.ap: e.tensor_tensor(out=o, in0=a, in1=b, op=op),
                        reads=in0.keys + in1.keys, writes=out.keys, nosame=nosame)

        def tsc(out, in0, s1, op0):
            return S.op("dve", lambda e, o=out.ap, a=in0.ap: e.tensor_scalar(
                out=o, in0=a, scalar1=s1, scalar2=None, op0=op0), reads=in0.keys, writes=out.keys)

        def cp(out, in_, eng="dve", nosame=False):
            if eng == "act":
                return act(out, in_, AF.Copy)
            return S.op(eng, lambda e, o=out.ap, i=in_.ap: e.tensor_copy(out=o, in_=i),
                        reads=in_.keys, writes=out.keys, nosame=nosame)

        def recip(out, in_):
            return S.op("dve", lambda e, o=out.ap, i=in_.ap: e.reciprocal(out=o, in_=i),
                        reads=in_.keys, writes=out.keys)

        def memset(v, val, eng="dve"):
            return S.op(eng, lambda e, a=v.ap: e.memset(a, val), writes=v.keys)

        def dma(eng, out_ap, in_ap, reads, writes, stream):
            return S.op(eng, lambda e, o=out_ap, i=in_ap: e.dma_start(out=o, in_=i),
                        reads=reads, writes=writes, dma=stream)

        cv_i = [0]

        def conv(dst_ap, src_ap, key, after=()):
            st = "cv%d" % (cv_i[0] % 6)
            cv_i[0] += 1
            dma("pool", dst_ap, src_ap, after, (key,), st)

        def wload(dst, src_b, src_d, key, stream):
            if preconv:
                dma("sp", dst.ap, src_b, (key,), dst.keys, stream)
            else:
                dma("pool", dst.ap, src_d, (), dst.keys, stream)

        dma("sp", gains.ap, gains_d, (), gains.all().keys, "ldc0")
        dma("sp", small.ap, small_d, (), small.all().keys, "ldc1")
        dma("pool", perm_bf.ap, perm_d, (), perm_bf.all().keys, "ldc2")
        memset(ones_f.all(), 1.0)
        memset(ones_bf.all(), 1.0)
        memset(misc.all(), 0.0)
        memset(misc.cols(0, 1), EPS)
        eps_c = misc.cols(0, 1)
        permA = perm_bf.cols(0, 128)
        permB = perm_bf.cols(128, 256)
        ident = perm_bf.cols(256, 384)
        tt(misc.cols(1, 2), small.cols(4, 5), small.cols(5, 6), ALU.mult)
        tt(misc.cols(2, 3), small.cols(6, 7), small.cols(7, 8), ALU.mult)
        mm(psc(7, 0, 2), ones_f.all(), misc.cols(1, 3), True, True)
        act(misc.cols(3, 5), psc(7, 0, 2), AF.Exp)
        tt(misc.cols(5, 6), misc.cols(4, 5), misc.cols(3, 4), ALU.subtract)
        tsc(misc.cols(5, 6), misc.cols(5, 6), -LAMBDA_INIT, ALU.add)
        tsc(misc.cols(6, 8), small.cols(8, 10), 1.0 - LAMBDA_INIT, ALU.mult)
        neglam = misc.cols(5, 6)

        if preconv:
            def conv_ffn(f):
                for j in range(JC):
                    conv(wgu_b[f][j], wgu_d[f][j], dk("wgu", f, j))
                for m in range(KC):
                    conv(wd_b[f][m], wd_d[f][m], dk("wd", f, m))
            conv_ffn(0)
            for c in range(36):
                conv(winc_b[c], winc_d[c], dk("winc", c))
            late_conv = [(wout_b[m], wout_d[m], dk("wout", m)) for m in range(KC)]
            late_conv += [(wgu_b[1][j], wgu_d[1][j], dk("wgu", 1, j)) for j in range(JC)]
            late_conv += [(wd_b[1][m], wd_d[1][m], dk("wd", 1, m)) for m in range(KC)]


        def rmsnorm_stats(src, nchunks, dim, psb, rbuf):
            for kc in range(nchunks):
                sq = sqb[kc % 2].all()
                act(sq, src(kc), AF.Square)
                mm(psv(psb), ones_f.all(), sq, kc == 0, kc == nchunks - 1)
            act(rbuf, psv(psb), AF.Sqrt, scale=1.0 / dim, bias=eps_c)
            recip(rbuf, rbuf)

        def norm_to_hT(gbase):
            r = rstd[0].all()
            rmsnorm_stats(lambda kc: xt_cur[0].sub(kc, T), KC, D, 6, r)
            for kc in range(KC):
                stt(hT.sub(kc, T), xt_cur[0].sub(kc, T), gains.cols(gbase + kc, gbase + kc + 1), r, ALU.mult, ALU.mult)

        def ffn(f):
            def ld_gu(j):
                s = wgu[j % 3]
                wload(s.all(), wgu_b[f][j] if preconv else None, wgu_d[f][j], dk("wgu", f, j), "ldgu%d" % (j % 3))

            def ld_d(i):
                m, half = divmod(i, 2)
                s = wdh[i % 3]
                sb = wd_b[f][m][:, half * 2816:(half + 1) * 2816] if preconv else None
                sd = wd_d[f][m][:, half * 2816:(half + 1) * 2816]
                wload(s.all(), sb, sd, dk("wd", f, m), "ldd%d" % (i % 3))

            for j in range(3):
                ld_gu(j)
            for i in range(3):
                ld_d(i)
            for j in range(JC):
                s = wgu[j % 3]
                pg, pu = (0, 1) if j % 2 == 0 else (2, 3)
                for kc in range(KC):
                    mm(psv(pg), s.sub(kc, 128), hT.sub(kc, T), kc == 0, kc == KC - 1)
                for kc in range(KC):
                    mm(psv(pu), s.sub(KC + kc, 128), hT.sub(kc, T), kc == 0, kc == KC - 1)
                if j + 3 < JC:
                    ld_gu(j + 3)
                sgv = sg[j % 2].all()
                act(sgv, psv(pg), AF.Silu)
                tt(actT.sub(j, T), sgv, psv(pu), ALU.mult)
            for m in range(KC):
                pbk = 4 + m % 2
                for half in range(2):
                    i = 2 * m + half
                    s = wdh[i % 3]
                    for jj in range(22):
                        jc = half * 22 + jj
                        mm(psv(pbk), s.sub(jj, 128), actT.sub(jc, T), jc == 0, jc == JC - 1)
                    if i + 3 < 2 * KC:
                        ld_d(i + 3)
                stt(xt_cur[0].sub(m, T), psv(pbk), 0.5, xt_cur[0].sub(m, T), ALU.mult, ALU.add)

        def load_tile(dst, src, keys, stream):
            for hh in range(2):
                dma("sp", dst.range3(hh * 8, hh * 8 + 8, T).ap,
                    src[hh * 8:hh * 8 + 8].rearrange("k p t -> p k t"),
                    keys, dst.range3(hh * 8, hh * 8 + 8, T).keys, "%s%d" % (stream, hh))

        def store_tile(dst, src, keys, stream):
            ops = []
            for hh in range(2):
                ops.append(dma("sp", dst[hh * 8:hh * 8 + 8].rearrange("k p t -> p k t"),
                               src.range3(hh * 8, hh * 8 + 8, T).ap,
                               src.range3(hh * 8, hh * 8 + 8, T).keys, keys, "%s%d" % (stream, hh)))
            return ops

        def qk_spec(c):
            if c < 8:
                return True, True, 0, "Q", c
            if c < 10:
                return True, True, 1, "K", c - 8
            if c < 18:
                return True, False, 2, "Q", 8 + (c - 10)
            if c < 26:
                return True, False, 3, "K", 2 + (c - 18)
            return False, False, 0, "V", c - 26

        for t in range(dbg.get("nt1", NT)):
            tsl = slice(t * T, (t + 1) * T)
            xt_cur[0] = xts[t % 2]
            if t == 0:
                load_tile(xts[0], xT[:, :, tsl], (), "ldx0_")
            if t + 1 < dbg.get("nt1", NT):
                load_tile(xts[(t + 1) % 2], xT[:, :, (t + 1) * T:(t + 2) * T], (), "ldx%d_" % ((t + 1) % 2))
            norm_to_hT(0)
            ffn(0)
            store_tile(X1[:, :, tsl], xt_cur[0], (dk("X1", t),), "stx%d_" % (t % 2))
            norm_to_hT(KC)
            dma("sp", ropeb.all3(T).ap, rope_d[:, :, tsl].rearrange("f p t -> p f t"), (), ropeb.all().keys, "ldrope")

            def ld_w(c):
                wload(wqk[c % 3].all(), winc_b[c] if preconv else None, winc_d[c], dk("winc", c), "ldqk%d" % (c % 3))

            for c in range(3):
                ld_w(c)
            PBK = (0, 1, 4)

            def stage_proj(c):
                s = wqk[c % 3]
                pbk = PBK[c % 3]
                for kc in range(KC):
                    mm(psv(pbk), s.sub(kc, 128), hT.sub(kc, T), kc == 0, kc == KC - 1)
                if c + 3 < 36:
                    ld_w(c + 3)
                if qk_spec(c)[0]:
                    act(sqb[c % 2].all(), psv(pbk), AF.Square)
                else:
                    act(vTb[c % 2].all(), psv(pbk), AF.Copy)

            def stage_ss(c):
                isqk, useA, gcol, kind, chunk = qk_spec(c)
                pbk = PBK[c % 3]
                if isqk:
                    r = rstd[c % 2].all()
                    mm(psv(2), ones_f.all(), sqb[c % 2].all(), True, True)
                    act(r, psv(2), AF.Sqrt, scale=1.0 / 128, bias=eps_c)
                    recip(r, r)
                    stt(qnb[c % 2].all(), psv(pbk), small.cols(gcol, gcol + 1), r, ALU.mult, ALU.mult)
                else:
                    res = resb[c % 3].all()
                    for s4 in range(4):
                        tr(psbf(5, s4 * 128, (s4 + 1) * 128), vTb[c % 2].cols(s4 * 128, (s4 + 1) * 128), ident)
                    cp(res, psbf(5, 0, 512))
                    dma("sp", VL[chunk][:, tsl], res.ap, res.keys,
                        (dk("VL", chunk, t),), "stres%d" % (c % 3))

            def stage_rot(c):
                isqk, useA, gcol, kind, chunk = qk_spec(c)
                if not isqk:
                    return
                res = resb[c % 3].all()
                qn = qnb[c % 2].all()
                mm(psv(3), permA if useA else permB, qn, True, True)
                cosv = ropeb.sub(0 if useA else 2, T)
                sinv = ropeb.sub(1 if useA else 3, T)
                t1 = t1b[c % 2].all()
                t2 = t2b[c % 2].all()
                tt(t1, qn, cosv, ALU.mult)
                tt(t2, psv(3), sinv, ALU.mult)
                tt(res, t1, t2, ALU.add)
                if kind == "Q":
                    dma("sp", QS[chunk][:, tsl], res.ap, res.keys, (dk("QS", chunk, t),), "stres%d" % (c % 3))
                else:
                    dma("sp", KL[chunk][:, tsl], res.ap, res.keys,
                        (dk("KL", chunk, t),), "stres%d" % (c % 3))

            for c in range(36 + 2):
                if c < 36:
                    stage_proj(c)
                if 1 <= c <= 36:
                    stage_ss(c - 1)
                if c >= 2:
                    stage_rot(c - 2)

        RUN_REST = dbg.get("stop") != "p1"
        def rest():
            groups4 = [[0, 1, 2, 3], [4, 5, 6, 7]]

            def gather(src_l, dst_l, nm, ch):
                S.op("pool", lambda e, a=src_l[ch], b=dst_l[ch]: e.collective_compute(
                    "AllGather", ALU.bypass, replica_groups=groups4, ins=[a.opt()], outs=[b.opt()]),
                    reads=tuple(dk(nm + "L", ch, t) for t in range(NT)), writes=(dk(nm + "G", ch),), dma="cc", inc=1)
            for ch in (0, 1):
                gather(KL, KGt, "K", ch)
                gather(VL, VGt, "V", ch)
            for h in range(4):
                for ch in (2 + 2 * h, 3 + 2 * h):
                    gather(KL, KGt, "K", ch)
                for ch in (2 + 2 * h, 3 + 2 * h):
                    gather(VL, VGt, "V", ch)

            if dbg.get("stop") == "cc":
                S.barrier("pool", [S.streams["cc"][-1]])
                return [lst[-1] for st, lst in S.streams.items() if st not in ("pe", "act", "dve", "pool", "cc")]
            def ld_kv(ri, src, chunk, nm, quarters=(0, 1, 2, 3)):
                for r in quarters:
                    dst = R[ri].cols(r * TOK, (r + 1) * TOK)
                    dma("sp", dst.ap, src[chunk][r * 128:(r + 1) * 128, :], (dk(nm, chunk),), dst.keys,
                        "ldR%d_%d" % (ri, r))

            groups = []
            for g in range(2):
                for r in range(4):
                    for qg in range(NT):
                        groups.append(dict(kind="A", k=g, v=[2 + g], qchunk=g * 4 + r, qg=qg, first=(r == 0 and qg == 0),
                                           unit=("A", g)))
            for h in range(4):
                for qg in range(NT):
                    for c in range(2):
                        groups.append(dict(kind="B", k=c, v=[2, 3], qchunk=8 + 2 * h + c, qg=qg, h=h, c=c,
                                           first=(qg == 0 and c == 0), unit=("B", h)))
            if dbg.get("groups") == "small":
                groups = [gr for gr in groups if gr["qg"] < 2 and ((gr["kind"] == "A" and gr["qchunk"] == 0) or
                                                                   (gr["kind"] == "B" and gr["h"] == 0))]
            NG = len(groups)

            reload_at = {}
            if NG == 128:
                reload_at[31] = [(0, KGt, 2, "KG"), (2, VGt, 2, "VG")]
                reload_at[63] = [(1, KGt, 3, "KG"), (3, VGt, 3, "VG")]
                for h in range(3):
                    nk = 2 + 2 * (h + 1)
                    reload_at[64 + 16 * h + 14] = [(0, KGt, nk, "KG")]
                    reload_at[64 + 16 * h + 15] = [(1, KGt, nk + 1, "KG"), (2, VGt, nk, "VG"), (3, VGt, nk + 1, "VG")]

            def unit_loads(unit):
                if unit[0] == "A":
                    g = unit[1]
                    ld_kv(g, KGt, g, "KG")
                    ld_kv(2 + g, VGt, g, "VG")
                else:
                    h = unit[1]
                    ld_kv(0, KGt, 2 + 2 * h, "KG")
                    ld_kv(1, KGt, 2 + 2 * h + 1, "KG")
                    ld_kv(2, VGt, 2 + 2 * h, "VG")
                    ld_kv(3, VGt, 2 + 2 * h + 1, "VG")

            def ld_q(gi):
                gr = groups[gi]
                qsl = slice(gr["qg"] * T, (gr["qg"] + 1) * T)
                dma("sp", Qt[gi % 4].ap, QS[gr["qchunk"]][:, qsl], (dk("QS", gr["qchunk"], gr["qg"]),),
                    Qt[gi % 4].all().keys, "ldq%d" % (gi % 4))

            ost_i = [0]

            def store_att(chunk, qg, v):
                dma("sp", AT[chunk][:, qg * T:(qg + 1) * T], v.ap, v.keys, (dk("AT", chunk, qg),),
                    "stat%d" % (ost_i[0] % 3))

            def next_ost():
                ost_i[0] += 1
                return ost[ost_i[0] % 3].all()

            PC0 = 192

            def obanks_of(gi):
                gr = groups[gi]
                if gr["kind"] == "A":
                    return [6]
                return [4, 5]

            def rcp_via_act(dst, src_ps, scale_in, bias, pw):
                act(dst, src_ps, AF.Ln, scale=scale_in, bias=bias)
                act(dst, dst, AF.Exp, scale=pw)

            def epilogue1(gi):
                gr = groups[gi]
                obanks = obanks_of(gi)
                if gr["kind"] == "B":
                    for half in range(2):
                        cp(tmpb.sub(half, T), psv(obanks[half]))
                else:
                    cp(tmpb.sub(gi % 2, T), psv(obanks[0]))

            def epilogue(gi):
                gr = groups[gi]
                a = gi % 2
                obanks = obanks_of(gi)
                tt(accD[a].cols(PC0, T), accD[a].cols(PC0, T), accP[a].cols(PC0, T), ALU.add)
                isA = gr["kind"] == "A"
                tb = 7 if isA else 6
                mm(psv(tb), ones_f.all(), accD[a].all(), not isA, True)
                sm = sums[a].all()
                rcp_via_act(sm, psv(tb), 1.0, None, -1.0)
                if gr["kind"] == "A":
                    o = next_ost()
                    tt(o, tmpb.sub(gi % 2, T), sm, ALU.mult)
                    store_att(gr["qchunk"], gr["qg"], o)
                elif gr["c"] == 0:
                    for half in range(2):
                        tt(o1.sub(half, T), tmpb.sub(half, T), sm, ALU.mult)
                else:
                    for half in range(2):
                        stt(tmpb.sub(half, T), tmpb.sub(half, T), neglam, sm, ALU.mult, ALU.mult)
                        tt(o1.sub(half, T), o1.sub(half, T), tmpb.sub(half, T), ALU.add)
                    for half in range(2):
                        act(sqd.sub(half, T), o1.sub(half, T), AF.Square)
                        mm(psv(7), ones_f.all(), sqd.sub(half, T), half == 0, half == 1)
                    rcp_via_act(rs3.all(), psv(7), 1.0 / 256, eps_c, -0.5)
                    for half in range(2):
                        o = next_ost()
                        stt(o, o1.sub(half, T), misc.cols(6 + half, 7 + half), rs3.all(), ALU.mult, ALU.mult)
                        store_att(8 + 2 * gr["h"] + half, gr["qg"], o)

            unit_loads(("A", 0))
            unit_loads(("A", 1))
            ld_q(0)
            ld_q(1)
            NKS = SEQ // 256
            steps = [(gi, s) for gi in range(NG) for s in range(NKS)]

            SUMP = 4

            def pe_sum_step(gi, s):
                return groups[gi]["kind"] == "A" and s % SUMP == SUMP - 1

            def emit_qk(i):
                gi, s = steps[i]
                gr = groups[gi]
                if s == 0 and gi + 2 < NG:
                    ld_q(gi + 2)
                if preconv and late_conv and i >= 64 and i % 24 == 0:
                    d_, s_, k_ = late_conv.pop(0)
                    conv(d_, s_, k_)
                p = i % 3 if gr["kind"] == "A" else i % 2
                for u in range(2):
                    kt = 2 * s + u
                    mm(psv(2 * p + u), R[gr["k"]].cols(kt * 128, (kt + 1) * 128), Qt[gi % 4].all(), True, True)
                Pv = Pb[i % NPB]
                S.op("act", lambda e, o=Pv.all3(T).ap, i_=ps[:, 2 * p:2 * p + 2, :]: e.activation(
                    out=o, in_=i_, func=AF.Exp, scale=SCALE),
                    reads=(PS_KEY + 2 * p, PS_KEY + 2 * p + 1), writes=Pv.all().keys)
                a = gi % 2
                if pe_sum_step(gi, s):
                    return
                if s == 0:
                    cp(accD[a].all(), Pv.sub(0, T), nosame=True)
                    cp(accP[a].cols(PC0, T), Pv.cols(T + PC0, 2 * T), eng="pool", nosame=True)
                else:
                    tt(accD[a].all(), accD[a].all(), Pv.sub(0, T), ALU.add, nosame=True)
                    tt(accP[a].cols(PC0, T), accP[a].cols(PC0, T), Pv.cols(T + PC0, 2 * T), ALU.add, eng="pool", nosame=True)
                tt(accD[a].cols(0, PC0), accD[a].cols(0, PC0), Pv.cols(T, T + PC0), ALU.add, nosame=True)

            def emit_pv(i):
                gi, s = steps[i]
                gr = groups[gi]
                obanks = obanks_of(gi)
                Pv = Pb[i % NPB]
                for u in range(2):
                    kt = 2 * s + u
                    for hf, ob in enumerate(obanks):
                        mm(psv(ob), R[gr["v"][hf]].cols(kt * 128, (kt + 1) * 128), Pv.sub(u, T),
                           kt == 0, kt == 2 * NKS - 1)
                    if pe_sum_step(gi, s):
                        mm(psv(7), ones_bf.all(), Pv.sub(u, T), s == SUMP - 1 and u == 0, False)
                if gi in reload_at and (s + 1) % 16 == 0:
                    for (ri, src_l, chunk, nm) in reload_at[gi]:
                        ld_kv(ri, src_l, chunk, nm, quarters=((s + 1) // 16 - 1,))
                if s == NKS - 1:
                    epilogue1(gi)
                    pending.append((i + 1 + (2 if gr["kind"] == "A" else EPI_DELAY), gi))

            NTL = len(steps)
            pending = []
            EPI_DELAY = 4
            next_pv = [0]

            def lag(j):
                return 2 if groups[steps[j][0]]["kind"] == "A" else 1

            for i in range(NTL + 3):
                while pending and pending[0][0] <= i:
                    epilogue(pending.pop(0)[1])
                if i < NTL:
                    gi, s = steps[i]
                    gr = groups[gi]
                    if s == 0 and gr["first"] and gr["kind"] == "B" and not reload_at:
                        while next_pv[0] < i:
                            emit_pv(next_pv[0])
                            next_pv[0] += 1
                        unit_loads(gr["unit"])
                    emit_qk(i)
                while next_pv[0] < NTL and next_pv[0] < i + 1 and next_pv[0] + lag(next_pv[0]) <= i:
                    emit_pv(next_pv[0])
                    next_pv[0] += 1
            assert next_pv[0] == NTL
            while pending:
                epilogue(pending.pop(0)[1])

            out_stores = []
            for t in range(0 if dbg.get("stop") == "p2" else NT):
                tsl = slice(t * T, (t + 1) * T)
                atkeys = tuple(dk("AT", ch, t) for ch in range(16))
                load_tile(hT, AT[:, :, tsl], atkeys, "ldat")
                xt_cur[0] = xts[t % 2]
                if t == 0:
                    load_tile(xts[0], X1[:, :, tsl], (dk("X1", t),), "ldx0_")
                if t + 1 < NT:
                    load_tile(xts[(t + 1) % 2], X1[:, :, (t + 1) * T:(t + 2) * T], (dk("X1", t + 1),), "ldx%d_" % ((t + 1) % 2))

                def ld_wo(m):
                    wload(wqk[m % 3].all(), wout_b[m] if preconv else None, wout_d[m], dk("wout", m), "ldqk%d" % (m % 3))

                for m in range(3):
                    ld_wo(m)
                for m in range(KC):
                    s = wqk[m % 3]
                    for kc in range(KC):
                        mm(psv(m % 2), s.sub(kc, 128), hT.sub(kc, T), kc == 0, kc == KC - 1)
                    if m + 3 < KC:
                        ld_wo(m + 3)
                    tt(xt_cur[0].sub(m, T), psv(m % 2), xt_cur[0].sub(m, T), ALU.add)
                norm_to_hT(2 * KC)
                ffn(1)
                r = rstd[0].all()
                rmsnorm_stats(lambda kc: xt_cur[0].sub(kc, T), KC, D, 6, r)
                for kc in range(KC):
                    stt(xt_cur[0].sub(kc, T), xt_cur[0].sub(kc, T), gains.cols(3 * KC + kc, 3 * KC + kc + 1), r, ALU.mult, ALU.mult)
                out_stores += store_tile(outT[:, :, tsl], xt_cur[0], (dk("OUT", t),), "sto%d_" % (t % 2))

            if dbg.get("stop") == "p2":
                return [lst[-1] for st, lst in S.streams.items() if st not in ("pe", "act", "dve", "pool")]
            return out_stores

        if RUN_REST:
            out_stores = rest()
        else:
            out_stores = [lst[-1] for st, lst in S.streams.items() if st not in ("pe", "act", "dve", "pool")]
        S.barrier("sp", out_stores)
        S.finalize()

        @block.tensor
        def _(e):
            S.replay("pe", e)

        @block.scalar
        def _(e):
            S.replay("act", e)

        @block.vector
        def _(e):
            S.replay("dve", e)

        @block.gpsimd
        def _(e):
            S.replay("pool", e)

        @block.sync
        def _(e):
            S.replay("sp", e)

    return nc


def _rope_tables(qtr):
    s = (np.arange(TOK, dtype=np.float32) + np.float32(qtr * TOK))
    row = np.floor(s / 64).astype(np.float32)
    col = (s - row * 64).astype(np.float32)
    fA = (np.float32(10000.0) ** (-np.arange(0, 64, 2, dtype=np.float32) / np.float32(64))).astype(np.float32)
    fB = (np.float32(500000.0) ** (-np.arange(0, 32, 2, dtype=np.float32) / np.float32(32))).astype(np.float32)
    angr = (row[None, :] * fA[:, None]).astype(np.float32)
    angc = (col[None, :] * fA[:, None]).astype(np.float32)
    angl = (s[None, :] * fB[:, None]).astype(np.float32)
    cosA = np.concatenate([np.cos(angr), np.cos(angr), np.cos(angc), np.cos(angc)], 0)
    sinA = np.concatenate([-np.sin(angr), np.sin(angr), -np.sin(angc), np.sin(angc)], 0)
    cosB = np.ones((128, TOK), np.float32)
    sinB = np.zeros((128, TOK), np.float32)
    cosB[0:16] = np.cos(angl)
    cosB[16:32] = np.cos(angl)
    sinB[0:16] = -np.sin(angl)
    sinB[16:32] = np.sin(angl)
    return np.ascontiguousarray(np.stack([cosA, sinA, cosB, sinB], 0).astype(np.float32))


def _perm_consts():
    PA = np.zeros((128, 128), np.float32)
    PB = np.zeros((128, 128), np.float32)
    for m in range(128):
        blk = (m // 64) * 64
        r = m - blk
        PA[blk + (r + 32) % 64, m] = 1.0
    for m in range(32):
        PB[(m + 16) % 32, m] = 1.0
    return np.ascontiguousarray(np.concatenate([PA, PB, np.eye(128, dtype=np.float32)], 1))


_NC_CACHE = {}


def kernel(x, ffn1_norm, ffn1_w_gu, ffn1_w_down, mix_norm, w_in,
           a_q_norm, a_k_norm, b_q_norm, b_k_norm,
           b_lambda_q1, b_lambda_k1, b_lambda_q2, b_lambda_k2, b_subln,
           w_out, ffn2_norm, ffn2_w_gu, ffn2_w_down, out_norm):
    f32 = np.float32
    x = np.asarray(x, f32)

    def lay_gu(w):
        w = np.asarray(w, f32)[0].reshape(KC, 128, 2, JC, 128)
        return np.ascontiguousarray(w.transpose(3, 1, 2, 0, 4)).reshape(JC, 128, 2 * KC * 128)

    def lay_d(w):
        w = np.asarray(w, f32)[0].reshape(JC, 128, KC, 128)
        return np.ascontiguousarray(w.transpose(2, 1, 0, 3)).reshape(KC, 128, JC * 128)

    def lay_sq(w):
        w = np.asarray(w, f32).reshape(KC, 128, KC, 128)
        return np.ascontiguousarray(w.transpose(2, 1, 0, 3)).reshape(KC, 128, KC * 128)

    win = np.asarray(w_in, f32)[0]
    col0 = ([c * 128 for c in range(8)] + [1024 + g * 128 for g in range(2)]
            + [1536 + i * 128 for i in range(8)] + [2560 + i * 128 for i in range(8)]
            + [1280 + g * 128 for g in range(2)] + [3584 + i * 128 for i in range(8)])
    winc = np.stack([win[:, c0:c0 + 128].reshape(KC, 128, 128).transpose(1, 0, 2).reshape(128, KC * 128)
                     for c0 in col0], 0)
    winc = np.ascontiguousarray(winc)

    def gl(v):
        return np.asarray(v, f32).reshape(KC, 128).T

    gains = np.ascontiguousarray(np.concatenate([gl(ffn1_norm), gl(mix_norm), gl(ffn2_norm), gl(out_norm)], 1))
    small = np.zeros((128, 16), f32)
    for i, v in enumerate([a_q_norm, a_k_norm, b_q_norm, b_k_norm, b_lambda_q1, b_lambda_k1, b_lambda_q2, b_lambda_k2]):
        small[:, i] = np.asarray(v, f32).reshape(128)
    sl = np.asarray(b_subln, f32).reshape(2, 128)
    small[:, 8] = sl[0]
    small[:, 9] = sl[1]

    common = dict(gains=gains, small=small, wgu1=lay_gu(ffn1_w_gu), wgu2=lay_gu(ffn2_w_gu),
                  wd1=lay_d(ffn1_w_down), wd2=lay_d(ffn2_w_down), winc=winc,
                  wout=lay_sq(np.asarray(w_out, f32)[0]), perm=_perm_consts())
    ropes = [_rope_tables(q) for q in range(4)]
    in_maps = []
    for c in range(8):
        b, q = divmod(c, 4)
        xs = x[b, q * TOK:(q + 1) * TOK, :]
        m = dict(common)
        m["xT"] = np.ascontiguousarray(xs.T).reshape(KC, 128, TOK)
        m["rope"] = ropes[q]
        in_maps.append(m)

    if "nc" not in _NC_CACHE:
        _NC_CACHE["nc"] = build_nc()
    nc = _NC_CACHE["nc"]
    res = run_bass_kernel_spmd(nc, in_maps, core_ids=list(range(8)))
    out = np.empty((2, SEQ, D), f32)
    for c in range(8):
        b, q = divmod(c, 4)
        o = np.asarray(res.results[c]["outT"], f32).reshape(D, TOK)
        out[b, q * TOK:(q + 1) * TOK, :] = o.T
    return out
```

```python
import math
from contextlib import ExitStack

import numpy as np
import concourse.bass as bass
import concourse.mybir as mybir
from concourse.bass_utils import run_bass_kernel_spmd

F32 = mybir.dt.float32
BF16 = mybir.dt.bfloat16
AF = mybir.ActivationFunctionType
ALU = mybir.AluOpType

D = 2048
KC = 16
DFF = 5632
JC = 44
T = 512
TOK = 4096
NT = TOK // T
SEQ = 16384
EPS = 1e-6
SCALE = 128 ** -0.5
LAMBDA_INIT = 0.8 - 0.6 * math.exp(0.0)
G = 256
ARENA_BYTES = 184 * 1024
PS_KEY = 1 << 20
DR_KEY = 1 << 21
SEM_LIMIT = 30000


class View:
    __slots__ = ("ap", "keys")

    def __init__(self, ap, keys):
        self.ap = ap
        self.keys = keys


class Op:
    __slots__ = ("eng", "emit", "deps", "signal", "sem", "val", "epoch", "stream", "inc", "idx")


class Sched:
    def __init__(self, sem_pool):
        self.sem_pool = list(sem_pool)
        self.q = {e: [] for e in ("pe", "act", "dve", "pool", "sp")}
        self.streams = {}
        self.lastw = {}
        self.readers = {}
        self.nops = 0

    def op(self, eng, emit, reads=(), writes=(), dma=None, inc=None, nosame=False):
        o = Op()
        o.eng = eng
        o.emit = emit
        o.signal = False
        o.sem = None
        o.val = 0
        o.epoch = 0
        o.stream = dma if dma is not None else eng
        o.inc = inc if inc is not None else (16 if dma is not None else 1)
        deps = {}
        st = o.stream
        is_dma = dma is not None
        lastw = self.lastw
        readers = self.readers
        for k in reads:
            w = lastw.get(k)
            if w is not None and not ((st == "pe" or nosame) and w.stream == st):
                deps[id(w)] = w
        for k in writes:
            w = lastw.get(k)
            if w is not None and (is_dma or w.stream != st):
                deps[id(w)] = w
            rs = readers.get(k)
            if rs:
                for rst, r in rs.items():
                    if is_dma or rst != st:
                        deps[id(r)] = r
        lst = self.streams.setdefault(st, [])
        if is_dma and lst and st != "cc":
            p = lst[-1]
            deps[id(p)] = p
        deps.pop(id(o), None)
        best = {}
        for d in deps.values():
            cur = best.get(d.stream)
            if cur is None or cur.idx < d.idx:
                best[d.stream] = d
        for d in best.values():
            d.signal = True
        o.deps = list(best.values())
        self.nops += 1
        o.idx = self.nops
        for k in writes:
            lastw[k] = o
            readers[k] = {}
        for k in reads:
            rs = readers.get(k)
            if rs is None:
                rs = readers[k] = {}
            rs[st] = o
        lst.append(o)
        self.q[eng].append(o)
        return o

    def barrier(self, eng, ops):
        o = Op()
        o.eng = eng
        o.emit = None
        o.signal = False
        o.sem = None
        o.val = 0
        o.epoch = 0
        o.stream = eng
        o.inc = 1
        o.idx = 0
        for d in ops:
            d.signal = True
        o.deps = list(ops)
        self.q[eng].append(o)
        return o

    def finalize(self):
        for st, lst in self.streams.items():
            cnt = 0
            epoch = 0
            sem = None
            for o in lst:
                if not o.signal:
                    continue
                if sem is None or cnt + o.inc > SEM_LIMIT:
                    sem = self.sem_pool.pop()
                    cnt = 0
                    epoch += 1
                cnt += o.inc
                o.sem = sem
                o.val = cnt
                o.epoch = epoch

    def replay(self, eng, e):
        waited = {}
        for o in self.q[eng]:
            best = {}
            for d in o.deps:
                cur = best.get(d.stream)
                if cur is None or (cur.epoch, cur.val) < (d.epoch, d.val):
                    best[d.stream] = d
            for d in best.values():
                cur = waited.get(d.stream)
                key = (d.epoch, d.val)
                if cur is not None and cur >= key:
                    continue
                e.wait_ge(d.sem, d.val)
                waited[d.stream] = key
            if o.emit is not None:
                inst = o.emit(e)
                if o.signal:
                    inst.then_inc(o.sem, o.inc)


class Buf:
    def __init__(self, arena, off, n, dt):
        self.esz = 4 if dt is F32 else 2
        self.off = off
        self.n = n
        self.dt = dt
        nb = n * self.esz
        assert off % 4 == 0 and nb % 4 == 0
        ap = arena[:, off // 4:(off + nb) // 4]
        if dt is not F32:
            ap = ap.bitcast(dt)
        self.ap = ap

    def _keys(self, a, b):
        lo = self.off + a * self.esz
        hi = self.off + b * self.esz
        return tuple(range(lo // G, (hi - 1) // G + 1))

    def cols(self, a, b):
        return View(self.ap[:, a:b], self._keys(a, b))

    def sub(self, i, m):
        return self.cols(i * m, (i + 1) * m)

    def all(self):
        return View(self.ap, self._keys(0, self.n))

    def all3(self, m):
        return View(self.ap.rearrange("p (n m) -> p n m", m=m), self._keys(0, self.n))

    def range3(self, i0, i1, m):
        return View(self.ap[:, i0 * m:i1 * m].rearrange("p (n m) -> p n m", m=m), self._keys(i0 * m, i1 * m))


def build_nc(preconv=True, dbg=None):
    dbg = dbg or {}
    nc = bass.Bass("TRN2", target_bir_lowering=False)

    def din(name, shape, dt=F32):
        return nc.dram_tensor(name, shape, dt, kind="ExternalInput").ap()

    xT = din("xT", [KC, 128, TOK])
    gains_d = din("gains", [128, 4 * KC])
    small_d = din("small", [128, 16])
    wgu_d = [din("wgu1", [JC, 128, 2 * KC * 128]), din("wgu2", [JC, 128, 2 * KC * 128])]
    wd_d = [din("wd1", [KC, 128, JC * 128]), din("wd2", [KC, 128, JC * 128])]
    winc_d = din("winc", [36, 128, KC * 128])
    wout_d = din("wout", [KC, 128, KC * 128])
    rope_d = din("rope", [4, 128, TOK])
    perm_d = din("perm", [128, 3 * 128])
    outT = nc.dram_tensor("outT", [KC, 128, TOK], F32, kind="ExternalOutput").ap()

    skind = dict(kind="ExternalOutput") if dbg else {}
    X1 = nc.dram_tensor("X1", [KC, 128, TOK], F32, **skind).ap()
    QS = nc.dram_tensor("QS", [16, 128, TOK], BF16, **skind).ap()
    KL = [nc.dram_tensor("KL%d" % i, [128, TOK], BF16).ap() for i in range(10)]
    VL = [nc.dram_tensor("VL%d" % i, [128, TOK], BF16).ap() for i in range(10)]
    KGt = [nc.dram_tensor("KG%d" % i, [512, TOK], BF16).ap() for i in range(10)]
    VGt = [nc.dram_tensor("VG%d" % i, [512, TOK], BF16).ap() for i in range(10)]
    AT = nc.dram_tensor("AT", [16, 128, TOK], BF16, **skind).ap()
    if preconv:
        wgu_b = [nc.dram_tensor("wgu1b", [JC, 128, 2 * KC * 128], BF16).ap(),
                 nc.dram_tensor("wgu2b", [JC, 128, 2 * KC * 128], BF16).ap()]
        wd_b = [nc.dram_tensor("wd1b", [KC, 128, JC * 128], BF16).ap(),
                nc.dram_tensor("wd2b", [KC, 128, JC * 128], BF16).ap()]
        winc_b = nc.dram_tensor("wincb", [36, 128, KC * 128], BF16).ap()
        wout_b = nc.dram_tensor("woutb", [KC, 128, KC * 128], BF16).ap()

    dkeys = {}

    def dk(*name):
        k = dkeys.get(name)
        if k is None:
            k = dkeys[name] = DR_KEY + len(dkeys)
        return k

    with ExitStack() as es:
        arena = es.enter_context(nc.sbuf_tensor("arena", [128, ARENA_BYTES // 4], F32))
        ps = es.enter_context(nc.psum_tensor("ps", [128, 8, 512], F32))
        sems = [es.enter_context(nc.semaphore("s%d" % i)) for i in range(90)]
        block = es.enter_context(nc.Block())
        S = Sched(sems)
        K = 1024

        def psv(b):
            return View(ps[:, b, :], (PS_KEY + b,))

        def psbf(b, a, c):
            return View(ps[:, b, :].bitcast(BF16)[:, a:c], (PS_KEY + b,))

        def psc(b, a, c):
            return View(ps[:, b, a:c], (PS_KEY + b,))

        cbase = 180 * K
        gains = Buf(arena, cbase, 64, F32)
        small = Buf(arena, cbase + 256, 16, F32)
        misc = Buf(arena, cbase + 512, 16, F32)
        ones_f = Buf(arena, cbase + 768, 128, F32)
        perm_bf = Buf(arena, cbase + 768 + 512, 3 * 128, BF16)
        ones_bf = Buf(arena, cbase + 768 + 512 + 768, 128, BF16)

        xts = [Buf(arena, 0, KC * T, F32), Buf(arena, 145 * K, KC * T, F32)]
        xt_cur = [xts[0]]
        hT = Buf(arena, 32 * K, KC * T, BF16)
        actT = Buf(arena, 48 * K, JC * T, BF16)
        wgu = [Buf(arena, 92 * K + i * 8 * K, 2 * KC * 128, BF16) for i in range(3)]
        wdh = [Buf(arena, 116 * K + i * 5632, 22 * 128, BF16) for i in range(3)]
        mb = 133 * K
        sqb = [Buf(arena, mb + i * 2 * K, T, F32) for i in range(2)]
        rstd = [Buf(arena, mb + 4 * K + i * 2 * K, T, F32) for i in range(2)]
        sg = [Buf(arena, mb + 8 * K + i * 2 * K, T, F32) for i in range(2)]
        pb0 = 48 * K
        wqk = [Buf(arena, pb0 + i * 4 * K, KC * 128, BF16) for i in range(3)]
        ropeb = Buf(arena, pb0 + 12 * K, 4 * T, F32)
        t1b = [Buf(arena, pb0 + 20 * K + i * 2 * K, T, F32) for i in range(2)]
        t2b = [Buf(arena, pb0 + 24 * K + i * 2 * K, T, F32) for i in range(2)]
        qnb = [Buf(arena, pb0 + 28 * K + i * K, T, BF16) for i in range(2)]
        resb = [Buf(arena, pb0 + 30 * K + i * K, T, BF16) for i in range(3)]
        vTb = [Buf(arena, pb0 + 33 * K + i * K, T, BF16) for i in range(2)]

        R = [Buf(arena, i * 32 * K, SEQ, BF16) for i in range(4)]
        p2 = 128 * K
        Qt = [Buf(arena, p2 + i * K, T, BF16) for i in range(4)]
        NPB = 6
        Pb = [Buf(arena, p2 + 4 * K + i * 2 * K, 2 * T, BF16) for i in range(NPB)]
        q2 = p2 + 16 * K
        o1 = Buf(arena, q2, 2 * T, F32)
        tmpb = Buf(arena, q2 + 4 * K, 2 * T, F32)
        sqd = Buf(arena, q2 + 8 * K, 2 * T, F32)
        sums = [Buf(arena, q2 + 12 * K + i * 2 * K, T, F32) for i in range(2)]
        rs3 = Buf(arena, q2 + 16 * K, T, F32)
        ost = [Buf(arena, q2 + 18 * K + i * K, T, BF16) for i in range(3)]
        accD = [Buf(arena, q2 + 21 * K + i * 2 * K, T, F32) for i in range(2)]
        accP = [Buf(arena, q2 + 25 * K + i * 2 * K, T, F32) for i in range(2)]
        assert q2 + 29 * K <= 180 * K

        def mm(out, l, r, start, stop):
            return S.op("pe", lambda e, o=out.ap, a=l.ap, b=r.ap: e.matmul(o, a, b, start=start, stop=stop),
                        reads=l.keys + r.keys, writes=out.keys)

        def tr(out, in_, ident):
            return S.op("pe", lambda e, o=out.ap, a=in_.ap, b=ident.ap: e.transpose(o, a, b),
                        reads=in_.keys + ident.keys, writes=out.keys)

        def act(out, in_, func, scale=None, bias=None):
            kw = {}
            rk = in_.keys
            if scale is not None:
                kw["scale"] = scale
            if bias is not None:
                kw["bias"] = bias.ap
                rk = rk + bias.keys
            return S.op("act", lambda e, o=out.ap, i=in_.ap: e.activation(out=o, in_=i, func=func, **kw),
                        reads=rk, writes=out.keys)

        def stt(out, in0, scalar, in1, op0, op1):
            rk = in0.keys + in1.keys
            sc = scalar
            if isinstance(scalar, View):
                rk = rk + scalar.keys
                sc = scalar.ap
            return S.op("dve", lambda e, o=out.ap, a=in0.ap, b=in1.ap: e.scalar_tensor_tensor(
                out=o, in0=a, scalar=sc, in1=b, op0=op0, op1=op1), reads=rk, writes=out.keys)

        def tt(out, in0, in1, op, eng="dve", nosame=False):
            return S.op(eng, lambda e, o=out.ap, a=in0.ap, b=in1.ap: e.tensor_tensor(out=o, in0=a, in1=b, op=op),
                        reads=in0.keys + in1.keys, writes=out.keys, nosame=nosame)

        def tsc(out, in0, s1, op0):
            return S.op("dve", lambda e, o=out.ap, a=in0.ap: e.tensor_scalar(
                out=o, in0=a, scalar1=s1, scalar2=None, op0=op0), reads=in0.keys, writes=out.keys)

        def cp(out, in_, eng="dve", nosame=False):
            if eng == "act":
                return act(out, in_, AF.Copy)
            return S.op(eng, lambda e, o=out.ap, i=in_.ap: e.tensor_copy(out=o, in_=i),
                        reads=in_.keys, writes=out.keys, nosame=nosame)

        def recip(out, in_):
            return S.op("dve", lambda e, o=out.ap, i=in_.ap: e.reciprocal(out=o, in_=i),
                        reads=in_.keys, writes=out.keys)

        def memset(v, val, eng="dve"):
            return S.op(eng, lambda e, a=v.ap: e.memset(a, val), writes=v.keys)

        def dma(eng, out_ap, in_ap, reads, writes, stream):
            return S.op(eng, lambda e, o=out_ap, i=in_ap: e.dma_start(out=o, in_=i),
                        reads=reads, writes=writes, dma=stream)

        cv_i = [0]

        def conv(dst_ap, src_ap, key, after=()):
            st = "cv%d" % (cv_i[0] % 6)
            cv_i[0] += 1
            dma("pool", dst_ap, src_ap, after, (key,), st)

        def wload(dst, src_b, src_d, key, stream):
            if preconv:
                dma("sp", dst.ap, src_b, (key,), dst.keys, stream)
            else:
                dma("pool", dst.ap, src_d, (), dst.keys, stream)

        dma("sp", gains.ap, gains_d, (), gains.all().keys, "ldc0")
        dma("sp", small.ap, small_d, (), small.all().keys, "ldc1")
        dma("pool", perm_bf.ap, perm_d, (), perm_bf.all().keys, "ldc2")
        memset(ones_f.all(), 1.0)
        memset(ones_bf.all(), 1.0)
        memset(misc.all(), 0.0)
        memset(misc.cols(0, 1), EPS)
        eps_c = misc.cols(0, 1)
        permA = perm_bf.cols(0, 128)
        permB = perm_bf.cols(128, 256)
        ident = perm_bf.cols(256, 384)
        tt(misc.cols(1, 2), small.cols(4, 5), small.cols(5, 6), ALU.mult)
        tt(misc.cols(2, 3), small.cols(6, 7), small.cols(7, 8), ALU.mult)
        mm(psc(7, 0, 2), ones_f.all(), misc.cols(1, 3), True, True)
        act(misc.cols(3, 5), psc(7, 0, 2), AF.Exp)
        tt(misc.cols(5, 6), misc.cols(4, 5), misc.cols(3, 4), ALU.subtract)
        tsc(misc.cols(5, 6), misc.cols(5, 6), -LAMBDA_INIT, ALU.add)
        tsc(misc.cols(6, 8), small.cols(8, 10), 1.0 - LAMBDA_INIT, ALU.mult)
        neglam = misc.cols(5, 6)

        if preconv:
            def conv_ffn(f):
                for j in range(JC):
                    conv(wgu_b[f][j], wgu_d[f][j], dk("wgu", f, j))
                for m in range(KC):
                    conv(wd_b[f][m], wd_d[f][m], dk("wd", f, m))
            conv_ffn(0)
            for c in range(36):
                conv(winc_b[c], winc_d[c], dk("winc", c))
            late_conv = [(wout_b[m], wout_d[m], dk("wout", m)) for m in range(KC)]
            late_conv += [(wgu_b[1][j], wgu_d[1][j], dk("wgu", 1, j)) for j in range(JC)]
            late_conv += [(wd_b[1][m], wd_d[1][m], dk("wd", 1, m)) for m in range(KC)]


        def rmsnorm_stats(src, nchunks, dim, psb, rbuf):
            for kc in range(nchunks):
                sq = sqb[kc % 2].all()
                act(sq, src(kc), AF.Square)
                mm(psv(psb), ones_f.all(), sq, kc == 0, kc == nchunks - 1)
            act(rbuf, psv(psb), AF.Sqrt, scale=1.0 / dim, bias=eps_c)
            recip(rbuf, rbuf)

        def norm_to_hT(gbase):
            r = rstd[0].all()
            rmsnorm_stats(lambda kc: xt_cur[0].sub(kc, T), KC, D, 6, r)
            for kc in range(KC):
                stt(hT.sub(kc, T), xt_cur[0].sub(kc, T), gains.cols(gbase + kc, gbase + kc + 1), r, ALU.mult, ALU.mult)

        def ffn(f, after_gu=None):
            def ld_gu(j):
                s = wgu[j % 3]
                wload(s.all(), wgu_b[f][j] if preconv else None, wgu_d[f][j], dk("wgu", f, j), "ldgu%d" % (j % 3))

            def ld_d(i):
                m, half = divmod(i, 2)
                s = wdh[i % 3]
                sb = wd_b[f][m][:, half * 2816:(half + 1) * 2816] if preconv else None
                sd = wd_d[f][m][:, half * 2816:(half + 1) * 2816]
                wload(s.all(), sb, sd, dk("wd", f, m), "ldd%d" % (i % 3))

            for j in range(3):
                ld_gu(j)
            for i in range(3):
                ld_d(i)
            for j in range(JC):
                s = wgu[j % 3]
                pg, pu = (0, 1) if j % 2 == 0 else (2, 3)
                for kc in range(KC):
                    mm(psv(pg), s.sub(kc, 128), hT.sub(kc, T), kc == 0, kc == KC - 1)
                for kc in range(KC):
                    mm(psv(pu), s.sub(KC + kc, 128), hT.sub(kc, T), kc == 0, kc == KC - 1)
                if j + 3 < JC:
                    ld_gu(j + 3)
                sgv = sg[j % 2].all()
                act(sgv, psv(pg), AF.Silu)
                tt(actT.sub(j, T), sgv, psv(pu), ALU.mult)
            if after_gu is not None:
                after_gu()
            for m in range(KC):
                pbk = 4 + m % 2
                for half in range(2):
                    i = 2 * m + half
                    s = wdh[i % 3]
                    for jj in range(22):
                        jc = half * 22 + jj
                        mm(psv(pbk), s.sub(jj, 128), actT.sub(jc, T), jc == 0, jc == JC - 1)
                    if i + 3 < 2 * KC:
                        ld_d(i + 3)
                stt(xt_cur[0].sub(m, T), psv(pbk), 0.5, xt_cur[0].sub(m, T), ALU.mult, ALU.add)

        def load_tile(dst, src, keys, stream):
            for hh in range(2):
                dma("sp", dst.range3(hh * 8, hh * 8 + 8, T).ap,
                    src[hh * 8:hh * 8 + 8].rearrange("k p t -> p k t"),
                    keys, dst.range3(hh * 8, hh * 8 + 8, T).keys, "%s%d" % (stream, hh))

        def store_tile(dst, src, keys, stream):
            ops = []
            for hh in range(2):
                ops.append(dma("sp", dst[hh * 8:hh * 8 + 8].rearrange("k p t -> p k t"),
                               src.range3(hh * 8, hh * 8 + 8, T).ap,
                               src.range3(hh * 8, hh * 8 + 8, T).keys, keys, "%s%d" % (stream, hh)))
            return ops

        def qk_spec(c):
            if c < 8:
                return True, True, 0, "Q", c
            if c < 10:
                return True, True, 1, "K", c - 8
            if c < 18:
                return True, False, 2, "Q", 8 + (c - 10)
            if c < 26:
                return True, False, 3, "K", 2 + (c - 18)
            return False, False, 0, "V", c - 26

        for t in range(dbg.get("nt1", NT)):
            tsl = slice(t * T, (t + 1) * T)
            xt_cur[0] = xts[t % 2]
            if t == 0:
                load_tile(xts[0], xT[:, :, tsl], (), "ldx0_")
            if t + 1 < dbg.get("nt1", NT):
                load_tile(xts[(t + 1) % 2], xT[:, :, (t + 1) * T:(t + 2) * T], (), "ldx%d_" % ((t + 1) % 2))
            norm_to_hT(0)
            ffn(0)
            store_tile(X1[:, :, tsl], xt_cur[0], (dk("X1", t),), "stx%d_" % (t % 2))
            norm_to_hT(KC)
            dma("sp", ropeb.all3(T).ap, rope_d[:, :, tsl].rearrange("f p t -> p f t"), (), ropeb.all().keys, "ldrope")

            def ld_w(c):
                wload(wqk[c % 3].all(), winc_b[c] if preconv else None, winc_d[c], dk("winc", c), "ldqk%d" % (c % 3))

            for c in range(3):
                ld_w(c)
            PBK = (0, 1, 4)

            def stage_proj(c):
                s = wqk[c % 3]
                pbk = PBK[c % 3]
                for kc in range(KC):
                    mm(psv(pbk), s.sub(kc, 128), hT.sub(kc, T), kc == 0, kc == KC - 1)
                if c + 3 < 36:
                    ld_w(c + 3)
                if qk_spec(c)[0]:
                    act(sqb[c % 2].all(), psv(pbk), AF.Square)
                else:
                    act(vTb[c % 2].all(), psv(pbk), AF.Copy)

            def stage_ss(c):
                isqk, useA, gcol, kind, chunk = qk_spec(c)
                pbk = PBK[c % 3]
                if isqk:
                    r = rstd[c % 2].all()
                    mm(psv(2), ones_f.all(), sqb[c % 2].all(), True, True)
                    act(r, psv(2), AF.Sqrt, scale=1.0 / 128, bias=eps_c)
                    recip(r, r)
                    stt(qnb[c % 2].all(), psv(pbk), small.cols(gcol, gcol + 1), r, ALU.mult, ALU.mult)
                else:
                    res = resb[c % 3].all()
                    for s4 in range(4):
                        tr(psbf(5, s4 * 128, (s4 + 1) * 128), vTb[c % 2].cols(s4 * 128, (s4 + 1) * 128), ident)
                    cp(res, psbf(5, 0, 512))
                    dma("sp", VL[chunk][:, tsl], res.ap, res.keys,
                        (dk("VL", chunk, t),), "stres%d" % (c % 3))

            def stage_rot(c):
                isqk, useA, gcol, kind, chunk = qk_spec(c)
                if not isqk:
                    return
                res = resb[c % 3].all()
                qn = qnb[c % 2].all()
                mm(psv(3), permA if useA else permB, qn, True, True)
                cosv = ropeb.sub(0 if useA else 2, T)
                sinv = ropeb.sub(1 if useA else 3, T)
                t1 = t1b[c % 2].all()
                t2 = t2b[c % 2].all()
                tt(t1, qn, cosv, ALU.mult)
                tt(t2, psv(3), sinv, ALU.mult)
                tt(res, t1, t2, ALU.add)
                if kind == "Q":
                    dma("sp", QS[chunk][:, tsl], res.ap, res.keys, (dk("QS", chunk, t),), "stres%d" % (c % 3))
                else:
                    dma("sp", KL[chunk][:, tsl], res.ap, res.keys,
                        (dk("KL", chunk, t),), "stres%d" % (c % 3))

            for c in range(36 + 2):
                if c < 36:
                    stage_proj(c)
                if 1 <= c <= 36:
                    stage_ss(c - 1)
                if c >= 2:
                    stage_rot(c - 2)

        RUN_REST = dbg.get("stop") != "p1"
        def rest():
            groups4 = [[0, 1, 2, 3], [4, 5, 6, 7]]

            def gather(src_l, dst_l, nm, ch):
                S.op("pool", lambda e, a=src_l[ch], b=dst_l[ch]: e.collective_compute(
                    "AllGather", ALU.bypass, replica_groups=groups4, ins=[a.opt()], outs=[b.opt()]),
                    reads=tuple(dk(nm + "L", ch, t) for t in range(NT)), writes=(dk(nm + "G", ch),), dma="cc", inc=1)
            for ch in (0, 1):
                gather(KL, KGt, "K", ch)
                gather(VL, VGt, "V", ch)
            for h in range(4):
                for ch in (2 + 2 * h, 3 + 2 * h):
                    gather(KL, KGt, "K", ch)
                for ch in (2 + 2 * h, 3 + 2 * h):
                    gather(VL, VGt, "V", ch)

            if dbg.get("stop") == "cc":
                S.barrier("pool", [S.streams["cc"][-1]])
                return [lst[-1] for st, lst in S.streams.items() if st not in ("pe", "act", "dve", "pool", "cc")]
            def ld_kv(ri, src, chunk, nm, quarters=(0, 1, 2, 3)):
                for r in quarters:
                    dst = R[ri].cols(r * TOK, (r + 1) * TOK)
                    dma("sp", dst.ap, src[chunk][r * 128:(r + 1) * 128, :], (dk(nm, chunk),), dst.keys,
                        "ldR%d_%d" % (ri, r))

            groups = []
            for g in range(2):
                for r in range(4):
                    for qg in range(NT):
                        groups.append(dict(kind="A", k=g, v=[2 + g], qchunk=g * 4 + r, qg=qg, first=(r == 0 and qg == 0),
                                           unit=("A", g)))
            for h in range(4):
                for qg in range(NT):
                    for c in range(2):
                        groups.append(dict(kind="B", k=c, v=[2, 3], qchunk=8 + 2 * h + c, qg=qg, h=h, c=c,
                                           first=(qg == 0 and c == 0), unit=("B", h)))
            if dbg.get("groups") == "small":
                groups = [gr for gr in groups if gr["qg"] < 2 and ((gr["kind"] == "A" and gr["qchunk"] == 0) or
                                                                   (gr["kind"] == "B" and gr["h"] == 0))]
            NG = len(groups)

            reload_at = {}
            if NG == 128:
                reload_at[31] = [(0, KGt, 2, "KG"), (2, VGt, 2, "VG")]
                reload_at[63] = [(1, KGt, 3, "KG"), (3, VGt, 3, "VG")]
                for h in range(3):
                    nk = 2 + 2 * (h + 1)
                    reload_at[64 + 16 * h + 14] = [(0, KGt, nk, "KG")]
                    reload_at[64 + 16 * h + 15] = [(1, KGt, nk + 1, "KG"), (2, VGt, nk, "VG"), (3, VGt, nk + 1, "VG")]

            def unit_loads(unit):
                if unit[0] == "A":
                    g = unit[1]
                    ld_kv(g, KGt, g, "KG")
                    ld_kv(2 + g, VGt, g, "VG")
                else:
                    h = unit[1]
                    ld_kv(0, KGt, 2 + 2 * h, "KG")
                    ld_kv(1, KGt, 2 + 2 * h + 1, "KG")
                    ld_kv(2, VGt, 2 + 2 * h, "VG")
                    ld_kv(3, VGt, 2 + 2 * h + 1, "VG")

            def ld_q(gi):
                gr = groups[gi]
                qsl = slice(gr["qg"] * T, (gr["qg"] + 1) * T)
                dma("sp", Qt[gi % 4].ap, QS[gr["qchunk"]][:, qsl], (dk("QS", gr["qchunk"], gr["qg"]),),
                    Qt[gi % 4].all().keys, "ldq%d" % (gi % 4))

            ost_i = [0]

            def store_att(chunk, qg, v):
                dma("sp", AT[chunk][:, qg * T:(qg + 1) * T], v.ap, v.keys, (dk("AT", chunk, qg),),
                    "stat%d" % (ost_i[0] % 3))

            def next_ost():
                ost_i[0] += 1
                return ost[ost_i[0] % 3].all()

            PC0 = 192

            def obanks_of(gi):
                gr = groups[gi]
                if gr["kind"] == "A":
                    return [6]
                return [4, 5]

            def rcp_via_act(dst, src_ps, scale_in, bias, pw):
                act(dst, src_ps, AF.Ln, scale=scale_in, bias=bias)
                act(dst, dst, AF.Exp, scale=pw)

            def epilogue1(gi):
                gr = groups[gi]
                obanks = obanks_of(gi)
                if gr["kind"] == "B":
                    for half in range(2):
                        cp(tmpb.sub(half, T), psv(obanks[half]))
                else:
                    cp(tmpb.sub(gi % 2, T), psv(obanks[0]))

            def epilogue(gi):
                gr = groups[gi]
                a = gi % 2
                obanks = obanks_of(gi)
                tt(accD[a].cols(PC0, T), accD[a].cols(PC0, T), accP[a].cols(PC0, T), ALU.add)
                isA = gr["kind"] == "A"
                tb = 7 if isA else 6
                mm(psv(tb), ones_f.all(), accD[a].all(), not isA, True)
                sm = sums[a].all()
                rcp_via_act(sm, psv(tb), 1.0, None, -1.0)
                if gr["kind"] == "A":
                    o = next_ost()
                    tt(o, tmpb.sub(gi % 2, T), sm, ALU.mult)
                    store_att(gr["qchunk"], gr["qg"], o)
                elif gr["c"] == 0:
                    for half in range(2):
                        tt(o1.sub(half, T), tmpb.sub(half, T), sm, ALU.mult)
                else:
                    for half in range(2):
                        stt(tmpb.sub(half, T), tmpb.sub(half, T), neglam, sm, ALU.mult, ALU.mult)
                        tt(o1.sub(half, T), o1.sub(half, T), tmpb.sub(half, T), ALU.add)
                    for half in range(2):
                        act(sqd.sub(half, T), o1.sub(half, T), AF.Square)
                        mm(psv(7), ones_f.all(), sqd.sub(half, T), half == 0, half == 1)
                    rcp_via_act(rs3.all(), psv(7), 1.0 / 256, eps_c, -0.5)
                    for half in range(2):
                        o = next_ost()
                        stt(o, o1.sub(half, T), misc.cols(6 + half, 7 + half), rs3.all(), ALU.mult, ALU.mult)
                        store_att(8 + 2 * gr["h"] + half, gr["qg"], o)

            unit_loads(("A", 0))
            unit_loads(("A", 1))
            ld_q(0)
            ld_q(1)
            NKS = SEQ // 256
            steps = [(gi, s) for gi in range(NG) for s in range(NKS)]

            SUMP = 4

            def pe_sum_step(gi, s):
                return groups[gi]["kind"] == "A" and s % SUMP == SUMP - 1

            def emit_qk(i):
                gi, s = steps[i]
                gr = groups[gi]
                if s == 0 and gi + 2 < NG:
                    ld_q(gi + 2)
                if preconv and late_conv and i >= 64 and i % 24 == 0:
                    d_, s_, k_ = late_conv.pop(0)
                    conv(d_, s_, k_)
                p = i % 3 if gr["kind"] == "A" else i % 2
                for u in range(2):
                    kt = 2 * s + u
                    mm(psv(2 * p + u), R[gr["k"]].cols(kt * 128, (kt + 1) * 128), Qt[gi % 4].all(), True, True)
                Pv = Pb[i % NPB]
                S.op("act", lambda e, o=Pv.all3(T).ap, i_=ps[:, 2 * p:2 * p + 2, :]: e.activation(
                    out=o, in_=i_, func=AF.Exp, scale=SCALE),
                    reads=(PS_KEY + 2 * p, PS_KEY + 2 * p + 1), writes=Pv.all().keys)
                a = gi % 2
                if pe_sum_step(gi, s):
                    return
                if s == 0:
                    cp(accD[a].all(), Pv.sub(0, T), nosame=True)
                    cp(accP[a].cols(PC0, T), Pv.cols(T + PC0, 2 * T), eng="pool", nosame=True)
                else:
                    tt(accD[a].all(), accD[a].all(), Pv.sub(0, T), ALU.add, nosame=True)
                    tt(accP[a].cols(PC0, T), accP[a].cols(PC0, T), Pv.cols(T + PC0, 2 * T), ALU.add, eng="pool", nosame=True)
                tt(accD[a].cols(0, PC0), accD[a].cols(0, PC0), Pv.cols(T, T + PC0), ALU.add, nosame=True)

            def emit_pv(i):
                gi, s = steps[i]
                gr = groups[gi]
                obanks = obanks_of(gi)
                Pv = Pb[i % NPB]
                for u in range(2):
                    kt = 2 * s + u
                    for hf, ob in enumerate(obanks):
                        mm(psv(ob), R[gr["v"][hf]].cols(kt * 128, (kt + 1) * 128), Pv.sub(u, T),
                           kt == 0, kt == 2 * NKS - 1)
                    if pe_sum_step(gi, s):
                        mm(psv(7), ones_bf.all(), Pv.sub(u, T), s == SUMP - 1 and u == 0, False)
                if gi in reload_at and (s + 1) % 16 == 0:
                    for (ri, src_l, chunk, nm) in reload_at[gi]:
                        ld_kv(ri, src_l, chunk, nm, quarters=((s + 1) // 16 - 1,))
                if s == NKS - 1:
                    epilogue1(gi)
                    pending.append((i + 1 + (2 if gr["kind"] == "A" else EPI_DELAY), gi))

            NTL = len(steps)
            pending = []
            EPI_DELAY = 4
            next_pv = [0]

            def lag(j):
                return 2 if groups[steps[j][0]]["kind"] == "A" else 1

            for i in range(NTL + 3):
                while pending and pending[0][0] <= i:
                    epilogue(pending.pop(0)[1])
                if i < NTL:
                    gi, s = steps[i]
                    gr = groups[gi]
                    if s == 0 and gr["first"] and gr["kind"] == "B" and not reload_at:
                        while next_pv[0] < i:
                            emit_pv(next_pv[0])
                            next_pv[0] += 1
                        unit_loads(gr["unit"])
                    emit_qk(i)
                while next_pv[0] < NTL and next_pv[0] < i + 1 and next_pv[0] + lag(next_pv[0]) <= i:
                    emit_pv(next_pv[0])
                    next_pv[0] += 1
            assert next_pv[0] == NTL
            while pending:
                epilogue(pending.pop(0)[1])

            out_stores = []
            NT3 = 0 if dbg.get("stop") == "p2" else NT

            def load_att(t):
                atkeys = tuple(dk("AT", ch, t) for ch in range(16))
                load_tile(hT, AT[:, :, t * T:(t + 1) * T], atkeys, "ldat")

            def load_x1(t):
                load_tile(xts[t % 2], X1[:, :, t * T:(t + 1) * T], (dk("X1", t),), "ldx%d_" % (t % 2))

            def wout_stage(t):
                xb = xts[t % 2]

                def ld_wo(m):
                    wload(wqk[m % 3].all(), wout_b[m] if preconv else None, wout_d[m], dk("wout", m), "ldqk%d" % (m % 3))

                for m in range(3):
                    ld_wo(m)
                for m in range(KC):
                    s = wqk[m % 3]
                    for kc in range(KC):
                        mm(psv(m % 2), s.sub(kc, 128), hT.sub(kc, T), kc == 0, kc == KC - 1)
                    if m + 3 < KC:
                        ld_wo(m + 3)
                    tt(xb.sub(m, T), psv(m % 2), xb.sub(m, T), ALU.add)

            if NT3:
                load_att(0)
                load_x1(0)
                wout_stage(0)
            for t in range(NT3):
                tsl = slice(t * T, (t + 1) * T)
                xt_cur[0] = xts[t % 2]
                if t + 1 < NT:
                    load_x1(t + 1)
                norm_to_hT(2 * KC)
                ffn(1, after_gu=(lambda tt_=t: load_att(tt_ + 1)) if t + 1 < NT else None)
                if t + 1 < NT:
                    wout_stage(t + 1)
                r = rstd[0].all()
                rmsnorm_stats(lambda kc: xt_cur[0].sub(kc, T), KC, D, 6, r)
                for kc in range(KC):
                    stt(xt_cur[0].sub(kc, T), xt_cur[0].sub(kc, T), gains.cols(3 * KC + kc, 3 * KC + kc + 1), r, ALU.mult, ALU.mult)
                out_stores += store_tile(outT[:, :, tsl], xt_cur[0], (dk("OUT", t),), "sto%d_" % (t % 2))

            if dbg.get("stop") == "p2":
                return [lst[-1] for st, lst in S.streams.items() if st not in ("pe", "act", "dve", "pool")]
            return out_stores

        if RUN_REST:
            out_stores = rest()
        else:
            out_stores = [lst[-1] for st, lst in S.streams.items() if st not in ("pe", "act", "dve", "pool")]
        S.barrier("sp", out_stores)
        S.finalize()

        @block.tensor
        def _(e):
            S.replay("pe", e)

        @block.scalar
        def _(e):
            S.replay("act", e)

        @block.vector
        def _(e):
            S.replay("dve", e)

        @block.gpsimd
        def _(e):
            S.replay("pool", e)

        @block.sync
        def _(e):
            S.replay("sp", e)

    return nc


def _rope_tables(qtr):
    s = (np.arange(TOK, dtype=np.float32) + np.float32(qtr * TOK))
    row = np.floor(s / 64).astype(np.float32)
    col = (s - row * 64).astype(np.float32)
    fA = (np.float32(10000.0) ** (-np.arange(0, 64, 2, dtype=np.float32) / np.float32(64))).astype(np.float32)
    fB = (np.float32(500000.0) ** (-np.arange(0, 32, 2, dtype=np.float32) / np.float32(32))).astype(np.float32)
    angr = (row[None, :] * fA[:, None]).astype(np.float32)
    angc = (col[None, :] * fA[:, None]).astype(np.float32)
    angl = (s[None, :] * fB[:, None]).astype(np.float32)
    cosA = np.concatenate([np.cos(angr), np.cos(angr), np.cos(angc), np.cos(angc)], 0)
    sinA = np.concatenate([-np.sin(angr), np.sin(angr), -np.sin(angc), np.sin(angc)], 0)
    cosB = np.ones((128, TOK), np.float32)
    sinB = np.zeros((128, TOK), np.float32)
    cosB[0:16] = np.cos(angl)
    cosB[16:32] = np.cos(angl)
    sinB[0:16] = -np.sin(angl)
    sinB[16:32] = np.sin(angl)
    return np.ascontiguousarray(np.stack([cosA, sinA, cosB, sinB], 0).astype(np.float32))


def _perm_consts():
    PA = np.zeros((128, 128), np.float32)
    PB = np.zeros((128, 128), np.float32)
    for m in range(128):
        blk = (m // 64) * 64
        r = m - blk
        PA[blk + (r + 32) % 64, m] = 1.0
    for m in range(32):
        PB[(m + 16) % 32, m] = 1.0
    return np.ascontiguousarray(np.concatenate([PA, PB, np.eye(128, dtype=np.float32)], 1))


_NC_CACHE = {}


def kernel(x, ffn1_norm, ffn1_w_gu, ffn1_w_down, mix_norm, w_in,
           a_q_norm, a_k_norm, b_q_norm, b_k_norm,
           b_lambda_q1, b_lambda_k1, b_lambda_q2, b_lambda_k2, b_subln,
           w_out, ffn2_norm, ffn2_w_gu, ffn2_w_down, out_norm):
    f32 = np.float32
    x = np.asarray(x, f32)

    def lay_gu(w):
        w = np.asarray(w, f32)[0].reshape(KC, 128, 2, JC, 128)
        return np.ascontiguousarray(w.transpose(3, 1, 2, 0, 4)).reshape(JC, 128, 2 * KC * 128)

    def lay_d(w):
        w = np.asarray(w, f32)[0].reshape(JC, 128, KC, 128)
        return np.ascontiguousarray(w.transpose(2, 1, 0, 3)).reshape(KC, 128, JC * 128)

    def lay_sq(w):
        w = np.asarray(w, f32).reshape(KC, 128, KC, 128)
        return np.ascontiguousarray(w.transpose(2, 1, 0, 3)).reshape(KC, 128, KC * 128)

    win = np.asarray(w_in, f32)[0]
    col0 = ([c * 128 for c in range(8)] + [1024 + g * 128 for g in range(2)]
            + [1536 + i * 128 for i in range(8)] + [2560 + i * 128 for i in range(8)]
            + [1280 + g * 128 for g in range(2)] + [3584 + i * 128 for i in range(8)])
    winc = np.stack([win[:, c0:c0 + 128].reshape(KC, 128, 128).transpose(1, 0, 2).reshape(128, KC * 128)
                     for c0 in col0], 0)
    winc = np.ascontiguousarray(winc)

    def gl(v):
        return np.asarray(v, f32).reshape(KC, 128).T

    gains = np.ascontiguousarray(np.concatenate([gl(ffn1_norm), gl(mix_norm), gl(ffn2_norm), gl(out_norm)], 1))
    small = np.zeros((128, 16), f32)
    for i, v in enumerate([a_q_norm, a_k_norm, b_q_norm, b_k_norm, b_lambda_q1, b_lambda_k1, b_lambda_q2, b_lambda_k2]):
        small[:, i] = np.asarray(v, f32).reshape(128)
    sl = np.asarray(b_subln, f32).reshape(2, 128)
    small[:, 8] = sl[0]
    small[:, 9] = sl[1]

    common = dict(gains=gains, small=small, wgu1=lay_gu(ffn1_w_gu), wgu2=lay_gu(ffn2_w_gu),
                  wd1=lay_d(ffn1_w_down), wd2=lay_d(ffn2_w_down), winc=winc,
                  wout=lay_sq(np.asarray(w_out, f32)[0]), perm=_perm_consts())
    ropes = [_rope_tables(q) for q in range(4)]
    in_maps = []
    for c in range(8):
        b, q = divmod(c, 4)
        xs = x[b, q * TOK:(q + 1) * TOK, :]
        m = dict(common)
        m["xT"] = np.ascontiguousarray(xs.T).reshape(KC, 128, TOK)
        m["rope"] = ropes[q]
        in_maps.append(m)

    if "nc" not in _NC_CACHE:
        _NC_CACHE["nc"] = build_nc()
    nc = _NC_CACHE["nc"]
    res = run_bass_kernel_spmd(nc, in_maps, core_ids=list(range(8)))
    out = np.empty((2, SEQ, D), f32)
    for c in range(8):
        b, q = divmod(c, 4)
        o = np.asarray(res.results[c]["outT"], f32).reshape(D, TOK)
        out[b, q * TOK:(q + 1) * TOK, :] = o.T
    return out
```

```python
import math
from contextlib import ExitStack

import numpy as np
import concourse.bass as bass
import concourse.mybir as mybir
from concourse.bass_utils import run_bass_kernel_spmd

F32 = mybir.dt.float32
BF16 = mybir.dt.bfloat16
AF = mybir.ActivationFunctionType
ALU = mybir.AluOpType

D = 2048
KC = 16
DFF = 5632
JC = 44
T = 512
TOK = 4096
NT = TOK // T
SEQ = 16384
EPS = 1e-6
SCALE = 128 ** -0.5
LAMBDA_INIT = 0.8 - 0.6 * math.exp(0.0)
G = 256
ARENA_BYTES = 184 * 1024
PS_KEY = 1 << 20
DR_KEY = 1 << 21
SEM_LIMIT = 30000


class View:
    __slots__ = ("ap", "keys")

    def __init__(self, ap, keys):
        self.ap = ap
        self.keys = keys


class Op:
    __slots__ = ("eng", "emit", "deps", "signal", "sem", "val", "epoch", "stream", "inc", "idx")


class Sched:
    def __init__(self, sem_pool):
        self.sem_pool = list(sem_pool)
        self.q = {e: [] for e in ("pe", "act", "dve", "pool", "sp")}
        self.streams = {}
        self.lastw = {}
        self.readers = {}
        self.nops = 0

    def op(self, eng, emit, reads=(), writes=(), dma=None, inc=None, nosame=False):
        o = Op()
        o.eng = eng
        o.emit = emit
        o.signal = False
        o.sem = None
        o.val = 0
        o.epoch = 0
        o.stream = dma if dma is not None else eng
        o.inc = inc if inc is not None else (16 if dma is not None else 1)
        deps = {}
        st = o.stream
        is_dma = dma is not None
        lastw = self.lastw
        readers = self.readers
        for k in reads:
            w = lastw.get(k)
            if w is not None and not ((st == "pe" or nosame) and w.stream == st):
                deps[id(w)] = w
        for k in writes:
            w = lastw.get(k)
            if w is not None and (is_dma or w.stream != st):
                deps[id(w)] = w
            rs = readers.get(k)
            if rs:
                for rst, r in rs.items():
                    if is_dma or rst != st:
                        deps[id(r)] = r
        lst = self.streams.setdefault(st, [])
        if is_dma and lst and st != "cc":
            p = lst[-1]
            deps[id(p)] = p
        deps.pop(id(o), None)
        best = {}
        for d in deps.values():
            cur = best.get(d.stream)
            if cur is None or cur.idx < d.idx:
                best[d.stream] = d
        for d in best.values():
            d.signal = True
        o.deps = list(best.values())
        self.nops += 1
        o.idx = self.nops
        for k in writes:
            lastw[k] = o
            readers[k] = {}
        for k in reads:
            rs = readers.get(k)
            if rs is None:
                rs = readers[k] = {}
            rs[st] = o
        lst.append(o)
        self.q[eng].append(o)
        return o

    def barrier(self, eng, ops):
        o = Op()
        o.eng = eng
        o.emit = None
        o.signal = False
        o.sem = None
        o.val = 0
        o.epoch = 0
        o.stream = eng
        o.inc = 1
        o.idx = 0
        for d in ops:
            d.signal = True
        o.deps = list(ops)
        self.q[eng].append(o)
        return o

    def finalize(self):
        for st, lst in self.streams.items():
            cnt = 0
            epoch = 0
            sem = None
            for o in lst:
                if not o.signal:
                    continue
                if sem is None or cnt + o.inc > SEM_LIMIT:
                    sem = self.sem_pool.pop()
                    cnt = 0
                    epoch += 1
                cnt += o.inc
                o.sem = sem
                o.val = cnt
                o.epoch = epoch

    def replay(self, eng, e):
        waited = {}
        for o in self.q[eng]:
            best = {}
            for d in o.deps:
                cur = best.get(d.stream)
                if cur is None or (cur.epoch, cur.val) < (d.epoch, d.val):
                    best[d.stream] = d
            for d in best.values():
                cur = waited.get(d.stream)
                key = (d.epoch, d.val)
                if cur is not None and cur >= key:
                    continue
                e.wait_ge(d.sem, d.val)
                waited[d.stream] = key
            if o.emit is not None:
                inst = o.emit(e)
                if o.signal:
                    inst.then_inc(o.sem, o.inc)


class Buf:
    def __init__(self, arena, off, n, dt):
        self.esz = 4 if dt is F32 else 2
        self.off = off
        self.n = n
        self.dt = dt
        nb = n * self.esz
        assert off % 4 == 0 and nb % 4 == 0
        ap = arena[:, off // 4:(off + nb) // 4]
        if dt is not F32:
            ap = ap.bitcast(dt)
        self.ap = ap

    def _keys(self, a, b):
        lo = self.off + a * self.esz
        hi = self.off + b * self.esz
        return tuple(range(lo // G, (hi - 1) // G + 1))

    def cols(self, a, b):
        return View(self.ap[:, a:b], self._keys(a, b))

    def sub(self, i, m):
        return self.cols(i * m, (i + 1) * m)

    def all(self):
        return View(self.ap, self._keys(0, self.n))

    def all3(self, m):
        return View(self.ap.rearrange("p (n m) -> p n m", m=m), self._keys(0, self.n))

    def range3(self, i0, i1, m):
        return View(self.ap[:, i0 * m:i1 * m].rearrange("p (n m) -> p n m", m=m), self._keys(i0 * m, i1 * m))


def build_nc(preconv=True, dbg=None):
    dbg = dbg or {}
    nc = bass.Bass("TRN2", target_bir_lowering=False)

    def din(name, shape, dt=F32):
        return nc.dram_tensor(name, shape, dt, kind="ExternalInput").ap()

    xT = din("xT", [KC, 128, TOK])
    gains_d = din("gains", [128, 4 * KC])
    small_d = din("small", [128, 16])
    wgu_d = [din("wgu1", [JC, 128, 2 * KC * 128]), din("wgu2", [JC, 128, 2 * KC * 128])]
    wd_d = [din("wd1", [KC, 128, JC * 128]), din("wd2", [KC, 128, JC * 128])]
    winc_d = din("winc", [36, 128, KC * 128])
    wout_d = din("wout", [KC, 128, KC * 128])
    rope_d = din("rope", [4, 128, TOK])
    perm_d = din("perm", [128, 3 * 128])
    outT = nc.dram_tensor("outT", [KC, 128, TOK], F32, kind="ExternalOutput").ap()

    skind = dict(kind="ExternalOutput") if dbg else {}
    X1 = nc.dram_tensor("X1", [KC, 128, TOK], F32, **skind).ap()
    QS = nc.dram_tensor("QS", [16, 128, TOK], BF16, **skind).ap()
    KL = [nc.dram_tensor("KL%d" % i, [128, TOK], BF16).ap() for i in range(10)]
    VL = [nc.dram_tensor("VL%d" % i, [128, TOK], BF16).ap() for i in range(10)]
    KGt = [nc.dram_tensor("KG%d" % i, [512, TOK], BF16).ap() for i in range(10)]
    VGt = [nc.dram_tensor("VG%d" % i, [512, TOK], BF16).ap() for i in range(10)]
    AT = nc.dram_tensor("AT", [16, 128, TOK], BF16, **skind).ap()
    if preconv:
        wgu_b = [nc.dram_tensor("wgu1b", [JC, 128, 2 * KC * 128], BF16).ap(),
                 nc.dram_tensor("wgu2b", [JC, 128, 2 * KC * 128], BF16).ap()]
        wd_b = [nc.dram_tensor("wd1b", [KC, 128, JC * 128], BF16).ap(),
                nc.dram_tensor("wd2b", [KC, 128, JC * 128], BF16).ap()]
        winc_b = nc.dram_tensor("wincb", [36, 128, KC * 128], BF16).ap()
        wout_b = nc.dram_tensor("woutb", [KC, 128, KC * 128], BF16).ap()

    dkeys = {}

    def dk(*name):
        k = dkeys.get(name)
        if k is None:
            k = dkeys[name] = DR_KEY + len(dkeys)
        return k

    with ExitStack() as es:
        arena = es.enter_context(nc.sbuf_tensor("arena", [128, ARENA_BYTES // 4], F32))
        ps = es.enter_context(nc.psum_tensor("ps", [128, 8, 512], F32))
        sems = [es.enter_context(nc.semaphore("s%d" % i)) for i in range(90)]
        block = es.enter_context(nc.Block())
        S = Sched(sems)
        K = 1024

        def psv(b):
            return View(ps[:, b, :], (PS_KEY + b,))

        def psbf(b, a, c):
            return View(ps[:, b, :].bitcast(BF16)[:, a:c], (PS_KEY + b,))

        def psc(b, a, c):
            return View(ps[:, b, a:c], (PS_KEY + b,))

        cbase = 180 * K
        gains = Buf(arena, cbase, 64, F32)
        small = Buf(arena, cbase + 256, 16, F32)
        misc = Buf(arena, cbase + 512, 16, F32)
        ones_f = Buf(arena, cbase + 768, 128, F32)
        perm_bf = Buf(arena, cbase + 768 + 512, 3 * 128, BF16)
        ones_bf = Buf(arena, cbase + 768 + 512 + 768, 128, BF16)

        xts = [Buf(arena, 0, KC * T, F32), Buf(arena, 145 * K, KC * T, F32)]
        xt_cur = [xts[0]]
        hT = Buf(arena, 32 * K, KC * T, BF16)
        actT = Buf(arena, 48 * K, JC * T, BF16)
        wgu = [Buf(arena, 92 * K + i * 8 * K, 2 * KC * 128, BF16) for i in range(3)]
        wdh = [Buf(arena, 116 * K + i * 5632, 22 * 128, BF16) for i in range(3)]
        mb = 133 * K
        sqb = [Buf(arena, mb + i * 2 * K, T, F32) for i in range(2)]
        rstd = [Buf(arena, mb + 4 * K + i * 2 * K, T, F32) for i in range(2)]
        sg = [Buf(arena, mb + 8 * K + i * 2 * K, T, F32) for i in range(2)]
        pb0 = 48 * K
        wqk = [Buf(arena, pb0 + i * 4 * K, KC * 128, BF16) for i in range(3)]
        ropeb = Buf(arena, pb0 + 12 * K, 4 * T, F32)
        t1b = [Buf(arena, pb0 + 20 * K + i * 2 * K, T, F32) for i in range(2)]
        t2b = [Buf(arena, pb0 + 24 * K + i * 2 * K, T, F32) for i in range(2)]
        qnb = [Buf(arena, pb0 + 28 * K + i * K, T, BF16) for i in range(2)]
        resb = [Buf(arena, pb0 + 30 * K + i * K, T, BF16) for i in range(3)]
        vTb = [Buf(arena, pb0 + 33 * K + i * K, T, BF16) for i in range(2)]

        R = [Buf(arena, i * 32 * K, SEQ, BF16) for i in range(4)]
        p2 = 128 * K
        Qt = [Buf(arena, p2 + i * K, T, BF16) for i in range(4)]
        NPB = 6
        Pb = [Buf(arena, p2 + 4 * K + i * 2 * K, 2 * T, BF16) for i in range(NPB)]
        q2 = p2 + 16 * K
        o1 = Buf(arena, q2, 2 * T, F32)
        tmpb = Buf(arena, q2 + 4 * K, 2 * T, F32)
        sqd = Buf(arena, q2 + 8 * K, 2 * T, F32)
        sums = [Buf(arena, q2 + 12 * K + i * 2 * K, T, F32) for i in range(2)]
        rs3 = Buf(arena, q2 + 16 * K, T, F32)
        ost = [Buf(arena, q2 + 18 * K + i * K, T, BF16) for i in range(3)]
        accD = [Buf(arena, q2 + 21 * K + i * 2 * K, T, F32) for i in range(2)]
        accP = [Buf(arena, q2 + 25 * K + i * 2 * K, T, F32) for i in range(2)]
        assert q2 + 29 * K <= 180 * K

        def mm(out, l, r, start, stop):
            return S.op("pe", lambda e, o=out.ap, a=l.ap, b=r.ap: e.matmul(o, a, b, start=start, stop=stop),
                        reads=l.keys + r.keys, writes=out.keys)

        def tr(out, in_, ident):
            return S.op("pe", lambda e, o=out.ap, a=in_.ap, b=ident.ap: e.transpose(o, a, b),
                        reads=in_.keys + ident.keys, writes=out.keys)

        def act(out, in_, func, scale=None, bias=None):
            kw = {}
            rk = in_.keys
            if scale is not None:
                kw["scale"] = scale
            if bias is not None:
                kw["bias"] = bias.ap
                rk = rk + bias.keys
            return S.op("act", lambda e, o=out.ap, i=in_.ap: e.activation(out=o, in_=i, func=func, **kw),
                        reads=rk, writes=out.keys)

        def stt(out, in0, scalar, in1, op0, op1):
            rk = in0.keys + in1.keys
            sc = scalar
            if isinstance(scalar, View):
                rk = rk + scalar.keys
                sc = scalar.ap
            return S.op("dve", lambda e, o=out.ap, a=in0.ap, b=in1.ap: e.scalar_tensor_tensor(
                out=o, in0=a, scalar=sc, in1=b, op0=op0, op1=op1), reads=rk, writes=out.keys)

        def tt(out, in0, in1, op, eng="dve", nosame=False):
            return S.op(eng, lambda e, o=out.ap, a=in0.ap, b=in1.ap: e.tensor_tensor(out=o, in0=a, in1=b, op=op),
                        reads=in0.keys + in1.keys, writes=out.keys, nosame=nosame)

        def tsc(out, in0, s1, op0):
            return S.op("dve", lambda e, o=out.ap, a=in0.ap: e.tensor_scalar(
                out=o, in0=a, scalar1=s1, scalar2=None, op0=op0), reads=in0.keys, writes=out.keys)

        def cp(out, in_, eng="dve", nosame=False):
            if eng == "act":
                return act(out, in_, AF.Copy)
            return S.op(eng, lambda e, o=out.ap, i=in_.ap: e.tensor_copy(out=o, in_=i),
                        reads=in_.keys, writes=out.keys, nosame=nosame)

        def recip(out, in_):
            return S.op("dve", lambda e, o=out.ap, i=in_.ap: e.reciprocal(out=o, in_=i),
                        reads=in_.keys, writes=out.keys)

        def memset(v, val, eng="dve"):
            return S.op(eng, lambda e, a=v.ap: e.memset(a, val), writes=v.keys)

        def dma(eng, out_ap, in_ap, reads, writes, stream):
            return S.op(eng, lambda e, o=out_ap, i=in_ap: e.dma_start(out=o, in_=i),
                        reads=reads, writes=writes, dma=stream)

        cv_i = [0]

        def conv(dst_ap, src_ap, key, after=()):
            st = "cv%d" % (cv_i[0] % 12)
            cv_i[0] += 1
            dma("pool", dst_ap, src_ap, after, (key,), st)

        def wload(dst, src_b, src_d, key, stream):
            if preconv:
                dma("sp", dst.ap, src_b, (key,), dst.keys, stream)
            else:
                dma("pool", dst.ap, src_d, (), dst.keys, stream)

        dma("sp", gains.ap, gains_d, (), gains.all().keys, "ldc0")
        dma("sp", small.ap, small_d, (), small.all().keys, "ldc1")
        dma("pool", perm_bf.ap, perm_d, (), perm_bf.all().keys, "ldc2")
        memset(ones_f.all(), 1.0)
        memset(ones_bf.all(), 1.0)
        memset(misc.all(), 0.0)
        memset(misc.cols(0, 1), EPS)
        eps_c = misc.cols(0, 1)
        permA = perm_bf.cols(0, 128)
        permB = perm_bf.cols(128, 256)
        ident = perm_bf.cols(256, 384)
        tt(misc.cols(1, 2), small.cols(4, 5), small.cols(5, 6), ALU.mult)
        tt(misc.cols(2, 3), small.cols(6, 7), small.cols(7, 8), ALU.mult)
        mm(psc(7, 0, 2), ones_f.all(), misc.cols(1, 3), True, True)
        act(misc.cols(3, 5), psc(7, 0, 2), AF.Exp)
        tt(misc.cols(5, 6), misc.cols(4, 5), misc.cols(3, 4), ALU.subtract)
        tsc(misc.cols(5, 6), misc.cols(5, 6), -LAMBDA_INIT, ALU.add)
        tsc(misc.cols(6, 8), small.cols(8, 10), 1.0 - LAMBDA_INIT, ALU.mult)
        neglam = misc.cols(5, 6)

        if preconv:
            def conv_ffn(f):
                for j in range(JC):
                    conv(wgu_b[f][j], wgu_d[f][j], dk("wgu", f, j))
                for m in range(KC):
                    conv(wd_b[f][m], wd_d[f][m], dk("wd", f, m))
            conv_ffn(0)
            for c in range(36):
                conv(winc_b[c], winc_d[c], dk("winc", c))
            late_conv = [(wout_b[m], wout_d[m], dk("wout", m)) for m in range(KC)]
            late_conv += [(wgu_b[1][j], wgu_d[1][j], dk("wgu", 1, j)) for j in range(JC)]
            late_conv += [(wd_b[1][m], wd_d[1][m], dk("wd", 1, m)) for m in range(KC)]


        def rmsnorm_stats(src, nchunks, dim, psb, rbuf):
            for kc in range(nchunks):
                sq = sqb[kc % 2].all()
                act(sq, src(kc), AF.Square)
                mm(psv(psb), ones_f.all(), sq, kc == 0, kc == nchunks - 1)
            act(rbuf, psv(psb), AF.Sqrt, scale=1.0 / dim, bias=eps_c)
            recip(rbuf, rbuf)

        def norm_to_hT(gbase):
            r = rstd[0].all()
            rmsnorm_stats(lambda kc: xt_cur[0].sub(kc, T), KC, D, 6, r)
            for kc in range(KC):
                stt(hT.sub(kc, T), xt_cur[0].sub(kc, T), gains.cols(gbase + kc, gbase + kc + 1), r, ALU.mult, ALU.mult)

        def ffn(f, after_gu=None):
            def ld_gu(j):
                s = wgu[j % 3]
                wload(s.all(), wgu_b[f][j] if preconv else None, wgu_d[f][j], dk("wgu", f, j), "ldgu%d" % (j % 3))

            def ld_d(i):
                m, half = divmod(i, 2)
                s = wdh[i % 3]
                sb = wd_b[f][m][:, half * 2816:(half + 1) * 2816] if preconv else None
                sd = wd_d[f][m][:, half * 2816:(half + 1) * 2816]
                wload(s.all(), sb, sd, dk("wd", f, m), "ldd%d" % (i % 3))

            for j in range(3):
                ld_gu(j)
            for i in range(3):
                ld_d(i)
            for j in range(JC):
                s = wgu[j % 3]
                pg, pu = (0, 1) if j % 2 == 0 else (2, 3)
                for kc in range(KC):
                    mm(psv(pg), s.sub(kc, 128), hT.sub(kc, T), kc == 0, kc == KC - 1)
                for kc in range(KC):
                    mm(psv(pu), s.sub(KC + kc, 128), hT.sub(kc, T), kc == 0, kc == KC - 1)
                if j + 3 < JC:
                    ld_gu(j + 3)
                sgv = sg[j % 2].all()
                act(sgv, psv(pg), AF.Silu)
                tt(actT.sub(j, T), sgv, psv(pu), ALU.mult)
            if after_gu is not None:
                after_gu()
            for m in range(KC):
                pbk = 4 + m % 2
                for half in range(2):
                    i = 2 * m + half
                    s = wdh[i % 3]
                    for jj in range(22):
                        jc = half * 22 + jj
                        mm(psv(pbk), s.sub(jj, 128), actT.sub(jc, T), jc == 0, jc == JC - 1)
                    if i + 3 < 2 * KC:
                        ld_d(i + 3)
                stt(xt_cur[0].sub(m, T), psv(pbk), 0.5, xt_cur[0].sub(m, T), ALU.mult, ALU.add)

        def load_tile(dst, src, keys, stream):
            for hh in range(2):
                dma("sp", dst.range3(hh * 8, hh * 8 + 8, T).ap,
                    src[hh * 8:hh * 8 + 8].rearrange("k p t -> p k t"),
                    keys, dst.range3(hh * 8, hh * 8 + 8, T).keys, "%s%d" % (stream, hh))

        def store_tile(dst, src, keys, stream):
            ops = []
            for hh in range(2):
                ops.append(dma("sp", dst[hh * 8:hh * 8 + 8].rearrange("k p t -> p k t"),
                               src.range3(hh * 8, hh * 8 + 8, T).ap,
                               src.range3(hh * 8, hh * 8 + 8, T).keys, keys, "%s%d" % (stream, hh)))
            return ops

        def qk_spec(c):
            if c < 8:
                return True, True, 0, "Q", c
            if c < 10:
                return True, True, 1, "K", c - 8
            if c < 18:
                return True, False, 2, "Q", 8 + (c - 10)
            if c < 26:
                return True, False, 3, "K", 2 + (c - 18)
            return False, False, 0, "V", c - 26

        for t in range(dbg.get("nt1", NT)):
            tsl = slice(t * T, (t + 1) * T)
            xt_cur[0] = xts[t % 2]
            if t == 0:
                load_tile(xts[0], xT[:, :, tsl], (), "ldx0_")
            if t + 1 < dbg.get("nt1", NT):
                load_tile(xts[(t + 1) % 2], xT[:, :, (t + 1) * T:(t + 2) * T], (), "ldx%d_" % ((t + 1) % 2))
            norm_to_hT(0)
            ffn(0)
            store_tile(X1[:, :, tsl], xt_cur[0], (dk("X1", t),), "stx%d_" % (t % 2))
            norm_to_hT(KC)
            dma("sp", ropeb.all3(T).ap, rope_d[:, :, tsl].rearrange("f p t -> p f t"), (), ropeb.all().keys, "ldrope")

            def ld_w(c):
                wload(wqk[c % 3].all(), winc_b[c] if preconv else None, winc_d[c], dk("winc", c), "ldqk%d" % (c % 3))

            for c in range(3):
                ld_w(c)
            PBK = (0, 1, 4)

            def stage_proj(c):
                s = wqk[c % 3]
                pbk = PBK[c % 3]
                for kc in range(KC):
                    mm(psv(pbk), s.sub(kc, 128), hT.sub(kc, T), kc == 0, kc == KC - 1)
                if c + 3 < 36:
                    ld_w(c + 3)
                if qk_spec(c)[0]:
                    act(sqb[c % 2].all(), psv(pbk), AF.Square)
                else:
                    act(vTb[c % 2].all(), psv(pbk), AF.Copy)

            def stage_ss(c):
                isqk, useA, gcol, kind, chunk = qk_spec(c)
                pbk = PBK[c % 3]
                if isqk:
                    r = rstd[c % 2].all()
                    mm(psv(2), ones_f.all(), sqb[c % 2].all(), True, True)
                    act(r, psv(2), AF.Sqrt, scale=1.0 / 128, bias=eps_c)
                    recip(r, r)
                    stt(qnb[c % 2].all(), psv(pbk), small.cols(gcol, gcol + 1), r, ALU.mult, ALU.mult)
                else:
                    res = resb[c % 3].all()
                    for s4 in range(4):
                        tr(psbf(5, s4 * 128, (s4 + 1) * 128), vTb[c % 2].cols(s4 * 128, (s4 + 1) * 128), ident)
                    cp(res, psbf(5, 0, 512))
                    dma("sp", VL[chunk][:, tsl], res.ap, res.keys,
                        (dk("VL", chunk, t),), "stres%d" % (c % 3))

            def stage_rot(c):
                isqk, useA, gcol, kind, chunk = qk_spec(c)
                if not isqk:
                    return
                res = resb[c % 3].all()
                qn = qnb[c % 2].all()
                mm(psv(3), permA if useA else permB, qn, True, True)
                cosv = ropeb.sub(0 if useA else 2, T)
                sinv = ropeb.sub(1 if useA else 3, T)
                t1 = t1b[c % 2].all()
                t2 = t2b[c % 2].all()
                tt(t1, qn, cosv, ALU.mult)
                tt(t2, psv(3), sinv, ALU.mult)
                tt(res, t1, t2, ALU.add)
                if kind == "Q":
                    dma("sp", QS[chunk][:, tsl], res.ap, res.keys, (dk("QS", chunk, t),), "stres%d" % (c % 3))
                else:
                    dma("sp", KL[chunk][:, tsl], res.ap, res.keys,
                        (dk("KL", chunk, t),), "stres%d" % (c % 3))

            for c in range(36 + 2):
                if c < 36:
                    stage_proj(c)
                if 1 <= c <= 36:
                    stage_ss(c - 1)
                if c >= 2:
                    stage_rot(c - 2)

        RUN_REST = dbg.get("stop") != "p1"
        def rest():
            groups4 = [[0, 1, 2, 3], [4, 5, 6, 7]]

            def gather(src_l, dst_l, nm, ch):
                S.op("pool", lambda e, a=src_l[ch], b=dst_l[ch]: e.collective_compute(
                    "AllGather", ALU.bypass, replica_groups=groups4, ins=[a.opt()], outs=[b.opt()]),
                    reads=tuple(dk(nm + "L", ch, t) for t in range(NT)), writes=(dk(nm + "G", ch),), dma="cc", inc=1)
            for ch in (0, 1):
                gather(KL, KGt, "K", ch)
                gather(VL, VGt, "V", ch)
            for h in range(4):
                for ch in (2 + 2 * h, 3 + 2 * h):
                    gather(KL, KGt, "K", ch)
                for ch in (2 + 2 * h, 3 + 2 * h):
                    gather(VL, VGt, "V", ch)

            if dbg.get("stop") == "cc":
                S.barrier("pool", [S.streams["cc"][-1]])
                return [lst[-1] for st, lst in S.streams.items() if st not in ("pe", "act", "dve", "pool", "cc")]
            def ld_kv(ri, src, chunk, nm, quarters=(0, 1, 2, 3)):
                for r in quarters:
                    dst = R[ri].cols(r * TOK, (r + 1) * TOK)
                    dma("sp", dst.ap, src[chunk][r * 128:(r + 1) * 128, :], (dk(nm, chunk),), dst.keys,
                        "ldR%d_%d" % (ri, r))

            groups = []
            for g in range(2):
                for r in range(4):
                    for qg in range(NT):
                        groups.append(dict(kind="A", k=g, v=[2 + g], qchunk=g * 4 + r, qg=qg, first=(r == 0 and qg == 0),
                                           unit=("A", g)))
            for h in range(4):
                for qg in range(NT):
                    for c in range(2):
                        groups.append(dict(kind="B", k=c, v=[2, 3], qchunk=8 + 2 * h + c, qg=qg, h=h, c=c,
                                           first=(qg == 0 and c == 0), unit=("B", h)))
            if dbg.get("groups") == "small":
                groups = [gr for gr in groups if gr["qg"] < 2 and ((gr["kind"] == "A" and gr["qchunk"] == 0) or
                                                                   (gr["kind"] == "B" and gr["h"] == 0))]
            NG = len(groups)

            reload_at = {}
            if NG == 128:
                reload_at[31] = [(0, KGt, 2, "KG"), (2, VGt, 2, "VG")]
                reload_at[63] = [(1, KGt, 3, "KG"), (3, VGt, 3, "VG")]
                for h in range(3):
                    nk = 2 + 2 * (h + 1)
                    reload_at[64 + 16 * h + 14] = [(0, KGt, nk, "KG")]
                    reload_at[64 + 16 * h + 15] = [(1, KGt, nk + 1, "KG"), (2, VGt, nk, "VG"), (3, VGt, nk + 1, "VG")]

            def unit_loads(unit):
                if unit[0] == "A":
                    g = unit[1]
                    ld_kv(g, KGt, g, "KG")
                    ld_kv(2 + g, VGt, g, "VG")
                else:
                    h = unit[1]
                    ld_kv(0, KGt, 2 + 2 * h, "KG")
                    ld_kv(1, KGt, 2 + 2 * h + 1, "KG")
                    ld_kv(2, VGt, 2 + 2 * h, "VG")
                    ld_kv(3, VGt, 2 + 2 * h + 1, "VG")

            def ld_q(gi):
                gr = groups[gi]
                qsl = slice(gr["qg"] * T, (gr["qg"] + 1) * T)
                dma("sp", Qt[gi % 4].ap, QS[gr["qchunk"]][:, qsl], (dk("QS", gr["qchunk"], gr["qg"]),),
                    Qt[gi % 4].all().keys, "ldq%d" % (gi % 4))

            ost_i = [0]

            def store_att(chunk, qg, v):
                dma("sp", AT[chunk][:, qg * T:(qg + 1) * T], v.ap, v.keys, (dk("AT", chunk, qg),),
                    "stat%d" % (ost_i[0] % 3))

            def next_ost():
                ost_i[0] += 1
                return ost[ost_i[0] % 3].all()

            PC0 = 192

            def obanks_of(gi):
                gr = groups[gi]
                if gr["kind"] == "A":
                    return [6]
                return [4, 5]

            def rcp_via_act(dst, src_ps, scale_in, bias, pw):
                act(dst, src_ps, AF.Ln, scale=scale_in, bias=bias)
                act(dst, dst, AF.Exp, scale=pw)

            def epilogue1(gi):
                gr = groups[gi]
                obanks = obanks_of(gi)
                if gr["kind"] == "B":
                    for half in range(2):
                        cp(tmpb.sub(half, T), psv(obanks[half]))
                else:
                    cp(tmpb.sub(gi % 2, T), psv(obanks[0]))

            def epilogue(gi):
                gr = groups[gi]
                a = gi % 2
                obanks = obanks_of(gi)
                tt(accD[a].cols(PC0, T), accD[a].cols(PC0, T), accP[a].cols(PC0, T), ALU.add)
                isA = gr["kind"] == "A"
                tb = 7 if isA else 6
                mm(psv(tb), ones_f.all(), accD[a].all(), not isA, True)
                sm = sums[a].all()
                rcp_via_act(sm, psv(tb), 1.0, None, -1.0)
                if gr["kind"] == "A":
                    o = next_ost()
                    tt(o, tmpb.sub(gi % 2, T), sm, ALU.mult)
                    store_att(gr["qchunk"], gr["qg"], o)
                elif gr["c"] == 0:
                    for half in range(2):
                        tt(o1.sub(half, T), tmpb.sub(half, T), sm, ALU.mult)
                else:
                    for half in range(2):
                        stt(tmpb.sub(half, T), tmpb.sub(half, T), neglam, sm, ALU.mult, ALU.mult)
                        tt(o1.sub(half, T), o1.sub(half, T), tmpb.sub(half, T), ALU.add)
                    for half in range(2):
                        act(sqd.sub(half, T), o1.sub(half, T), AF.Square)
                        mm(psv(7), ones_f.all(), sqd.sub(half, T), half == 0, half == 1)
                    rcp_via_act(rs3.all(), psv(7), 1.0 / 256, eps_c, -0.5)
                    for half in range(2):
                        o = next_ost()
                        stt(o, o1.sub(half, T), misc.cols(6 + half, 7 + half), rs3.all(), ALU.mult, ALU.mult)
                        store_att(8 + 2 * gr["h"] + half, gr["qg"], o)

            unit_loads(("A", 0))
            unit_loads(("A", 1))
            ld_q(0)
            ld_q(1)
            NKS = SEQ // 256
            steps = [(gi, s) for gi in range(NG) for s in range(NKS)]

            SUMP = 4

            def pe_sum_step(gi, s):
                return groups[gi]["kind"] == "A" and s % SUMP == SUMP - 1

            def emit_qk(i):
                gi, s = steps[i]
                gr = groups[gi]
                if s == 0 and gi + 2 < NG:
                    ld_q(gi + 2)
                if preconv and late_conv and i >= 64 and i % 24 == 0:
                    d_, s_, k_ = late_conv.pop(0)
                    conv(d_, s_, k_)
                p = i % 3 if gr["kind"] == "A" else i % 2
                for u in range(2):
                    kt = 2 * s + u
                    mm(psv(2 * p + u), R[gr["k"]].cols(kt * 128, (kt + 1) * 128), Qt[gi % 4].all(), True, True)
                Pv = Pb[i % NPB]
                S.op("act", lambda e, o=Pv.all3(T).ap, i_=ps[:, 2 * p:2 * p + 2, :]: e.activation(
                    out=o, in_=i_, func=AF.Exp, scale=SCALE),
                    reads=(PS_KEY + 2 * p, PS_KEY + 2 * p + 1), writes=Pv.all().keys)
                a = gi % 2
                if pe_sum_step(gi, s):
                    return
                if s == 0:
                    cp(accD[a].all(), Pv.sub(0, T), nosame=True)
                    cp(accP[a].cols(PC0, T), Pv.cols(T + PC0, 2 * T), eng="pool", nosame=True)
                else:
                    tt(accD[a].all(), accD[a].all(), Pv.sub(0, T), ALU.add, nosame=True)
                    tt(accP[a].cols(PC0, T), accP[a].cols(PC0, T), Pv.cols(T + PC0, 2 * T), ALU.add, eng="pool", nosame=True)
                tt(accD[a].cols(0, PC0), accD[a].cols(0, PC0), Pv.cols(T, T + PC0), ALU.add, nosame=True)

            def emit_pv(i):
                gi, s = steps[i]
                gr = groups[gi]
                obanks = obanks_of(gi)
                Pv = Pb[i % NPB]
                for u in range(2):
                    kt = 2 * s + u
                    for hf, ob in enumerate(obanks):
                        mm(psv(ob), R[gr["v"][hf]].cols(kt * 128, (kt + 1) * 128), Pv.sub(u, T),
                           kt == 0, kt == 2 * NKS - 1)
                    if pe_sum_step(gi, s):
                        mm(psv(7), ones_bf.all(), Pv.sub(u, T), s == SUMP - 1 and u == 0, False)
                if gi in reload_at and (s + 1) % 16 == 0:
                    for (ri, src_l, chunk, nm) in reload_at[gi]:
                        ld_kv(ri, src_l, chunk, nm, quarters=((s + 1) // 16 - 1,))
                if s == NKS - 1:
                    epilogue1(gi)
                    pending.append((i + 1 + (2 if gr["kind"] == "A" else EPI_DELAY), gi))

            NTL = len(steps)
            pending = []
            EPI_DELAY = 4
            next_pv = [0]

            def lag(j):
                return 2 if groups[steps[j][0]]["kind"] == "A" else 1

            for i in range(NTL + 3):
                while pending and pending[0][0] <= i:
                    epilogue(pending.pop(0)[1])
                if i < NTL:
                    gi, s = steps[i]
                    gr = groups[gi]
                    if s == 0 and gr["first"] and gr["kind"] == "B" and not reload_at:
                        while next_pv[0] < i:
                            emit_pv(next_pv[0])
                            next_pv[0] += 1
                        unit_loads(gr["unit"])
                    emit_qk(i)
                while next_pv[0] < NTL and next_pv[0] < i + 1 and next_pv[0] + lag(next_pv[0]) <= i:
                    emit_pv(next_pv[0])
                    next_pv[0] += 1
            assert next_pv[0] == NTL
            while pending:
                epilogue(pending.pop(0)[1])

            out_stores = []
            NT3 = 0 if dbg.get("stop") == "p2" else NT

            def load_att(t):
                atkeys = tuple(dk("AT", ch, t) for ch in range(16))
                load_tile(hT, AT[:, :, t * T:(t + 1) * T], atkeys, "ldat")

            def load_x1(t):
                load_tile(xts[t % 2], X1[:, :, t * T:(t + 1) * T], (dk("X1", t),), "ldx%d_" % (t % 2))

            def wout_stage(t):
                xb = xts[t % 2]

                def ld_wo(m):
                    wload(wqk[m % 3].all(), wout_b[m] if preconv else None, wout_d[m], dk("wout", m), "ldqk%d" % (m % 3))

                for m in range(3):
                    ld_wo(m)
                for m in range(KC):
                    s = wqk[m % 3]
                    for kc in range(KC):
                        mm(psv(m % 2), s.sub(kc, 128), hT.sub(kc, T), kc == 0, kc == KC - 1)
                    if m + 3 < KC:
                        ld_wo(m + 3)
                    tt(xb.sub(m, T), psv(m % 2), xb.sub(m, T), ALU.add)

            if NT3:
                load_att(0)
                load_x1(0)
                wout_stage(0)
            for t in range(NT3):
                tsl = slice(t * T, (t + 1) * T)
                xt_cur[0] = xts[t % 2]
                if t + 1 < NT:
                    load_x1(t + 1)
                norm_to_hT(2 * KC)
                ffn(1, after_gu=(lambda tt_=t: load_att(tt_ + 1)) if t + 1 < NT else None)
                if t + 1 < NT:
                    wout_stage(t + 1)
                r = rstd[0].all()
                rmsnorm_stats(lambda kc: xt_cur[0].sub(kc, T), KC, D, 6, r)
                for kc in range(KC):
                    stt(xt_cur[0].sub(kc, T), xt_cur[0].sub(kc, T), gains.cols(3 * KC + kc, 3 * KC + kc + 1), r, ALU.mult, ALU.mult)
                out_stores += store_tile(outT[:, :, tsl], xt_cur[0], (dk("OUT", t),), "sto%d_" % (t % 2))

            if dbg.get("stop") == "p2":
                return [lst[-1] for st, lst in S.streams.items() if st not in ("pe", "act", "dve", "pool")]
            return out_stores

        if RUN_REST:
            out_stores = rest()
        else:
            out_stores = [lst[-1] for st, lst in S.streams.items() if st not in ("pe", "act", "dve", "pool")]
        S.barrier("sp", out_stores)
        S.finalize()

        @block.tensor
        def _(e):
            S.replay("pe", e)

        @block.scalar
        def _(e):
            S.replay("act", e)

        @block.vector
        def _(e):
            S.replay("dve", e)

        @block.gpsimd
        def _(e):
            S.replay("pool", e)

        @block.sync
        def _(e):
            S.replay("sp", e)

    return nc


def _rope_tables(qtr):
    s = (np.arange(TOK, dtype=np.float32) + np.float32(qtr * TOK))
    row = np.floor(s / 64).astype(np.float32)
    col = (s - row * 64).astype(np.float32)
    fA = (np.float32(10000.0) ** (-np.arange(0, 64, 2, dtype=np.float32) / np.float32(64))).astype(np.float32)
    fB = (np.float32(500000.0) ** (-np.arange(0, 32, 2, dtype=np.float32) / np.float32(32))).astype(np.float32)
    angr = (row[None, :] * fA[:, None]).astype(np.float32)
    angc = (col[None, :] * fA[:, None]).astype(np.float32)
    angl = (s[None, :] * fB[:, None]).astype(np.float32)
    cosA = np.concatenate([np.cos(angr), np.cos(angr), np.cos(angc), np.cos(angc)], 0)
    sinA = np.concatenate([-np.sin(angr), np.sin(angr), -np.sin(angc), np.sin(angc)], 0)
    cosB = np.ones((128, TOK), np.float32)
    sinB = np.zeros((128, TOK), np.float32)
    cosB[0:16] = np.cos(angl)
    cosB[16:32] = np.cos(angl)
    sinB[0:16] = -np.sin(angl)
    sinB[16:32] = np.sin(angl)
    return np.ascontiguousarray(np.stack([cosA, sinA, cosB, sinB], 0).astype(np.float32))


def _perm_consts():
    PA = np.zeros((128, 128), np.float32)
    PB = np.zeros((128, 128), np.float32)
    for m in range(128):
        blk = (m // 64) * 64
        r = m - blk
        PA[blk + (r + 32) % 64, m] = 1.0
    for m in range(32):
        PB[(m + 16) % 32, m] = 1.0
    return np.ascontiguousarray(np.concatenate([PA, PB, np.eye(128, dtype=np.float32)], 1))


_NC_CACHE = {}


def kernel(x, ffn1_norm, ffn1_w_gu, ffn1_w_down, mix_norm, w_in,
           a_q_norm, a_k_norm, b_q_norm, b_k_norm,
           b_lambda_q1, b_lambda_k1, b_lambda_q2, b_lambda_k2, b_subln,
           w_out, ffn2_norm, ffn2_w_gu, ffn2_w_down, out_norm):
    f32 = np.float32
    x = np.asarray(x, f32)

    def lay_gu(w):
        w = np.asarray(w, f32)[0].reshape(KC, 128, 2, JC, 128)
        return np.ascontiguousarray(w.transpose(3, 1, 2, 0, 4)).reshape(JC, 128, 2 * KC * 128)

    def lay_d(w):
        w = np.asarray(w, f32)[0].reshape(JC, 128, KC, 128)
        return np.ascontiguousarray(w.transpose(2, 1, 0, 3)).reshape(KC, 128, JC * 128)

    def lay_sq(w):
        w = np.asarray(w, f32).reshape(KC, 128, KC, 128)
        return np.ascontiguousarray(w.transpose(2, 1, 0, 3)).reshape(KC, 128, KC * 128)

    win = np.asarray(w_in, f32)[0]
    col0 = ([c * 128 for c in range(8)] + [1024 + g * 128 for g in range(2)]
            + [1536 + i * 128 for i in range(8)] + [2560 + i * 128 for i in range(8)]
            + [1280 + g * 128 for g in range(2)] + [3584 + i * 128 for i in range(8)])
    winc = np.stack([win[:, c0:c0 + 128].reshape(KC, 128, 128).transpose(1, 0, 2).reshape(128, KC * 128)
                     for c0 in col0], 0)
    winc = np.ascontiguousarray(winc)

    def gl(v):
        return np.asarray(v, f32).reshape(KC, 128).T

    gains = np.ascontiguousarray(np.concatenate([gl(ffn1_norm), gl(mix_norm), gl(ffn2_norm), gl(out_norm)], 1))
    small = np.zeros((128, 16), f32)
    for i, v in enumerate([a_q_norm, a_k_norm, b_q_norm, b_k_norm, b_lambda_q1, b_lambda_k1, b_lambda_q2, b_lambda_k2]):
        small[:, i] = np.asarray(v, f32).reshape(128)
    sl = np.asarray(b_subln, f32).reshape(2, 128)
    small[:, 8] = sl[0]
    small[:, 9] = sl[1]

    common = dict(gains=gains, small=small, wgu1=lay_gu(ffn1_w_gu), wgu2=lay_gu(ffn2_w_gu),
                  wd1=lay_d(ffn1_w_down), wd2=lay_d(ffn2_w_down), winc=winc,
                  wout=lay_sq(np.asarray(w_out, f32)[0]), perm=_perm_consts())
    ropes = [_rope_tables(q) for q in range(4)]
    in_maps = []
    for c in range(8):
        b, q = divmod(c, 4)
        xs = x[b, q * TOK:(q + 1) * TOK, :]
        m = dict(common)
        m["xT"] = np.ascontiguousarray(xs.T).reshape(KC, 128, TOK)
        m["rope"] = ropes[q]
        in_maps.append(m)

    if "nc" not in _NC_CACHE:
        _NC_CACHE["nc"] = build_nc()
    nc = _NC_CACHE["nc"]
    res = run_bass_kernel_spmd(nc, in_maps, core_ids=list(range(8)))
    out = np.empty((2, SEQ, D), f32)
    for c in range(8):
        b, q = divmod(c, 4)
        o = np.asarray(res.results[c]["outT"], f32).reshape(D, TOK)
        out[b, q * TOK:(q + 1) * TOK, :] = o.T
    return out
```
